# Optimizing a Trainium2 kernel written in Bass

```python
import math
import jax, jax.numpy as jnp
from jax import lax
import numpy as np

D_MODEL = 1024
BATCH = 4
SEQ = 8192
DEPTH = 2
DEC_BATCH = 32
DEC_SEQ = 1
PAST_LEN = 16384
PAGE_SIZE = 128

D_MIX = D_MODEL
POOL_WINDOWS = (2, 4, 8, 16)
N_POOL_GROUPS = len(POOL_WINDOWS)
D_POOL = D_MIX // 4
POOL_GROUP = D_POOL // N_POOL_GROUPS
POOL_BUF = max(POOL_WINDOWS) - 1

HEAD_DIM = 64
N_ATT_HEADS = (D_MIX // 4) // HEAD_DIM
D_ATT = N_ATT_HEADS * HEAD_DIM
DILATED = ((128, 1), (512, 4), (2048, 16))
ATT_WIN = max(w for w, _ in DILATED)
ATT_BLOCK = 128
ROPE_THETA = 10000.0

D_SSM = D_MIX - D_POOL - D_ATT
SSM_HEAD_DIM = 64
N_SSM_HEADS = D_SSM // SSM_HEAD_DIM
SSM_STATE = 128
SSM_GROUPS = 2
CONV_WIDTH = 4
SSM_CHUNK = 128
D_CONV = D_SSM + 2 * SSM_GROUPS * SSM_STATE
D_IN_PROJ = D_POOL + 3 * D_ATT + D_SSM + D_CONV + N_SSM_HEADS

D_FF = ((8 * D_MODEL // 3 + 127) // 128) * 128
RMS_EPS = 1e-6

kernel_name = 'hybrid_pool_dilated_ssd_decoder_step'


def rms_norm(x, g):
    xf = x.astype(jnp.float32)
    y = xf * lax.rsqrt(jnp.mean(xf * xf, -1, keepdims=True) + RMS_EPS)
    return (y * g.astype(jnp.float32)).astype(x.dtype)


def swiglu(x, w_gate, w_up, w_down):
    return jnp.matmul(jax.nn.silu(jnp.matmul(x, w_gate)) * jnp.matmul(x, w_up), w_down)


def rope(x, pos):
    half = x.shape[-1] // 2
    inv = ROPE_THETA ** (-jnp.arange(half, dtype=jnp.float32) / half)
    ang = pos.astype(jnp.float32)[:, None] * inv[None]
    cos = jnp.cos(ang)[None, :, None, :]
    sin = jnp.sin(ang)[None, :, None, :]
    x1, x2 = x[..., :half], x[..., half:]
    return jnp.concatenate([x1 * cos - x2 * sin, x2 * cos + x1 * sin], -1)


def split_proj(proj):
    cuts = np.cumsum([D_POOL, D_ATT, D_ATT, D_ATT, D_SSM, D_CONV]).tolist()
    return jnp.split(proj, cuts, axis=-1)


def pool_mix(u, pos, pool_w, pool_scale):
    n, L, _ = u.shape
    pad = max(POOL_WINDOWS)
    cs = jnp.pad(jnp.cumsum(u, axis=1), ((0, 0), (pad, 0), (0, 0)))
    means = []
    for g, w in enumerate(POOL_WINDOWS):
        c = cs[:, :, g * POOL_GROUP:(g + 1) * POOL_GROUP]
        s = c[:, pad:pad + L] - c[:, pad - w:pad - w + L]
        cnt = jnp.minimum(w, pos + 1).astype(jnp.float32)
        means.append(s / cnt[None, :, None])
    d = jnp.stack(means, 2) - u.reshape(n, L, N_POOL_GROUPS, POOL_GROUP)
    y = jnp.einsum('nlgc,gcd->nlgd', d, pool_w).reshape(n, L, D_POOL)
    return y * pool_scale


def dilated_branch_prompt(q, k, v, dil, n_back):
    b, S, h, e = q.shape
    L = S // dil
    nb = -(-L // ATT_BLOCK)
    Lp = nb * ATT_BLOCK

    def to_res(t):
        return jnp.pad(t.reshape(b, L, dil, h, e), ((0, 0), (0, Lp - L), (0, 0), (0, 0), (0, 0)))

    def band(t):
        tp = jnp.pad(t, ((0, 0), (ATT_BLOCK, 0), (0, 0), (0, 0), (0, 0)))
        prev = tp[:, :Lp].reshape(b, nb, ATT_BLOCK, dil, h, e)
        cur = t.reshape(b, nb, ATT_BLOCK, dil, h, e)
        return jnp.concatenate([prev, cur], 2)

    qb = to_res(q).reshape(b, nb, ATT_BLOCK, dil, h, e)
    kb, vb = band(to_res(k)), band(to_res(v))
    s = jnp.einsum('bnqrhe,bnkrhe->bnrhqk', qb, kb) / math.sqrt(e)
    i = jnp.arange(ATT_BLOCK)[:, None]
    j = jnp.arange(2 * ATT_BLOCK)[None]
    dist = ATT_BLOCK + i - j
    blk = jnp.arange(nb)[:, None, None]
    valid = (dist >= 0) & (dist <= n_back) & ((blk > 0) | (j >= ATT_BLOCK))
    s = jnp.where(valid[None, :, None, None], s, -jnp.inf)
    m = jnp.max(s, -1, keepdims=True)
    p = jnp.exp(s - m)
    l = jnp.sum(p, -1, keepdims=True)
    o = jnp.einsum('bnrhqk,bnkrhe->bnqrhe', p / l, vb)
    lse = jnp.transpose((m + jnp.log(l))[..., 0], (0, 1, 4, 2, 3))
    o = o.reshape(b, Lp, dil, h, e)[:, :L].reshape(b, S, h, e)
    lse = lse.reshape(b, Lp, dil, h)[:, :L].reshape(b, S, h)
    return o, lse


def dilated_branch_sample(q, kc, vc, dil, n_back, lb):
    T, e = q.shape[1], q.shape[-1]
    idx = lb + jnp.arange(T)[:, None] - dil * jnp.arange(n_back + 1)[None]
    valid = idx >= 0
    idx = jnp.maximum(idx, 0)
    kg, vg = kc[:, idx], vc[:, idx]
    s = jnp.einsum('nthe,ntjhe->nthj', q, kg) / math.sqrt(e)
    s = jnp.where(valid[None, :, None, :], s, -jnp.inf)
    m = jnp.max(s, -1, keepdims=True)
    p = jnp.exp(s - m)
    l = jnp.sum(p, -1, keepdims=True)
    o = jnp.einsum('nthj,ntjhe->nthe', p / l, vg)
    return o, (m + jnp.log(l))[..., 0]


def combine_by_denominator(outs, lses):
    wts = jax.nn.softmax(jnp.stack(lses, 0), axis=0)
    return jnp.einsum('gnth,gnthe->nthe', wts, jnp.stack(outs, 0))


def causal_conv(u, w, bias):
    L = u.shape[1] - (CONV_WIDTH - 1)
    return sum(u[:, t:t + L] * w[t] for t in range(CONV_WIDTH)) + bias


def split_ssm(u):
    n, L, _ = u.shape
    gn = SSM_GROUPS * SSM_STATE
    xs = u[..., :D_SSM].reshape(n, L, N_SSM_HEADS, SSM_HEAD_DIM)
    bm = u[..., D_SSM:D_SSM + gn].reshape(n, L, SSM_GROUPS, SSM_STATE)
    cm = u[..., D_SSM + gn:].reshape(n, L, SSM_GROUPS, SSM_STATE)
    return xs, bm, cm


def ssd_chunked(x, dt, a, bm, cm):
    b, S, H, P = x.shape
    G, N = bm.shape[2], bm.shape[3]
    J = H // G
    Q = SSM_CHUNK
    nc = S // Q
    xc = x.reshape(b, nc, Q, G, J, P)
    dtc = dt.reshape(b, nc, Q, G, J)
    bc = bm.reshape(b, nc, Q, G, N)
    cc = cm.reshape(b, nc, Q, G, N)
    acs = jnp.cumsum(dtc * a.reshape(G, J), axis=2)
    diff = acs[:, :, :, None] - acs[:, :, None]
    causal = jnp.tril(jnp.ones((Q, Q), bool))
    decay = jnp.exp(jnp.where(causal[:, :, None, None], diff, -jnp.inf))
    cb = jnp.einsum('bclgn,bcsgn->bclsg', cc, bc)
    mat = cb[..., None] * decay * dtc[:, :, None]
    y_diag = jnp.einsum('bclsgj,bcsgjp->bclgjp', mat, xc)
    decay_end = jnp.exp(acs[:, :, -1:] - acs)
    states = jnp.einsum('bclgn,bclgj,bclgjp->bcgjpn', bc, decay_end * dtc, xc)
    chunk_decay = jnp.exp(acs[:, :, -1])

    def step(h, inp):
        st, dec = inp
        return dec[..., None, None] * h + st, h

    h0 = jnp.zeros((b, G, J, P, N), x.dtype)
    h_last, h_prev = lax.scan(step, h0, (jnp.moveaxis(states, 1, 0), jnp.moveaxis(chunk_decay, 1, 0)))
    h_prev = jnp.moveaxis(h_prev, 0, 1)
    y_off = jnp.einsum('bclgn,bcgjpn,bclgj->bclgjp', cc, h_prev, jnp.exp(acs))
    return (y_diag + y_off).reshape(b, S, H, P), h_last.reshape(b, H, P, N)


def ssd_recurrent(x, dt, a, bm, cm, h0):
    J = x.shape[2] // bm.shape[2]
    bh = jnp.repeat(bm, J, axis=2)
    ch = jnp.repeat(cm, J, axis=2)

    def step(h, inp):
        xt, dtt, bt, ct = inp
        h = jnp.exp(dtt * a)[..., None, None] * h + (dtt[..., None] * xt)[..., None] * bt[:, :, None, :]
        return h, jnp.einsum('nhpk,nhk->nhp', h, ct)

    h_last, ys = lax.scan(step, h0, tuple(jnp.moveaxis(t, 1, 0) for t in (x, dt, bh, ch)))
    return jnp.moveaxis(ys, 0, 1), h_last


def ssm_gate_norm(y, xs, z, d_skip, ssm_norm):
    n, L = xs.shape[:2]
    y = (y + d_skip[:, None] * xs).reshape(n, L, D_SSM) * jax.nn.silu(z)
    yg = y.reshape(n, L, SSM_GROUPS, D_SSM // SSM_GROUPS)
    yg = yg * lax.rsqrt(jnp.mean(yg * yg, -1, keepdims=True) + RMS_EPS)
    return yg.reshape(n, L, D_SSM) * ssm_norm


def mixer_prompt(hn, w_in, pool_w, pool_scale, conv_w, conv_b, dt_bias, a_log, d_skip, ssm_norm, w_out):
    n, L, _ = hn.shape
    xa, q, k, v, z, xbc, dtr = split_proj(jnp.matmul(hn, w_in).astype(jnp.float32))
    pos = jnp.arange(L)
    ya = pool_mix(xa, pos, pool_w, pool_scale)
    q = rope(q.reshape(n, L, N_ATT_HEADS, HEAD_DIM), pos)
    k = rope(k.reshape(n, L, N_ATT_HEADS, HEAD_DIM), pos)
    v = v.reshape(n, L, N_ATT_HEADS, HEAD_DIM)
    outs, lses = [], []
    for win, dil in DILATED:
        o, lse = dilated_branch_prompt(q, k, v, dil, win // dil)
        outs.append(o)
        lses.append(lse)
    yb = combine_by_denominator(outs, lses).reshape(n, L, D_ATT)
    u = jax.nn.silu(causal_conv(jnp.pad(xbc, ((0, 0), (CONV_WIDTH - 1, 0), (0, 0))), conv_w, conv_b))
    xs, bm, cm = split_ssm(u)
    dt = jax.nn.softplus(dtr + dt_bias)
    y, h_last = ssd_chunked(xs, dt, -jnp.exp(a_log.astype(jnp.float32)), bm, cm)
    yc = ssm_gate_norm(y, xs, z, d_skip, ssm_norm)
    out = jnp.matmul(jnp.concatenate([ya, yb, yc], -1).astype(hn.dtype), w_out)
    wb = min(ATT_WIN, L)
    return out, (xa[:, -POOL_BUF:], k[:, -wb:], v[:, -wb:], xbc[:, -(CONV_WIDTH - 1):], h_last)


def mixer_sample(hn, c_pool, c_k, c_v, s_conv, s_ssm, w_in, pool_w, pool_scale, conv_w, conv_b,
                 dt_bias, a_log, d_skip, ssm_norm, w_out):
    n, T, _ = hn.shape
    f32 = jnp.float32
    xa, q, k, v, z, xbc, dtr = split_proj(jnp.matmul(hn, w_in).astype(f32))
    pos = PAST_LEN + jnp.arange(T)
    xa_cat = jnp.concatenate([c_pool.astype(f32), xa], 1)
    pos_cat = PAST_LEN - POOL_BUF + jnp.arange(POOL_BUF + T)
    ya = pool_mix(xa_cat, pos_cat, pool_w, pool_scale)[:, POOL_BUF:]
    q = rope(q.reshape(n, T, N_ATT_HEADS, HEAD_DIM), pos)
    k = rope(k.reshape(n, T, N_ATT_HEADS, HEAD_DIM), pos)
    v = v.reshape(n, T, N_ATT_HEADS, HEAD_DIM)
    lb = c_k.shape[1]
    kc = jnp.concatenate([c_k.astype(f32), k], 1)
    vc = jnp.concatenate([c_v.astype(f32), v], 1)
    outs, lses = [], []
    for win, dil in DILATED:
        o, lse = dilated_branch_sample(q, kc, vc, dil, win // dil, lb)
        outs.append(o)
        lses.append(lse)
    yb = combine_by_denominator(outs, lses).reshape(n, T, D_ATT)
    xbc_cat = jnp.concatenate([s_conv.astype(f32), xbc], 1)
    u = jax.nn.silu(causal_conv(xbc_cat, conv_w, conv_b))
    xs, bm, cm = split_ssm(u)
    dt = jax.nn.softplus(dtr + dt_bias)
    y, h_last = ssd_recurrent(xs, dt, -jnp.exp(a_log.astype(f32)), bm, cm, s_ssm.astype(f32))
    yc = ssm_gate_norm(y, xs, z, d_skip, ssm_norm)
    out = jnp.matmul(jnp.concatenate([ya, yb, yc], -1).astype(hn.dtype), w_out)
    return out, (xa_cat[:, -POOL_BUF:], kc[:, -lb:], vc[:, -lb:], xbc_cat[:, -(CONV_WIDTH - 1):], h_last)


def setup_inputs(seed: int = 0) -> dict:
    key = jax.random.key(seed)
    ks = iter(jax.random.split(key, 32))
    f32 = jnp.float32

    def nrm(shape, scale=1.0):
        return scale * jax.random.normal(next(ks), shape, f32)

    wb = min(ATT_WIN, PAST_LEN)
    inp = {}
    inp['x_prompt'] = nrm((BATCH, SEQ, D_MODEL))
    inp['x_sample'] = nrm((DEC_BATCH, DEC_SEQ, D_MODEL))
    inp['cache_pool'] = nrm((DEPTH, DEC_BATCH, POOL_BUF, D_POOL))
    inp['cache_k'] = nrm((DEPTH, DEC_BATCH, wb, N_ATT_HEADS, HEAD_DIM))
    inp['cache_v'] = nrm((DEPTH, DEC_BATCH, wb, N_ATT_HEADS, HEAD_DIM))
    inp['state_conv'] = nrm((DEPTH, DEC_BATCH, CONV_WIDTH - 1, D_CONV))
    inp['state_ssm'] = nrm((DEPTH, DEC_BATCH, N_SSM_HEADS, SSM_HEAD_DIM, SSM_STATE), 0.1)
    inp['ffn1_norm'] = 1.0 + nrm((DEPTH, D_MODEL), 0.02)
    inp['ffn1_w_gate'] = nrm((DEPTH, D_MODEL, D_FF), D_MODEL ** -0.5)
    inp['ffn1_w_up'] = nrm((DEPTH, D_MODEL, D_FF), D_MODEL ** -0.5)
    inp['ffn1_w_down'] = nrm((DEPTH, D_FF, D_MODEL), D_FF ** -0.5)
    inp['mix_norm'] = 1.0 + nrm((DEPTH, D_MODEL), 0.02)
    inp['w_in'] = nrm((DEPTH, D_MODEL, D_IN_PROJ), D_MODEL ** -0.5)
    inp['pool_w'] = nrm((DEPTH, N_POOL_GROUPS, POOL_GROUP, POOL_GROUP), POOL_GROUP ** -0.5)
    inp['pool_scale'] = 1.0 + nrm((DEPTH, D_POOL), 0.02)
    inp['conv_w'] = nrm((DEPTH, CONV_WIDTH, D_CONV), CONV_WIDTH ** -0.5)
    inp['conv_b'] = nrm((DEPTH, D_CONV), 0.02)
    dt0 = jnp.exp(jax.random.uniform(next(ks), (DEPTH, N_SSM_HEADS), f32, math.log(1e-3), math.log(1e-1)))
    inp['dt_bias'] = dt0 + jnp.log(-jnp.expm1(-dt0))
    inp['a_log'] = jnp.log(jax.random.uniform(next(ks), (DEPTH, N_SSM_HEADS), f32, 1.0, 16.0))
    inp['d_skip'] = 1.0 + nrm((DEPTH, N_SSM_HEADS), 0.1)
    inp['ssm_norm'] = 1.0 + nrm((DEPTH, D_SSM), 0.02)
    inp['w_out'] = nrm((DEPTH, D_MIX, D_MODEL), D_MIX ** -0.5)
    inp['ffn2_norm'] = 1.0 + nrm((DEPTH, D_MODEL), 0.02)
    inp['ffn2_w_gate'] = nrm((DEPTH, D_MODEL, D_FF), D_MODEL ** -0.5)
    inp['ffn2_w_up'] = nrm((DEPTH, D_MODEL, D_FF), D_MODEL ** -0.5)
    inp['ffn2_w_down'] = nrm((DEPTH, D_FF, D_MODEL), D_FF ** -0.5)
    inp['final_norm'] = 1.0 + nrm((D_MODEL,), 0.02)
    return inp


def reference(x_prompt, x_sample, cache_pool, cache_k, cache_v, state_conv, state_ssm,
              ffn1_norm, ffn1_w_gate, ffn1_w_up, ffn1_w_down, mix_norm, w_in, pool_w, pool_scale,
              conv_w, conv_b, dt_bias, a_log, d_skip, ssm_norm, w_out,
              ffn2_norm, ffn2_w_gate, ffn2_w_up, ffn2_w_down, final_norm):
    yp, ys = x_prompt, x_sample
    st_p = [[] for _ in range(5)]
    st_s = [[] for _ in range(5)]
    for i in range(DEPTH):
        mw = (w_in[i], pool_w[i], pool_scale[i], conv_w[i], conv_b[i], dt_bias[i], a_log[i],
              d_skip[i], ssm_norm[i], w_out[i])
        f1 = (ffn1_w_gate[i], ffn1_w_up[i], ffn1_w_down[i])
        f2 = (ffn2_w_gate[i], ffn2_w_up[i], ffn2_w_down[i])
        yp = yp + 0.5 * swiglu(rms_norm(yp, ffn1_norm[i]), *f1)
        ys = ys + 0.5 * swiglu(rms_norm(ys, ffn1_norm[i]), *f1)
        mo_p, new_p = mixer_prompt(rms_norm(yp, mix_norm[i]), *mw)
        mo_s, new_s = mixer_sample(rms_norm(ys, mix_norm[i]), cache_pool[i], cache_k[i], cache_v[i],
                                   state_conv[i], state_ssm[i], *mw)
        yp = yp + mo_p.astype(yp.dtype)
        ys = ys + mo_s.astype(ys.dtype)
        yp = yp + 0.5 * swiglu(rms_norm(yp, ffn2_norm[i]), *f2)
        ys = ys + 0.5 * swiglu(rms_norm(ys, ffn2_norm[i]), *f2)
        for lst, val in zip(st_p, new_p):
            lst.append(val)
        for lst, val in zip(st_s, new_s):
            lst.append(val)
    y_prompt = rms_norm(yp, final_norm)
    y_sample = rms_norm(ys, final_norm)
    pool_p = jnp.stack(st_p[0], 0).astype(cache_pool.dtype)
    pool_s = jnp.stack(st_s[0], 0).astype(cache_pool.dtype)
    k_p = jnp.stack(st_p[1], 0).astype(cache_k.dtype)
    k_s = jnp.stack(st_s[1], 0).astype(cache_k.dtype)
    v_p = jnp.stack(st_p[2], 0).astype(cache_v.dtype)
    v_s = jnp.stack(st_s[2], 0).astype(cache_v.dtype)
    conv_p = jnp.stack(st_p[3], 0).astype(state_conv.dtype)
    conv_s = jnp.stack(st_s[3], 0).astype(state_conv.dtype)
    ssm_p = jnp.stack(st_p[4], 0).astype(state_ssm.dtype)
    ssm_s = jnp.stack(st_s[4], 0).astype(state_ssm.dtype)
    return (y_prompt, y_sample, pool_p, pool_s, k_p, k_s, v_p, v_s, conv_p, conv_s, ssm_p, ssm_s)
```

```python
import numpy as np
from contextlib import ExitStack
import concourse.bass as bass
import concourse.mybir as mybir
from concourse.bass_utils import run_bass_kernel_spmd

F32 = mybir.dt.float32
BF16 = mybir.dt.bfloat16
AF = mybir.ActivationFunctionType
ALU = mybir.AluOpType
AX = mybir.AxisListType

ENGS = ("pe", "act", "dve", "pool", "sp")
NDMASEM = 24


class _Rec:
    def __getattr__(self, name):
        def f(*a, **k):
            self.call = (name, a, k)
            return self
        return f


def _free_elems(ap):
    n = 1
    for s in ap.shape[1:]:
        n *= int(s)
    return n


def _est_cost(eng, call, dma):
    name, a, k = call
    out = k.get("out", a[0] if a else None)
    try:
        n = _free_elems(out)
        parts = int(out.shape[0])
    except Exception:
        n, parts = 256, 128
    if dma:
        byt = n * parts * 4
        return (700.0 if eng == "pool" else 120.0), 2200.0 + byt / 80.0
    if eng == "pe":
        c = 70.0 + 0.45 * max(n, 64)
        try:
            lhs = k.get("lhsT", None)
            if lhs is not None and lhs.dtype == F32:
                c = 70.0 + 1.8 * max(n, 64)
        except Exception:
            pass
        return c, c
    if eng == "act":
        c = 230.0 + 0.62 * n
    elif eng == "dve":
        c = 120.0 + 1.0 * n
    elif eng == "pool":
        c = 300.0 + 3.0 * n
        if name in ("tensor_scalar_mul", "tensor_scalar"):
            c = 300.0 + 15.0 * n
    else:
        c = 100.0
    return c, c


class Prog:
    def __init__(self, nc, schedule=True):
        self.nc = nc
        self.nodes = []
        self.last_w = {}
        self.readers = {}
        self.excl = set()
        self.seg = 0
        self.schedule = schedule

    def op(self, eng, fn, reads=(), writes=(), dma=False, cc=False):
        rec = _Rec()
        fn(rec)
        xr = [b for b in reads if b in self.excl]
        if xr:
            writes = list(writes) + [b for b in xr if b not in writes]
        deps = set()
        for b in reads:
            if b in self.last_w:
                deps.add(self.last_w[b])
        for b in writes:
            if b in self.last_w:
                deps.add(self.last_w[b])
            deps.update(self.readers.get(b, ()))
        nid = len(self.nodes)
        busy, lat = _est_cost(eng, rec.call, dma)
        if cc:
            busy, lat = 1500.0, 60000.0
        tbl = 0
        if eng == "act" and rec.call[0] == "activation":
            f_ = rec.call[2].get("func", None)
            tbl = 1 if f_ in (AF.Exp, AF.Ln) else 2 if f_ == AF.Silu else 3 if f_ == AF.Sqrt else 0
        self.nodes.append(dict(eng=eng, call=rec.call, dma=dma, cc=cc, deps=deps, seg=self.seg, busy=busy, lat=lat, tbl=tbl))
        for b in writes:
            self.last_w[b] = nid
            self.readers[b] = []
        for b in reads:
            if b not in writes:
                self.readers.setdefault(b, []).append(nid)
        return nid

    def barrier(self):
        self.seg += 1
        self.last_w.clear()
        self.readers.clear()

    def finish(self, eng="sp"):
        self.final_eng = eng

    def _sched(self, ids):
        nodes = self.nodes
        if not self.schedule:
            return list(ids)
        idset = set(ids)
        succ = {i: [] for i in ids}
        ndep = {}
        for i in ids:
            ds = [d for d in nodes[i]["deps"] if d in idset]
            ndep[i] = len(ds)
            for d in ds:
                succ[d].append(i)
        prio = {}
        for i in reversed(ids):
            m = 0.0
            for s in succ[i]:
                if prio[s] > m:
                    m = prio[s]
            prio[i] = nodes[i]["lat"] + m
        ready = {e: [] for e in ENGS}
        for i in ids:
            if ndep[i] == 0:
                ready[nodes[i]["eng"]].append(i)
        efree = {e: 0.0 for e in ENGS}
        fin = {}
        rdy_t = {i: 0.0 for i in ids}
        order = []
        SYNC = 850.0
        WINDOW = 6000
        remaining = len(ids)
        cur_tbl = 0
        lo = 0
        done = set()
        while remaining:
            while lo < len(ids) and ids[lo] in done:
                lo += 1
            limit = ids[min(lo + WINDOW, len(ids) - 1)]
            best = None
            for e in ENGS:
                r = ready[e]
                if not r:
                    continue
                cb = None
                for i in r:
                    if i > limit:
                        continue
                    st = max(efree[e], rdy_t[i])
                    tb = nodes[i]["tbl"]
                    if tb and tb != cur_tbl:
                        st += 1300.0
                    key = (st, -prio[i], i)
                    if cb is None or key < cb[0]:
                        cb = (key, i)
                if cb is not None and (best is None or cb[0] < best[0]):
                    best = cb
            key, i = best
            e = nodes[i]["eng"]
            st = key[0]
            ready[e].remove(i)
            if nodes[i]["tbl"]:
                cur_tbl = nodes[i]["tbl"]
            efree[e] = st + nodes[i]["busy"]
            fin[i] = st + nodes[i]["lat"]
            order.append(i)
            done.add(i)
            remaining -= 1
            for s in succ[i]:
                ndep[s] -= 1
                t = fin[i] + (SYNC if nodes[s]["eng"] != e else 40.0)
                if t > rdy_t[s]:
                    rdy_t[s] = t
                if ndep[s] == 0:
                    ready[nodes[s]["eng"]].append(s)
        self.sim_time = getattr(self, "sim_time", 0.0) + max(fin.values())
        return order

    def emit(self, stack):
        nc = self.nc
        nodes = self.nodes
        nseg = self.seg + 1
        segs = [[] for _ in range(nseg)]
        for i, n in enumerate(nodes):
            segs[n["seg"]].append(i)
        streams = {e: [] for e in ENGS}
        tok = {}
        dma_cnt = [0] * NDMASEM
        dma_rr = 0
        seen = {e: {} for e in ENGS}
        ckeys = set()
        cckeys = []
        pending = {}
        for si, ids in enumerate(segs):
            if not ids:
                continue
            order = self._sched(ids)
            count = {e: 0 for e in ENGS}
            first = {e: True for e in ENGS}
            for i in order:
                n = nodes[i]
                e = n["eng"]
                deps = set(tok[d] for d in n["deps"])
                if first[e] and pending:
                    deps.update(pending.items())
                first[e] = False
                if n["cc"]:
                    tok[i] = (("cc", len(cckeys)), 1)
                    cckeys.append(tok[i][0])
                    inc = 1
                elif n["dma"]:
                    k = dma_rr
                    dma_rr = (dma_rr + 1) % NDMASEM
                    if dma_cnt[k] > 0:
                        deps.add((("dma", k), dma_cnt[k]))
                    dma_cnt[k] += 16
                    tok[i] = (("dma", k), dma_cnt[k])
                    inc = 16
                else:
                    count[e] += 1
                    tok[i] = ((e, si), count[e])
                    ckeys.add((e, si))
                    inc = 1
                waits = []
                sn = seen[e]
                for key, val in sorted(deps, key=lambda kv: (str(kv[0]), kv[1])):
                    if e == "pe" and key[0] == "pe":
                        continue
                    if sn.get(key, 0) >= val:
                        continue
                    sn[key] = val
                    waits.append((key, val))
                streams[e].append((n["call"], waits, tok[i][0], inc))
            pending = {(e, si): count[e] for e in ENGS if count[e] > 0}
            for k in range(NDMASEM):
                if dma_cnt[k] > 0:
                    pending[("dma", k)] = dma_cnt[k]
            for ck in cckeys:
                pending[ck] = 1
        final = sorted(pending.items(), key=lambda kv: str(kv[0]))
        sems = {}
        for ck in sorted(ckeys):
            sems[ck] = stack.enter_context(nc.semaphore("c_%s_%d" % ck))
        for k in range(NDMASEM):
            sems[("dma", k)] = stack.enter_context(nc.semaphore("d_%d" % k))
        for ck in cckeys:
            sems[ck] = stack.enter_context(nc.semaphore("cc_%d" % ck[1]))
        block = stack.enter_context(nc.Block())
        reg = {"pe": block.tensor, "act": block.scalar, "dve": block.vector,
               "pool": block.gpsimd, "sp": block.sync}
        for e in ENGS:
            stream = streams[e]
            fin_ = final if e == self.final_eng else []

            def body(engine, stream=stream, fin_=fin_):
                for fn, waits, skey, inc in stream:
                    for key, val in waits:
                        engine.wait_ge(sems[key], val)
                    ins = getattr(engine, fn[0])(*fn[1], **fn[2])
                    ins.then_inc(sems[skey], inc)
                for key, val in fin_:
                    engine.wait_ge(sems[key], val)

            reg[e](body)


D = 1024
KC = 8
EPS = 1e-6
DIN = 2568
TM = 256
NRING = 18
NPRE_TILES = 16


def host_consts(ntok, pos0=0):
    import ml_dtypes
    c = {}
    i = np.arange(128)
    c["tri"] = (i[:, None] <= i[None, :]).astype(np.float32)
    c["ustr"] = (i[None, :] < i[:, None]).astype(np.float32)
    c["ones"] = np.ones((128, 128), np.float32)
    c["identf"] = np.eye(128, dtype=np.float32)
    m = np.zeros((128, NRING, TM), np.float32)
    for r in range(NRING):
        for jb in range(2):
            b = 16 + jb - r
            if b < 0 or b > 16:
                continue
            d = 128 * b + i[None, :] - i[:, None]
            w = ((d >= 0) & (d <= 128)).astype(np.float32)
            w += ((d >= 0) & (d <= 512) & (d % 4 == 0))
            w += ((d >= 0) & (d <= 2048) & (d % 16 == 0))
            m[:, r, jb * 128:(jb + 1) * 128] = w
    c["masks"] = m.astype(ml_dtypes.bfloat16)
    half = 32
    inv = (10000.0 ** (-np.arange(half, dtype=np.float32) / half)).astype(np.float32)
    pos = (pos0 + np.arange(ntok)).astype(np.float32)
    ang = pos[:, None] * inv[None]
    c["cos"] = np.cos(ang).astype(np.float32)
    c["sin"] = np.sin(ang).astype(np.float32)
    ic = np.zeros((2, 128, 2, TM), np.float32)
    for which in range(2):
        for ch in range(2):
            for hfp in range(2):
                w = (2, 4, 8, 16)[ch * 2 + hfp]
                p = np.arange(TM) + (0 if which == 0 else 100000)
                ic[which, hfp * 64:(hfp + 1) * 64, ch, :] = 1.0 / np.minimum(w, p + 1)
    c["invcnt"] = ic
    return c


def mixer_phase(nc, p, st, tag, ntok, X_in, X_out, W, C, outs, last_tok_base, X_prev=None, npre=0, wst=None):
    sb = lambda name, shape, dt: st.enter_context(nc.sbuf_tensor(tag + name, shape, dt))
    ps = lambda name, shape, dt: st.enter_context(nc.psum_tensor(tag + name, shape, dt))
    K = lambda *a: (tag,) + a

    sbw = (lambda name, shape, dt: wst.enter_context(nc.sbuf_tensor(tag + name, shape, dt))) if wst is not None else sb
    win = sbw("win", [128, KC, DIN], BF16)
    wout = sbw("wout", [128, KC, D], BF16)
    poolw = sbw("poolw", [128, 2, 128], BF16)
    identf = sbw("identf", [128, 128], F32)
    ident = sbw("ident", [128, 128], BF16)
    gain = sb("gain", [128, D], F32)
    pscale = sb("pscale", [128, 2], F32)
    convw = sb("convw", [128, 8, 4], F32)
    convb = sb("convb", [128, 8], F32)
    dtb = sb("dtb", [128, 8], F32)
    aneg = sb("aneg", [128, 8], F32)
    dskip = sb("dskip", [128, 8], F32)
    ssmn = sb("ssmn", [128, 512], F32)
    tri = sb("tri", [128, 128], F32)
    ustr = sb("ustr", [128, 128], F32)
    ones = sb("ones", [128, 128], F32)
    masks = sb("masks", [128, NRING, TM], BF16)
    invcnt = sb("invcnt", [128, 2, 2, TM], F32)
    epsb = sb("epsb", [128, 1], F32)
    kring = sb("kring", [128, 2, NRING * 128], BF16)
    vring = sb("vring", [128, NRING, 4, 65], BF16)
    hT = sb("hT", [128, 8, 64], F32)
    hTb = sb("hTb", [128, 8, 64], BF16)

    xt = [sb("x%d" % i, [128, 2, D], F32) for i in range(2)]
    xn = [sb("xn%d" % i, [128, D], BF16) for i in range(2)]
    ss = sb("ss", [128, 4], F32)
    rstd = sb("rstd", [128, 4], F32)
    xnT = sb("xnT", [128, KC, TM], BF16)
    xa = sb("xa", [128, 2, 15 + TM], F32)
    s2 = sb("s2", [128, 2, 15 + TM], F32)
    s4 = sb("s4", [128, 2, 15 + TM], F32)
    pd_ = sb("pd", [128, 2, TM], BF16)
    xbc = sb("xbc", [128, 8, 3 + TM], BF16)
    dconv = sb("dconv", [128, 8, 4, 128], BF16)
    uT = sb("uT", [128, 8, TM], BF16)
    cosb = sb("cosb", [128, 2, 32], F32)
    sinb = sb("sinb", [128, 2, 32], F32)
    qk = [sb("qk%d" % i, [128, 512], F32) for i in range(2)]
    qkr = [sb("qkr%d" % i, [128, 512], F32) for i in range(2)]
    rt = [sb("rt%d" % i, [128, 256], F32) for i in range(4)]
    qkb = sb("qkb", [128, 512], BF16)
    qTz = sb("qTz", [128, 2, 2, TM], BF16)
    vf = [sb("vf%d" % i, [128, 256], F32) for i in range(2)]
    zs = sb("zs", [128, 2, 512], F32)
    Pt = [sb("P%d" % i, [128, 2, TM], BF16) for i in range(3)]
    rl = sb("rl", [128, 8], F32)
    ybt = [sb("ybt%d" % i, [128, 256], BF16) for i in range(2)]
    catT = sb("catT", [128, KC, TM], BF16)
    xs_tm = sb("xs_tm", [128, 512], BF16)
    B_tm = sb("B_tm", [128, 256], BF16)
    dtr = sb("dtr", [128, 8], F32)
    dtt = sb("dtt", [128, 8], F32)
    dt1 = sb("dt1", [128, 8], F32)
    dta = sb("dta", [128, 8], F32)
    R = sb("R", [128, 8, 128], F32)
    E = sb("E", [128, 8, 128], BF16)
    CBm = sb("CBm", [128, 2, 128], BF16)
    MT = sb("MT", [128, 8, 128], BF16)
    xdt = sb("xdt", [128, 8, 64], BF16)
    xde = sb("xde", [128, 8, 64], BF16)
    eacs = sb("eacs", [128, 8], F32)
    cdb = sb("cdb", [128, 8], F32)
    yt = sb("yt", [128, 512], F32)
    y2 = sb("y2", [128, 512], F32)
    ssq = sb("ssq", [128, 2], F32)
    grs = sb("grs", [128, 2], F32)
    yc = sb("yc", [128, 512], BF16)
    stg = sb("stg", [128, 1280], F32)
    hfin = sb("hfin", [128, 4, 128], F32)

    bank = [ps("b%d" % i, [128, 512], F32) for i in range(8)]
    bk = [K("bank", i) for i in range(8)]
    p.excl = set(getattr(p, "excl", ())) | set(bk)

    def bfv(i):
        return bank[i][:].bitcast(BF16)

    for k in range(KC):
        p.op("pool", lambda e, k=k: e.dma_start(out=win[:, k, :], in_=W["w_in"][k * 128:(k + 1) * 128, :]),
             writes=[K("win", k)], dma=True)
    for k in range(KC):
        p.op("pool", lambda e, k=k: e.dma_start(out=wout[:, k, :], in_=W["w_out"][k * 128:(k + 1) * 128, :]),
             writes=[K("wout", k)], dma=True)
    p.op("pool", lambda e: e.memset(poolw[:], 0.0), writes=[K("poolw")])
    for g in range(4):
        p.op("pool", lambda e, g=g: e.dma_start(
            out=poolw[(g % 2) * 64:(g % 2) * 64 + 64, g // 2, (g % 2) * 64:(g % 2) * 64 + 64], in_=W["pool_w"][g]),
            reads=[K("poolw")], writes=[K("poolw")], dma=True)
    p.op("pool", lambda e: e.memset(epsb[:], EPS), writes=[K("epsb")])
    p.op("pool", lambda e: e.memset(qTz[:], 0.0), writes=[K("qTz")])
    p.op("pool", lambda e: e.memset(hT[:], 0.0), writes=[K("hT")])
    p.op("pool", lambda e: e.memset(hTb[:], 0.0), writes=[K("hTb")])
    p.op("pool", lambda e: e.memset(xa[:, :, 0:15], 0.0), writes=[K("xa")])
    p.op("pool", lambda e: e.memset(xbc[:, :, 0:3], 0.0), writes=[K("xbc", c_) for c_ in range(8)])
    if X_prev is None:
        p.op("pool", lambda e: e.memset(vring[:, :, :, 64:65], 1.0), writes=[K("vring", s_) for s_ in range(NRING)])
    else:
        p.op("pool", lambda e: e.memset(vring[:], 0.0), writes=[K("vring", s_) for s_ in range(NRING)])
        p.op("pool", lambda e: e.memset(kring[:], 0.0), writes=[K("kring", s_) for s_ in range(NRING)])

    def ld(dst, src, key, eng="sp"):
        p.op(eng, lambda e: e.dma_start(out=dst, in_=src), writes=[key], dma=True)

    ld(gain[:], W["mix_norm"].partition_broadcast(128), K("gain"))
    ld(pscale[:], W["pool_scale_l"], K("pscale"))
    ld(convw[:], W["conv_w_l"], K("convw"))
    ld(convb[:], W["conv_b_l"], K("convb"))
    ld(dtb[:], W["dt_bias"].partition_broadcast(128), K("dtb"))
    ld(aneg[:], W["a_log"].partition_broadcast(128), K("aneg"))
    ld(dskip[:], W["d_skip"].partition_broadcast(128), K("dskip"))
    ld(ssmn[:], W["ssm_norm"].partition_broadcast(128), K("ssmn"))
    ld(tri[:], C["tri"], K("tri"))
    ld(ustr[:], C["ustr"], K("ustr"))
    ld(ones[:], C["ones"], K("ones"))
    ld(identf[:], C["identf"], K("identf"))
    ld(masks[:], C["masks"], K("masks"))
    ld(invcnt[:], C["invcnt"].rearrange("w p c t -> p w c t"), K("invcnt"))
    p.op("dve", lambda e: e.tensor_copy(out=ident[:], in_=identf[:]), reads=[K("identf")], writes=[K("ident")])
    for c_ in range(8):
        for k_ in range(4):
            p.op("dve" if (c_ + k_) % 2 else "pool", lambda e, c_=c_, k_=k_: e.tensor_scalar_mul(
                out=dconv[:, c_, k_, :], in0=identf[:], scalar1=convw[:, c_, k_:k_ + 1]) if (c_ + k_) % 2 else
                e.tensor_tensor(out=dconv[:, c_, k_, :], in0=identf[:], in1=convw[:, c_, k_:k_ + 1].to_broadcast([128, 128]),
                                op=ALU.mult),
                reads=[K("identf"), K("convw")], writes=[K("dconv", c_, k_)])
    p.op("act", lambda e: e.activation(out=aneg[:], in_=aneg[:], func=AF.Exp), reads=[K("aneg")], writes=[K("aneg")])
    p.op("dve", lambda e: e.tensor_scalar_mul(out=aneg[:], in0=aneg[:], scalar1=-1.0), reads=[K("aneg")], writes=[K("aneg")])

    npre = 0 if X_prev is None else npre
    if X_prev is not None:
        flag = sb("flag", [128, 1], F32)
        wl = sb("wl", [128, 8], F32)
        ld(flag[:], C["flag"], K("flag"))

    def pre_proj_fm(bi, col0):
        for k in range(KC):
            p.op("pe", lambda e, k=k: e.matmul(bank[bi][:, 0:TM], lhsT=win[:, k, col0:col0 + 128],
                                               rhs=xnT[:, k, :], start=(k == 0), stop=(k == KC - 1)),
                 reads=[K("xnT", 0), K("xnT", 1), K("win", k)], writes=[bk[bi]])

    def pre_proj_tm(bi, b, col0, ncols):
        for k in range(KC):
            p.op("pe", lambda e, k=k: e.matmul(bank[bi][:, 0:ncols], lhsT=xnT[:, k, b * 128:(b + 1) * 128],
                                               rhs=win[:, k, col0:col0 + ncols], start=(k == 0), stop=(k == KC - 1)),
                 reads=[K("xnT", b), K("win", k)], writes=[bk[bi]])

    for pi in range(npre):
        x = xt[pi % 2]
        xk = K("x", pi % 2)
        src = X_prev(pi)
        last_pre = (pi == npre - 1)
        kv_pre = (pi >= npre - 8)
        p.op("sp", lambda e: e.dma_start(out=x[:], in_=src.rearrange("(b p) d -> p b d", p=128)), writes=[xk], dma=True)
        if kv_pre:
            t0p = pi * TM
            p.op("sp", lambda e: e.dma_start(out=cosb[:], in_=C["cos_prev"][t0p:t0p + TM, :].rearrange("(b p) d -> p b d", p=128)),
                 writes=[K("cosb")], dma=True)
            p.op("sp", lambda e: e.dma_start(out=sinb[:], in_=C["sin_prev"][t0p:t0p + TM, :].rearrange("(b p) d -> p b d", p=128)),
                 writes=[K("sinb")], dma=True)
        for b in range(2):
            xnb = xn[b]
            xnk = K("xn", b)
            p.op("act", lambda e: e.activation(out=xnb[:], in_=x[:, b, :], func=AF.Square, accum_out=ss[:, b:b + 1]),
                 reads=[xk], writes=[xnk, K("ss", b)])
            p.op("act", lambda e: e.activation(out=rstd[:, b:b + 1], in_=ss[:, b:b + 1], func=AF.Ln, scale=1.0 / D, bias=epsb[:]),
                 reads=[K("ss", b), K("epsb")], writes=[K("rstd", b)])
            p.op("act", lambda e: e.activation(out=rstd[:, b:b + 1], in_=rstd[:, b:b + 1], func=AF.Exp, scale=-0.5),
                 reads=[K("rstd", b)], writes=[K("rstd", b)])
            p.op("dve", lambda e: e.scalar_tensor_tensor(out=xnb[:], in0=x[:, b, :], scalar=rstd[:, b:b + 1], in1=gain[:],
                                                         op0=ALU.mult, op1=ALU.mult),
                 reads=[xk, K("rstd", b), K("gain")], writes=[xnk])
            for k in range(KC):
                p.op("pe", lambda e, k=k: e.transpose(out=bfv(7)[:, k * 128:(k + 1) * 128], in_=xnb[:, k * 128:(k + 1) * 128],
                                                      identity=ident[:]),
                     reads=[xnk, K("ident")], writes=[bk[7]])
            p.op("act", lambda e: e.copy(out=xnT[:, :, b * 128:(b + 1) * 128], in_=bfv(7).rearrange("p (k t) -> p k t", k=KC)),
                 reads=[bk[7]], writes=[K("xnT", b)])
        if kv_pre:
            for b in range(2):
                ab = 2 * (pi - npre) + b
                slot = ab % NRING
                pre_proj_tm(2, b, 512, 512)
                p.op("act", lambda e: e.copy(out=qk[b][:, 0:256], in_=bank[2][:, 0:256]), reads=[bk[2]], writes=[K("qk", b)])
                p.op("act", lambda e: e.activation(out=vring[:, slot, :, 0:64],
                                                   in_=bank[2][:, 256:512].rearrange("p (h j) -> p h j", h=4),
                                                   func=AF.Copy, scale=flag[:, 0:1]),
                     reads=[bk[2], K("flag")], writes=[K("vring", slot)])
                p.op("dve", lambda e: e.tensor_copy(out=vring[:, slot, :, 64:65],
                                                    in_=flag[:, 0:1].unsqueeze(1).to_broadcast([128, 4, 1])),
                     reads=[K("flag")], writes=[K("vring", slot)])
                v4k = lambda t: t.rearrange("p (h s j) -> p h s j", h=4, s=2)
                r3k = lambda t: t[:, 0:128].rearrange("p (h j) -> p h j", h=4)
                cb_ = cosb[:, b, :].unsqueeze(1).to_broadcast([128, 4, 32])
                sb_ = sinb[:, b, :].unsqueeze(1).to_broadcast([128, 4, 32])
                x1 = v4k(qk[b][:, 0:256])[:, :, 0, :]
                x2 = v4k(qk[b][:, 0:256])[:, :, 1, :]
                o1 = v4k(qkr[b][:, 0:256])[:, :, 0, :]
                o2 = v4k(qkr[b][:, 0:256])[:, :, 1, :]
                p.op("dve", lambda e: e.tensor_tensor(out=r3k(rt[0]), in0=x1, in1=cb_, op=ALU.mult),
                     reads=[K("qk", b), K("cosb")], writes=[K("rt", 0)])
                p.op("dve", lambda e: e.tensor_tensor(out=r3k(rt[1]), in0=x2, in1=sb_, op=ALU.mult),
                     reads=[K("qk", b), K("sinb")], writes=[K("rt", 1)])
                p.op("dve", lambda e: e.tensor_tensor(out=o1, in0=r3k(rt[0]), in1=r3k(rt[1]), op=ALU.subtract),
                     reads=[K("rt", 0), K("rt", 1)], writes=[K("qkr", b)])
                p.op("pool", lambda e: e.tensor_tensor(out=r3k(rt[2]), in0=x2, in1=cb_, op=ALU.mult),
                     reads=[K("qk", b), K("cosb")], writes=[K("rt", 2)])
                p.op("pool", lambda e: e.tensor_tensor(out=r3k(rt[3]), in0=x1, in1=sb_, op=ALU.mult),
                     reads=[K("qk", b), K("sinb")], writes=[K("rt", 3)])
                p.op("pool", lambda e: e.tensor_tensor(out=o2, in0=r3k(rt[2]), in1=r3k(rt[3]), op=ALU.add),
                     reads=[K("rt", 2), K("rt", 3)], writes=[K("qkr", b)])
                p.op("act", lambda e: e.copy(out=qkb[:, 0:256], in_=qkr[b][:, 0:256]), reads=[K("qkr", b)], writes=[K("qkb")])
                for c in range(2):
                    p.op("pe", lambda e, c=c: e.transpose(out=bfv(7)[:, c * 128:(c + 1) * 128], in_=qkb[:, c * 128:(c + 1) * 128],
                                                          identity=ident[:]),
                         reads=[K("qkb"), K("ident")], writes=[bk[7]])
                p.op("act", lambda e: e.copy(out=kring[:, :, slot * 128:(slot + 1) * 128],
                                             in_=bfv(7)[:, 0:256].rearrange("p (c t) -> p c t", c=2)),
                     reads=[bk[7]], writes=[K("kring", slot)])
        if last_pre:
            for c in range(2):
                pre_proj_fm(c, c * 128)
                p.op("act", lambda e, c=c: e.copy(out=xa[:, c, 15:15 + TM], in_=bank[c][:, 0:TM]), reads=[bk[c]], writes=[K("xa")])
            p.op("pool", lambda e: e.tensor_copy(out=xa[:, :, 0:15], in_=xa[:, :, TM:TM + 15]), reads=[K("xa")], writes=[K("xa")])
        nconv = 8 if last_pre else 6
        for c in range(nconv):
            bi = c % 2
            cb_ = 2 + (c % 2)
            pre_proj_fm(bi, 1536 + c * 128)
            p.op("act", lambda e, c=c, bi=bi: e.copy(out=xbc[:, c, 3:3 + TM], in_=bank[bi][:, 0:TM]), reads=[bk[bi]], writes=[K("xbc", c)])
            if c >= 6:
                continue
            for kk in range(4):
                p.op("pe", lambda e, c=c, kk=kk: e.matmul(bank[cb_][:, 0:TM], lhsT=dconv[:, c, kk, :], rhs=xbc[:, c, kk:kk + TM],
                                                          start=(kk == 0), stop=(kk == 3)),
                     reads=[K("xbc", c), K("dconv", c, kk)], writes=[bk[cb_]])
            p.op("act", lambda e, c=c: e.activation(out=uT[:, c, :], in_=bank[cb_][:, 0:TM], func=AF.Silu, bias=convb[:, c:c + 1]),
                 reads=[bk[cb_], K("convb")], writes=[K("uT", c)])
        p.op("pool", lambda e: e.tensor_copy(out=xbc[:, :, 0:3], in_=xbc[:, :, TM:TM + 3]),
             reads=[K("xbc", c_) for c_ in range(8)], writes=[K("xbc", c_) for c_ in range(8)])
        if last_pre:
            p.op("dve", lambda e: e.tensor_scalar_mul(out=xa[:, :, 0:15], in0=xa[:, :, 0:15], scalar1=flag[:, 0:1]),
                 reads=[K("xa"), K("flag")], writes=[K("xa")])
            p.op("dve", lambda e: e.tensor_scalar_mul(out=xbc[:, :, 0:3], in0=xbc[:, :, 0:3], scalar1=flag[:, 0:1]),
                 reads=[K("xbc", c_) for c_ in range(8)] + [K("flag")], writes=[K("xbc", c_) for c_ in range(8)])
        for b in range(2):
            cs = slice(b * 128, (b + 1) * 128)
            pre_proj_tm(0, b, 2560, 8)
            p.op("dve", lambda e: e.tensor_tensor(out=dtr[:], in0=bank[0][:, 0:8], in1=dtb[:], op=ALU.add),
                 reads=[bk[0], K("dtb")], writes=[K("dtr")])
            p.op("dve", lambda e: e.tensor_scalar_mul(out=dtt[:], in0=dtr[:], scalar1=-1.0), reads=[K("dtr")], writes=[K("dtt")])
            p.op("dve", lambda e: e.tensor_tensor(out=dtt[:], in0=dtt[:], in1=dtr[:], op=ALU.max),
                 reads=[K("dtr"), K("dtt")], writes=[K("dtt")])
            p.op("act", lambda e: e.activation(out=dtt[:], in_=dtt[:], func=AF.Exp, scale=-1.0), reads=[K("dtt")], writes=[K("dtt")])
            p.op("act", lambda e: e.activation(out=dtt[:], in_=dtt[:], func=AF.Ln, bias=1.0), reads=[K("dtt")], writes=[K("dtt")])
            p.op("dve", lambda e: e.scalar_tensor_tensor(out=dt1[:], in0=dtr[:], scalar=0.0, in1=dtt[:], op0=ALU.max, op1=ALU.add),
                 reads=[K("dtr"), K("dtt")], writes=[K("dt1")])
            p.op("dve", lambda e: e.tensor_scalar_mul(out=dt1[:], in0=dt1[:], scalar1=flag[:, 0:1]),
                 reads=[K("dt1"), K("flag")], writes=[K("dt1")])
            p.op("dve", lambda e: e.tensor_tensor(out=dta[:], in0=dt1[:], in1=aneg[:], op=ALU.mult),
                 reads=[K("dt1"), K("aneg")], writes=[K("dta")])
            for c in range(6):
                p.op("pe", lambda e, c=c: e.transpose(out=bfv(7)[:, c * 128:(c + 1) * 128], in_=uT[:, c, cs], identity=ident[:]),
                     reads=[K("uT", c), K("ident")], writes=[bk[7]])
            p.op("act", lambda e: e.copy(out=xs_tm[:], in_=bfv(7)[:, 0:512]), reads=[bk[7]], writes=[K("xs_tm")])
            p.op("act", lambda e: e.copy(out=B_tm[:], in_=bfv(7)[:, 512:768]), reads=[bk[7]], writes=[K("B_tm")])
            p.op("pe", lambda e: e.matmul(bank[0][:, 16:24], lhsT=tri[:], rhs=dta[:], start=True, stop=True),
                 reads=[K("tri"), K("dta")], writes=[bk[0]])
            p.op("pe", lambda e: e.matmul(bank[0][:, 32:40], lhsT=ones[:], rhs=dta[:], start=True, stop=True),
                 reads=[K("ones"), K("dta")], writes=[bk[0]])
            p.op("act", lambda e: e.activation(out=cdb[:], in_=bank[0][:, 32:40], func=AF.Exp), reads=[bk[0]], writes=[K("cdb")])
            p.op("act", lambda e: e.copy(out=eacs[:], in_=bank[0][:, 16:24]), reads=[bk[0]], writes=[K("eacs")])
            p.op("dve", lambda e: e.tensor_tensor(out=wl[:], in0=bank[0][:, 32:40], in1=eacs[:], op=ALU.subtract),
                 reads=[bk[0], K("eacs")], writes=[K("wl")])
            p.op("act", lambda e: e.activation(out=wl[:], in_=wl[:], func=AF.Exp), reads=[K("wl")], writes=[K("wl")])
            p.op("dve", lambda e: e.tensor_tensor(out=wl[:], in0=wl[:], in1=dt1[:], op=ALU.mult),
                 reads=[K("wl"), K("dt1")], writes=[K("wl")])
            p.op("dve", lambda e: e.tensor_tensor(out=xde[:], in0=xs_tm[:].rearrange("p (h j) -> p h j", h=8),
                                                  in1=wl[:].unsqueeze(2).to_broadcast([128, 8, 64]), op=ALU.mult),
                 reads=[K("xs_tm"), K("wl")], writes=[K("xde")])
            for g in range(2):
                p.op("pe", lambda e, g=g: e.matmul(bank[2][:, g * 256:(g + 1) * 256], lhsT=B_tm[:, g * 128:(g + 1) * 128],
                                                   rhs=xde[:, g * 4:(g + 1) * 4, :].rearrange("p h j -> p (h j)"),
                                                   start=True, stop=True),
                     reads=[K("B_tm"), K("xde")], writes=[bk[2]])
            p.op("dve", lambda e: e.tensor_tensor(out=hT[:], in0=hT[:], in1=cdb[:].unsqueeze(2).to_broadcast([128, 8, 64]),
                                                  op=ALU.mult),
                 reads=[K("hT"), K("cdb")], writes=[K("hT")])
            p.op("dve", lambda e: e.tensor_tensor(out=hT[:].rearrange("p h j -> p (h j)"), in0=hT[:].rearrange("p h j -> p (h j)"),
                                                  in1=bank[2][:], op=ALU.add),
                 reads=[K("hT"), bk[2]], writes=[K("hT")])
        if last_pre:
            p.op("act", lambda e: e.copy(out=hTb[:], in_=hT[:]), reads=[K("hT")], writes=[K("hTb")])

    winkeys = [K("win", k) for k in range(KC)]
    ntiles = ntok // TM
    for ti in range(ntiles):
        t0 = ti * TM
        x = xt[ti % 2]
        xk = K("x", ti % 2)
        p.op("sp", lambda e, x=x, t0=t0: e.dma_start(
            out=x[:], in_=X_in[t0:t0 + TM, :].rearrange("(b p) d -> p b d", p=128)), writes=[xk], dma=True)
        p.op("sp", lambda e, t0=t0: e.dma_start(
            out=cosb[:], in_=C["cos"][t0:t0 + TM, :].rearrange("(b p) d -> p b d", p=128)), writes=[K("cosb")], dma=True)
        p.op("sp", lambda e, t0=t0: e.dma_start(
            out=sinb[:], in_=C["sin"][t0:t0 + TM, :].rearrange("(b p) d -> p b d", p=128)), writes=[K("sinb")], dma=True)
        for b in range(2):
            xnb = xn[b]
            xnk = K("xn", b)
            p.op("act", lambda e, b=b, xnb=xnb, x=x: e.activation(out=xnb[:], in_=x[:, b, :], func=AF.Square,
                                                              accum_out=ss[:, b:b + 1]),
                 reads=[xk], writes=[xnk, K("ss", b)])
            p.op("act", lambda e, b=b: e.activation(out=rstd[:, b:b + 1], in_=ss[:, b:b + 1], func=AF.Ln,
                                                   scale=1.0 / D, bias=epsb[:]),
                 reads=[K("ss", b), K("epsb")], writes=[K("rstd", b)])
            p.op("act", lambda e, b=b: e.activation(out=rstd[:, b:b + 1], in_=rstd[:, b:b + 1], func=AF.Exp, scale=-0.5),
                 reads=[K("rstd", b)], writes=[K("rstd", b)])
            p.op("dve", lambda e, b=b, xnb=xnb, x=x: e.scalar_tensor_tensor(
                out=xnb[:], in0=x[:, b, :], scalar=rstd[:, b:b + 1], in1=gain[:], op0=ALU.mult, op1=ALU.mult),
                reads=[xk, K("rstd", b), K("gain")], writes=[xnk])
            for k in range(KC):
                p.op("pe", lambda e, xnb=xnb, k=k: e.transpose(
                    out=bfv(7)[:, k * 128:(k + 1) * 128], in_=xnb[:, k * 128:(k + 1) * 128], identity=ident[:]),
                    reads=[xnk, K("ident")], writes=[bk[7]])
            p.op("act", lambda e, b=b: e.copy(
                out=xnT[:, :, b * 128:(b + 1) * 128], in_=bfv(7).rearrange("p (k t) -> p k t", k=KC)),
                reads=[bk[7]], writes=[K("xnT", b)])
        xk2 = [K("xnT", 0), K("xnT", 1)]

        def proj_fm(bi, col0, ncols=128):
            for k in range(KC):
                p.op("pe", lambda e, k=k: e.matmul(bank[bi][0:ncols, 0:TM], lhsT=win[:, k, col0:col0 + ncols],
                                                   rhs=xnT[:, k, :], start=(k == 0), stop=(k == KC - 1)),
                     reads=xk2 + [K("win", k)], writes=[bk[bi]])

        def proj_tm(bi, b, col0, ncols, ocol=0):
            for k in range(KC):
                p.op("pe", lambda e, k=k: e.matmul(bank[bi][:, ocol:ocol + ncols], lhsT=xnT[:, k, b * 128:(b + 1) * 128],
                                                   rhs=win[:, k, col0:col0 + ncols], start=(k == 0), stop=(k == KC - 1)),
                     reads=[K("xnT", b), K("win", k)], writes=[bk[bi]])

        for c in range(2):
            proj_fm(c, c * 128)
            p.op("act", lambda e, c=c: e.copy(out=xa[:, c, 15:15 + TM], in_=bank[c][:, 0:TM]),
                 reads=[bk[c]], writes=[K("xa")])
        L = 15 + TM
        p.op("dve", lambda e: e.tensor_tensor(out=s2[:, :, 1:L], in0=xa[:, :, 1:L], in1=xa[:, :, 0:L - 1], op=ALU.add),
             reads=[K("xa")], writes=[K("s2")])
        p.op("dve", lambda e: e.tensor_tensor(out=s4[:, :, 3:L], in0=s2[:, :, 3:L], in1=s2[:, :, 1:L - 2], op=ALU.add),
             reads=[K("s2")], writes=[K("s4")])
        p.op("dve", lambda e: e.tensor_tensor(out=s2[:, 1, 7:L], in0=s4[:, 1, 7:L], in1=s4[:, 1, 3:L - 4], op=ALU.add),
             reads=[K("s4"), K("s2")], writes=[K("s2")])
        p.op("dve", lambda e: e.tensor_tensor(out=s4[64:128, 1, 15:L], in0=s2[64:128, 1, 15:L], in1=s2[64:128, 1, 7:L - 8],
                                              op=ALU.add),
             reads=[K("s2"), K("s4")], writes=[K("s4")])
        wh = 0 if ti == 0 else 1
        srcs = [(s2, 0, 0), (s4, 64, 0), (s2, 0, 1), (s4, 64, 1)]
        for g, (sbuf_, r0, c) in enumerate(srcs):
            p.op("dve", lambda e, sbuf_=sbuf_, r0=r0, c=c: e.tensor_tensor(
                out=sbuf_[r0:r0 + 64, c, 15:L], in0=sbuf_[r0:r0 + 64, c, 15:L], in1=invcnt[r0:r0 + 64, wh, c, :], op=ALU.mult),
                reads=[K("s2"), K("s4"), K("invcnt")], writes=[K("s2"), K("s4")])
            p.op("dve", lambda e, sbuf_=sbuf_, r0=r0, c=c: e.tensor_tensor(
                out=pd_[r0:r0 + 64, c, :], in0=sbuf_[r0:r0 + 64, c, 15:L], in1=xa[r0:r0 + 64, c, 15:L], op=ALU.subtract),
                reads=[K("s2"), K("s4"), K("xa")], writes=[K("pd")])
        for c in range(2):
            p.op("pe", lambda e, c=c: e.matmul(bank[c][:, 0:TM], lhsT=poolw[:, c, :], rhs=pd_[:, c, :], start=True, stop=True),
                 reads=[K("pd"), K("poolw")], writes=[bk[c]])
            p.op("act", lambda e, c=c: e.activation(out=catT[:, c, :], in_=bank[c][:, 0:TM], func=AF.Copy,
                                                   scale=pscale[:, c:c + 1]),
                 reads=[bk[c], K("pscale")], writes=[K("catT", c)])
        p.op("pool", lambda e: e.tensor_copy(out=xa[:, :, 0:15], in_=xa[:, :, TM:TM + 15]), reads=[K("xa")], writes=[K("xa")])

        for b in range(2):
            ab = 2 * ti + b
            slot = ab % NRING
            proj_tm(2, b, 256, 512)
            p.op("act", lambda e, b=b: e.copy(out=qk[b][:], in_=bank[2][:]), reads=[bk[2]], writes=[K("qk", b)])
            v4 = lambda t: t[:].rearrange("p (h s j) -> p h s j", h=8, s=2)
            cb_ = cosb[:, b, :].unsqueeze(1).to_broadcast([128, 8, 32])
            sb_ = sinb[:, b, :].unsqueeze(1).to_broadcast([128, 8, 32])
            r3 = lambda t: t[:].rearrange("p (h j) -> p h j", h=8)
            x1 = v4(qk[b])[:, :, 0, :]
            x2 = v4(qk[b])[:, :, 1, :]
            o1 = v4(qkr[b])[:, :, 0, :]
            o2 = v4(qkr[b])[:, :, 1, :]
            p.op("dve", lambda e, x1=x1, cb_=cb_: e.tensor_tensor(out=r3(rt[0]), in0=x1, in1=cb_, op=ALU.mult),
                 reads=[K("qk", b), K("cosb")], writes=[K("rt", 0)])
            p.op("dve", lambda e, x2=x2, sb_=sb_: e.tensor_tensor(out=r3(rt[1]), in0=x2, in1=sb_, op=ALU.mult),
                 reads=[K("qk", b), K("sinb")], writes=[K("rt", 1)])
            p.op("dve", lambda e, o1=o1: e.tensor_tensor(out=o1, in0=r3(rt[0]), in1=r3(rt[1]), op=ALU.subtract),
                 reads=[K("rt", 0), K("rt", 1)], writes=[K("qkr", b)])
            p.op("pool", lambda e, x2=x2, cb_=cb_: e.tensor_tensor(out=r3(rt[2]), in0=x2, in1=cb_, op=ALU.mult),
                 reads=[K("qk", b), K("cosb")], writes=[K("rt", 2)])
            p.op("pool", lambda e, x1=x1, sb_=sb_: e.tensor_tensor(out=r3(rt[3]), in0=x1, in1=sb_, op=ALU.mult),
                 reads=[K("qk", b), K("sinb")], writes=[K("rt", 3)])
            p.op("pool", lambda e, o2=o2: e.tensor_tensor(out=o2, in0=r3(rt[2]), in1=r3(rt[3]), op=ALU.add),
                 reads=[K("rt", 2), K("rt", 3)], writes=[K("qkr", b)])
            tok0 = t0 + b * 128
            if tok0 >= last_tok_base:
                o0 = tok0 - last_tok_base
                p.op("sp", lambda e, b=b, o0=o0: e.dma_start(out=outs["k"][o0:o0 + 128, :], in_=qkr[b][:, 256:512]),
                     reads=[K("qkr", b)], writes=[K("kout", tok0)], dma=True)
            p.op("act", lambda e, b=b: e.copy(out=qkb[:], in_=qkr[b][:]), reads=[K("qkr", b)], writes=[K("qkb")])
            for c in range(4):
                p.op("pe", lambda e, c=c: e.transpose(out=bfv(7)[:, c * 128:(c + 1) * 128], in_=qkb[:, c * 128:(c + 1) * 128],
                                                      identity=ident[:]),
                     reads=[K("qkb"), K("ident")], writes=[bk[7]])
            p.op("dve", lambda e, b=b: e.tensor_copy(out=qTz[0:64, :, 0, b * 128:(b + 1) * 128],
                                                     in_=bfv(7)[0:64, 0:256].rearrange("p (c t) -> p c t", c=2)),
                 reads=[bk[7], K("qTz")], writes=[K("qTz")])
            p.op("act", lambda e, b=b: e.copy(out=qTz[64:128, :, 1, b * 128:(b + 1) * 128],
                                             in_=bfv(7)[64:128, 0:256].rearrange("p (c t) -> p c t", c=2)),
                 reads=[bk[7], K("qTz")], writes=[K("qTz")])
            p.op("act", lambda e, slot=slot: e.copy(out=kring[:, :, slot * 128:(slot + 1) * 128],
                                                   in_=bfv(7)[:, 256:512].rearrange("p (c t) -> p c t", c=2)),
                 reads=[bk[7]], writes=[K("kring", slot)])
            proj_tm(3, b, 768, 256)
            p.op("act", lambda e, slot=slot: e.copy(out=vring[:, slot, :, 0:64],
                                                   in_=bank[3][:, 0:256].rearrange("p (h j) -> p h j", h=4)),
                 reads=[bk[3]], writes=[K("vring", slot)])
            if X_prev is not None:
                p.op("pool", lambda e, slot=slot: e.memset(vring[:, slot, :, 64:65], 1.0),
                     writes=[K("vring", slot)])
            if tok0 >= last_tok_base:
                o0 = tok0 - last_tok_base
                p.op("dve", lambda e, b=b: e.tensor_copy(out=vf[b][:], in_=bank[3][:, 0:256]), reads=[bk[3]], writes=[K("vf", b)])
                p.op("sp", lambda e, b=b, o0=o0: e.dma_start(out=outs["v"][o0:o0 + 128, :], in_=vf[b][:]),
                     reads=[K("vf", b)], writes=[K("vout", tok0)], dma=True)
            proj_tm(4, b, 1024, 512)
            p.op("act", lambda e, b=b: e.activation(out=zs[:, b, :], in_=bank[4][:], func=AF.Silu),
                 reads=[bk[4]], writes=[K("zs", b)])

        rs = [r for r in range(NRING) if (X_prev is not None) or 2 * ti - 16 + r >= 0]
        lastr = {0: max(r for r in rs if 0 <= 16 - r <= 16), 1: max(r for r in rs if 0 <= 17 - r <= 16)}
        for hp in range(2):
            first = {0: True, 1: True}
            for n_, r in enumerate(rs):
                ab = 2 * ti - 16 + r
                slot = ab % NRING
                js = [j for j in (0, 1) if 0 <= 16 + j - r <= 16]
                si = 5 + (n_ % 2)
                Pb = Pt[n_ % 3]
                pk = K("P", n_ % 3)
                for hh in range(2):
                    p.op("pe", lambda e: e.matmul(
                        bank[si][:, hh * TM:(hh + 1) * TM], lhsT=kring[:, hp, slot * 128:(slot + 1) * 128],
                        rhs=qTz[:, hp, hh, :], start=True, stop=True, skip_group_check=True),
                        reads=[K("kring", slot), K("qTz")], writes=[bk[si]])
                p.op("act", lambda e: e.activation(out=Pb[:].rearrange("p h t -> p (h t)"), in_=bank[si][:],
                                                   func=AF.Exp, scale=0.125),
                     reads=[bk[si]], writes=[pk])
                p.op("dve", lambda e: e.tensor_tensor(out=Pb[:], in0=Pb[:],
                                                      in1=masks[:, r, :].unsqueeze(1).to_broadcast([128, 2, TM]), op=ALU.mult),
                     reads=[pk, K("masks")], writes=[pk])
                for j in js:
                    for hh in range(2):
                        h = 2 * hp + hh
                        p.op("pe", lambda e, st_=(first[j] and hh == 0), sp_=(r == lastr[j]): e.matmul(
                            bank[3 + j][:, h * 65:(h + 1) * 65], lhsT=Pb[:, hh, j * 128:(j + 1) * 128], rhs=vring[:, slot, h, :],
                            start=st_, stop=sp_, skip_group_check=True),
                            reads=[pk, K("vring", slot)], writes=[bk[3 + j]])
                    first[j] = False
            for hh in range(2):
                h = 2 * hp + hh
                for j in range(2):
                    p.op("dve", lambda e: e.reciprocal(out=rl[:, 2 * h + j:2 * h + j + 1],
                                                       in_=bank[3 + j][:, h * 65 + 64:h * 65 + 65]),
                         reads=[bk[3 + j]], writes=[K("rl", h, j)])
                    p.op("dve", lambda e: e.tensor_scalar_mul(out=ybt[j][:, h * 64:(h + 1) * 64],
                                                              in0=bank[3 + j][:, h * 65:h * 65 + 64],
                                                              scalar1=rl[:, 2 * h + j:2 * h + j + 1]),
                         reads=[bk[3 + j], K("rl", h, j)], writes=[K("ybt", j)])
        for j in range(2):
            for c in range(2):
                p.op("pe", lambda e, j=j, c=c: e.transpose(out=bfv(7)[:, c * 128:(c + 1) * 128],
                                                          in_=ybt[j][:, c * 128:(c + 1) * 128], identity=ident[:]),
                     reads=[K("ybt", j), K("ident")], writes=[bk[7]])
            p.op("act", lambda e, j=j: e.copy(out=catT[:, 2:4, j * 128:(j + 1) * 128],
                                             in_=bfv(7)[:, 0:256].rearrange("p (c t) -> p c t", c=2)),
                 reads=[bk[7]], writes=[K("catT", 2, j)])

        for c in range(8):
            bi = c % 2
            cb_ = 2 + (c % 2)
            proj_fm(bi, 1536 + c * 128)
            p.op("act", lambda e, c=c, bi=bi: e.copy(out=xbc[:, c, 3:3 + TM], in_=bank[bi][:, 0:TM]),
                 reads=[bk[bi]], writes=[K("xbc", c)])
            for kk in range(4):
                p.op("pe", lambda e, c=c, kk=kk: e.matmul(bank[cb_][:, 0:TM], lhsT=dconv[:, c, kk, :], rhs=xbc[:, c, kk:kk + TM],
                                                          start=(kk == 0), stop=(kk == 3)),
                     reads=[K("xbc", c), K("dconv", c, kk)], writes=[bk[cb_]])
            p.op("act", lambda e, c=c: e.activation(out=uT[:, c, :], in_=bank[cb_][:, 0:TM], func=AF.Silu, bias=convb[:, c:c + 1]),
                 reads=[bk[cb_], K("convb")], writes=[K("uT", c)])
        p.op("pool", lambda e: e.tensor_copy(out=xbc[:, :, 0:3], in_=xbc[:, :, TM:TM + 3]),
             reads=[K("xbc", c_) for c_ in range(8)], writes=[K("xbc", c_) for c_ in range(8)])

        for b in range(2):
            cs = slice(b * 128, (b + 1) * 128)
            proj_tm(0, b, 2560, 8)
            p.op("dve", lambda e: e.tensor_tensor(out=dtr[:], in0=bank[0][:, 0:8], in1=dtb[:], op=ALU.add),
                 reads=[bk[0], K("dtb")], writes=[K("dtr")])
            p.op("dve", lambda e: e.tensor_scalar_mul(out=dtt[:], in0=dtr[:], scalar1=-1.0),
                 reads=[K("dtr")], writes=[K("dtt")])
            p.op("dve", lambda e: e.tensor_tensor(out=dtt[:], in0=dtt[:], in1=dtr[:], op=ALU.max),
                 reads=[K("dtr"), K("dtt")], writes=[K("dtt")])
            p.op("act", lambda e: e.activation(out=dtt[:], in_=dtt[:], func=AF.Exp, scale=-1.0), reads=[K("dtt")], writes=[K("dtt")])
            p.op("act", lambda e: e.activation(out=dtt[:], in_=dtt[:], func=AF.Ln, bias=1.0), reads=[K("dtt")], writes=[K("dtt")])
            p.op("dve", lambda e: e.scalar_tensor_tensor(out=dt1[:], in0=dtr[:], scalar=0.0, in1=dtt[:], op0=ALU.max, op1=ALU.add),
                 reads=[K("dtr"), K("dtt")], writes=[K("dt1")])
            p.op("dve", lambda e: e.tensor_tensor(out=dta[:], in0=dt1[:], in1=aneg[:], op=ALU.mult),
                 reads=[K("dt1"), K("aneg")], writes=[K("dta")])
            for c in range(6):
                p.op("pe", lambda e, c=c: e.transpose(out=bfv(7)[:, c * 128:(c + 1) * 128], in_=uT[:, c, cs], identity=ident[:]),
                     reads=[K("uT", c), K("ident")], writes=[bk[7]])
            p.op("act", lambda e: e.copy(out=xs_tm[:], in_=bfv(7)[:, 0:512]), reads=[bk[7]], writes=[K("xs_tm")])
            p.op("act", lambda e: e.copy(out=B_tm[:], in_=bfv(7)[:, 512:768]), reads=[bk[7]], writes=[K("B_tm")])
            p.op("pool", lambda e: e.tensor_tensor(out=R[:], in0=tri[:].unsqueeze(1).to_broadcast([128, 8, 128]),
                                                   in1=dta[:].unsqueeze(2).to_broadcast([128, 8, 128]), op=ALU.mult),
                 reads=[K("tri"), K("dta")], writes=[K("R")])
            for hf in range(2):
                p.op("pe", lambda e, hf=hf: e.matmul(bank[1 + hf][:], lhsT=ustr[:],
                                                     rhs=R[:, hf * 4:(hf + 1) * 4, :].rearrange("p h l -> p (h l)"),
                                                     start=True, stop=True),
                     reads=[K("ustr"), K("R")], writes=[bk[1 + hf]])
                p.op("act", lambda e, hf=hf: e.activation(out=E[:, hf * 4:(hf + 1) * 4, :].rearrange("p h l -> p (h l)"),
                                                         in_=bank[1 + hf][:], func=AF.Exp),
                     reads=[bk[1 + hf]], writes=[K("E", hf)])
            p.op("pe", lambda e: e.matmul(bank[0][:, 16:24], lhsT=tri[:], rhs=dta[:], start=True, stop=True),
                 reads=[K("tri"), K("dta")], writes=[bk[0]])
            p.op("pe", lambda e: e.matmul(bank[0][:, 32:40], lhsT=ones[:], rhs=dta[:], start=True, stop=True),
                 reads=[K("ones"), K("dta")], writes=[bk[0]])
            p.op("act", lambda e: e.activation(out=eacs[:], in_=bank[0][:, 16:24], func=AF.Exp), reads=[bk[0]], writes=[K("eacs")])
            p.op("act", lambda e: e.activation(out=cdb[:], in_=bank[0][:, 32:40], func=AF.Exp), reads=[bk[0]], writes=[K("cdb")])
            for g in range(2):
                p.op("pe", lambda e, g=g: e.matmul(bank[0][:, 128 + g * 128:256 + g * 128], lhsT=uT[:, 4 + g, cs], rhs=uT[:, 6 + g, cs],
                                                   start=True, stop=True),
                     reads=[K("uT", 4 + g), K("uT", 6 + g)], writes=[bk[0]])
            p.op("dve", lambda e: e.tensor_tensor(out=CBm[:], in0=bank[0][:, 128:384].rearrange("p (g l) -> p g l", g=2),
                                                  in1=tri[:].unsqueeze(1).to_broadcast([128, 2, 128]), op=ALU.mult),
                 reads=[bk[0], K("tri")], writes=[K("CBm")])
            p.op("dve", lambda e: e.tensor_tensor(out=MT[:].rearrange("p (g j) l -> p g j l", g=2),
                                                  in0=E[:].rearrange("p (g j) l -> p g j l", g=2),
                                                  in1=CBm[:].unsqueeze(2).to_broadcast([128, 2, 4, 128]), op=ALU.mult),
                 reads=[K("E", 0), K("E", 1), K("CBm")], writes=[K("MT")])
            p.op("pool", lambda e: e.tensor_tensor(out=xdt[:], in0=xs_tm[:].rearrange("p (h j) -> p h j", h=8),
                                                   in1=dt1[:].unsqueeze(2).to_broadcast([128, 8, 64]), op=ALU.mult),
                 reads=[K("xs_tm"), K("dt1")], writes=[K("xdt")])
            for h in range(8):
                p.op("pe", lambda e, h=h: e.matmul(bank[3][:, h * 64:(h + 1) * 64], lhsT=MT[:, h, :], rhs=xdt[:, h, :],
                                                   start=True, stop=True),
                     reads=[K("MT"), K("xdt")], writes=[bk[3]])
            for h in range(8):
                p.op("pe", lambda e, h=h: e.matmul(bank[4][:, h * 64:(h + 1) * 64], lhsT=uT[:, 6 + h // 4, cs], rhs=hTb[:, h, :],
                                                   start=True, stop=True),
                     reads=[K("uT", 6 + h // 4), K("hTb")], writes=[bk[4]])
            v8 = lambda t: t.rearrange("p (h j) -> p h j", h=8)
            p.op("dve", lambda e: e.tensor_tensor(out=v8(yt[:]), in0=v8(bank[4][:]),
                                                  in1=eacs[:].unsqueeze(2).to_broadcast([128, 8, 64]), op=ALU.mult),
                 reads=[bk[4], K("eacs")], writes=[K("yt")])
            p.op("dve", lambda e: e.tensor_tensor(out=yt[:], in0=yt[:], in1=bank[3][:], op=ALU.add),
                 reads=[bk[3], K("yt")], writes=[K("yt")])
            p.op("pool", lambda e: e.tensor_tensor(out=v8(y2[:]), in0=v8(xs_tm[:]),
                                                   in1=dskip[:].unsqueeze(2).to_broadcast([128, 8, 64]), op=ALU.mult),
                 reads=[K("xs_tm"), K("dskip")], writes=[K("y2")])
            p.op("pool", lambda e: e.tensor_tensor(out=yt[:], in0=yt[:], in1=y2[:], op=ALU.add),
                 reads=[K("yt"), K("y2")], writes=[K("yt")])
            p.op("pool", lambda e, b=b: e.tensor_tensor(out=yt[:], in0=yt[:], in1=zs[:, b, :], op=ALU.mult),
                 reads=[K("yt"), K("zs", b)], writes=[K("yt")])
            for g in range(2):
                p.op("act", lambda e, g=g: e.activation(out=y2[:, g * 256:(g + 1) * 256], in_=yt[:, g * 256:(g + 1) * 256],
                                                       func=AF.Square, accum_out=ssq[:, g:g + 1]),
                     reads=[K("yt")], writes=[K("y2"), K("ssq", g)])
                p.op("act", lambda e, g=g: e.activation(out=grs[:, g:g + 1], in_=ssq[:, g:g + 1], func=AF.Ln,
                                                       scale=1.0 / 256, bias=epsb[:]),
                     reads=[K("ssq", g), K("epsb")], writes=[K("grs", g)])
                p.op("act", lambda e, g=g: e.activation(out=grs[:, g:g + 1], in_=grs[:, g:g + 1], func=AF.Exp, scale=-0.5),
                     reads=[K("grs", g)], writes=[K("grs", g)])
                p.op("dve", lambda e, g=g: e.scalar_tensor_tensor(
                    out=yc[:, g * 256:(g + 1) * 256], in0=yt[:, g * 256:(g + 1) * 256], scalar=grs[:, g:g + 1],
                    in1=ssmn[:, g * 256:(g + 1) * 256], op0=ALU.mult, op1=ALU.mult),
                    reads=[K("yt"), K("grs", g), K("ssmn")], writes=[K("yc")])
            for c in range(4):
                p.op("pe", lambda e, c=c: e.transpose(out=bfv(7)[:, c * 128:(c + 1) * 128], in_=yc[:, c * 128:(c + 1) * 128],
                                                      identity=ident[:]),
                     reads=[K("yc"), K("ident")], writes=[bk[7]])
            p.op("act", lambda e, b=b: e.copy(out=catT[:, 4:8, b * 128:(b + 1) * 128],
                                             in_=bfv(7)[:, 0:512].rearrange("p (c t) -> p c t", c=4)),
                 reads=[bk[7]], writes=[K("catT", 4, b)])
            p.op("pool", lambda e: e.tensor_tensor(out=xde[:], in0=xdt[:], in1=E[:, :, 127:128].to_broadcast([128, 8, 64]),
                                                   op=ALU.mult),
                 reads=[K("xdt"), K("E", 0), K("E", 1)], writes=[K("xde")])
            for g in range(2):
                p.op("pe", lambda e, g=g: e.matmul(bank[2][:, g * 256:(g + 1) * 256], lhsT=B_tm[:, g * 128:(g + 1) * 128],
                                                   rhs=xde[:, g * 4:(g + 1) * 4, :].rearrange("p h j -> p (h j)"),
                                                   start=True, stop=True),
                     reads=[K("B_tm"), K("xde")], writes=[bk[2]])
            p.op("dve", lambda e: e.tensor_tensor(out=hT[:], in0=hT[:], in1=cdb[:].unsqueeze(2).to_broadcast([128, 8, 64]),
                                                  op=ALU.mult),
                 reads=[K("hT"), K("cdb")], writes=[K("hT")])
            p.op("dve", lambda e: e.tensor_tensor(out=hT[:].rearrange("p h j -> p (h j)"), in0=hT[:].rearrange("p h j -> p (h j)"),
                                                  in1=bank[2][:], op=ALU.add),
                 reads=[K("hT"), bk[2]], writes=[K("hT")])
            p.op("act", lambda e: e.copy(out=hTb[:], in_=hT[:]), reads=[K("hT")], writes=[K("hTb")])

        ckeys = [K("catT", 0), K("catT", 1), K("catT", 2, 0), K("catT", 2, 1), K("catT", 4, 0), K("catT", 4, 1)]
        i = 0
        for b in range(2):
            for hf in range(2):
                bi = 5 + (i % 2); i += 1
                for k in range(KC):
                    p.op("pe", lambda e, k=k, b=b, hf=hf, bi=bi: e.matmul(
                        bank[bi][:], lhsT=catT[:, k, b * 128:(b + 1) * 128], rhs=wout[:, k, hf * 512:(hf + 1) * 512],
                        start=(k == 0), stop=(k == KC - 1)),
                        reads=ckeys + [K("wout", k)], writes=[bk[bi]])
                p.op("dve", lambda e, b=b, hf=hf, bi=bi, x=x: e.tensor_tensor(
                    out=x[:, b, hf * 512:(hf + 1) * 512], in0=bank[bi][:], in1=x[:, b, hf * 512:(hf + 1) * 512], op=ALU.add),
                    reads=[bk[bi], xk], writes=[xk])
        p.op("sp", lambda e, x=x, t0=t0: e.dma_start(
            out=X_out[t0:t0 + TM, :].rearrange("(b p) d -> p b d", p=128), in_=x[:]),
            reads=[xk], writes=[K("Xout", ti)], dma=True)

        if ti == ntiles - 1:
            proj_tm(0, 1, 0, 256)
            proj_tm(1, 1, 1536, 512)
            proj_tm(2, 1, 2048, 512)
            p.op("act", lambda e: e.copy(out=stg[:, 0:256], in_=bank[0][:, 0:256]), reads=[bk[0]], writes=[K("stg", 0)])
            p.op("act", lambda e: e.copy(out=stg[:, 256:768], in_=bank[1][:]), reads=[bk[1]], writes=[K("stg", 1)])
            p.op("act", lambda e: e.copy(out=stg[:, 768:1280], in_=bank[2][:]), reads=[bk[2]], writes=[K("stg", 2)])
            p.op("sp", lambda e: e.dma_start(out=outs["pool"], in_=stg[113:128, 0:256]),
                 reads=[K("stg", 0)], writes=[K("poolout")], dma=True)
            p.op("sp", lambda e: e.dma_start(out=outs["conv"], in_=stg[125:128, 256:1280]),
                 reads=[K("stg", 1), K("stg", 2)], writes=[K("convout")], dma=True)
            for c in range(4):
                p.op("pe", lambda e, c=c: e.transpose(out=bank[3][:, c * 128:(c + 1) * 128],
                                                      in_=hT[:, 2 * c:2 * c + 2, :].rearrange("p h j -> p (h j)"),
                                                      identity=identf[:]),
                     reads=[K("hT"), K("identf")], writes=[bk[3]])
            p.op("act", lambda e: e.copy(out=hfin[:].rearrange("p c n -> p (c n)"), in_=bank[3][:]), reads=[bk[3]], writes=[K("hfin")])
            p.op("sp", lambda e: e.dma_start(out=outs["ssm"].rearrange("(c hp) n -> hp c n", c=4), in_=hfin[:]),
                 reads=[K("hfin")], writes=[K("ssmout")], dma=True)
    return dict(win=win, wout=wout, poolw=poolw, identf=identf, ident=ident)


DFF = 2816
FC = DFF // 128
NT = 8192
NS = 4
PAST = 16384
WBK = 2048


def ffn_phase(nc, p, st, tag, X_in, X_out, Xs_in, Xs_out, gain_d, wg_d, wu_d, wd_d, ntok,
              fin_gain=None, Y_out=None, Ys_out=None, on_tile_out=None):
    T = 512
    sb = lambda name, shape, dt: st.enter_context(nc.sbuf_tensor(tag + name, shape, dt))
    ps = lambda name, shape, dt: st.enter_context(nc.psum_tensor(tag + name, shape, dt))
    K = lambda *a: (tag,) + a
    wg = sb("wg", [128, KC, DFF], BF16)
    wu = sb("wu", [128, KC, DFF], BF16)
    wd = sb("wd", [128, FC, D], BF16)
    gain = sb("gain", [128, D], F32)
    ident = sb("ident", [128, 128], BF16)
    identf = sb("identf", [128, 128], F32)
    xt = [sb("x%d" % i, [128, 4, D], F32) for i in range(2)]
    xn_ = sb("xn0", [128, D], BF16)
    xn = [xn_, xn_]
    ss = sb("ss", [128, 8], F32)
    epsb = sb("epsb", [128, 1], F32)
    rstd = sb("rstd", [128, 8], F32)
    xnT = sb("xnT", [128, KC, T], BF16)
    hT = sb("hT", [128, FC, T], BF16)
    sg = [sb("sg%d" % i, [128, T], BF16) for i in range(2)]
    pT = [ps("pT%d" % i, [128, 1024], BF16) for i in range(2)]
    pg = [ps("pg%d" % i, [128, T], F32) for i in range(2)]
    pu = [ps("pu%d" % i, [128, T], F32) for i in range(2)]
    pd = [ps("pd%d" % i, [128, 512], F32) for i in range(2)]
    p.excl = set(p.excl) | {K("pT", i) for i in range(2)} | {K("pg", i) for i in range(2)} \
        | {K("pu", i) for i in range(2)} | {K("pd", i) for i in range(2)}
    if fin_gain is not None:
        fgain = xnT
        fgain = sb("fgain", [128, D], F32)

    p.op("pool", lambda e: e.memset(epsb[:], EPS), writes=[K("epsb")])
    p.op("pool", lambda e: e.memset(identf[:], 0.0), writes=[K("identf")])
    p.op("pool", lambda e: e.affine_select(out=identf[:], in_=identf[:], pattern=[[-1, 128]],
                                          compare_op=ALU.not_equal, fill=1.0, base=0, channel_multiplier=1),
         reads=[K("identf")], writes=[K("identf")])
    p.op("dve", lambda e: e.tensor_copy(out=ident[:], in_=identf[:]), reads=[K("identf")], writes=[K("ident")])
    p.op("sp", lambda e: e.dma_start(out=gain[:], in_=gain_d.partition_broadcast(128)), writes=[K("gain")], dma=True)
    CB = 1408
    for cb in range(0, DFF, CB):
        w_ = min(CB, DFF - cb)
        for k in range(KC):
            p.op("pool", lambda e, k=k: e.dma_start(out=wg[:, k, cb:cb + w_], in_=wg_d[k * 128:(k + 1) * 128, cb:cb + w_]),
                 writes=[K("wg", k, cb // CB)], dma=True)
            p.op("pool", lambda e, k=k: e.dma_start(out=wu[:, k, cb:cb + w_], in_=wu_d[k * 128:(k + 1) * 128, cb:cb + w_]),
                 writes=[K("wu", k, cb // CB)], dma=True)
    for c in range(FC):
        p.op("pool", lambda e, c=c: e.dma_start(out=wd[:, c, :], in_=wd_d[c * 128:(c + 1) * 128, :]),
             writes=[K("wd", c)], dma=True)
    if fin_gain is not None:
        p.op("sp", lambda e: e.dma_start(out=fgain[:], in_=fin_gain.partition_broadcast(128)), writes=[K("fgain")], dma=True)

    state = {"ti": 0}

    def do_tile(src, dst, ydst, P, nb):
        ti = state["ti"]
        state["ti"] += 1
        N = nb * P
        x = xt[ti % 2]
        xk = K("x", ti % 2)
        p.op("sp", lambda e: e.dma_start(out=x[0:P, 0:nb, :], in_=src.rearrange("(b p) d -> p b d", p=P)),
             writes=[xk], dma=True)
        for b in range(nb):
            xnb = xn[b % 2]
            xnk = K("xn", 0)
            p.op("act", lambda e: e.activation(out=xnb[0:P, :], in_=x[0:P, b, :], func=AF.Square, accum_out=ss[0:P, b:b + 1]),
                 reads=[xk], writes=[xnk, K("ss", b)])
            p.op("act", lambda e: e.activation(out=rstd[0:P, b:b + 1], in_=ss[0:P, b:b + 1], func=AF.Sqrt,
                                               scale=1.0 / D, bias=epsb[0:P, :]),
                 reads=[K("ss", b), K("epsb")], writes=[K("rstd", b)])
            p.op("dve", lambda e: e.reciprocal(out=rstd[0:P, b:b + 1], in_=rstd[0:P, b:b + 1]),
                 reads=[K("rstd", b)], writes=[K("rstd", b)])
            p.op("dve", lambda e: e.scalar_tensor_tensor(out=xnb[0:P, :], in0=x[0:P, b, :], scalar=rstd[0:P, b:b + 1],
                                                         in1=gain[0:P, :], op0=ALU.mult, op1=ALU.mult),
                 reads=[xk, K("rstd", b), K("gain")], writes=[xnk])
            pt = pT[b % 2]
            ptk = K("pT", b % 2)
            for k in range(KC):
                p.op("pe", lambda e, k=k: e.transpose(out=pt[:, k * P:(k + 1) * P], in_=xnb[0:P, k * 128:(k + 1) * 128],
                                                      identity=ident[0:P, 0:P]),
                     reads=[xnk, K("ident")], writes=[ptk])
            p.op("act", lambda e: e.copy(out=xnT[:, :, b * P:(b + 1) * P],
                                         in_=pt[:, 0:KC * P].rearrange("p (k t) -> p k t", k=KC)),
                 reads=[ptk], writes=[K("xnT", b)])
        xnT_keys = [K("xnT", b) for b in range(nb)]
        for c in range(FC):
            g_ = pg[c % 2]; u_ = pu[c % 2]; s_ = sg[c % 2]
            gk = K("pg", c % 2); uk = K("pu", c % 2); sk = K("sg", c % 2)
            for k in range(KC):
                p.op("pe", lambda e, k=k: e.matmul(g_[:, 0:N], lhsT=wg[:, k, c * 128:(c + 1) * 128], rhs=xnT[:, k, 0:N],
                                                   start=(k == 0), stop=(k == KC - 1)),
                     reads=xnT_keys + [K("wg", k, (c * 128) // 1408)], writes=[gk])
            for k in range(KC):
                p.op("pe", lambda e, k=k: e.matmul(u_[:, 0:N], lhsT=wu[:, k, c * 128:(c + 1) * 128], rhs=xnT[:, k, 0:N],
                                                   start=(k == 0), stop=(k == KC - 1)),
                     reads=xnT_keys + [K("wu", k, (c * 128) // 1408)], writes=[uk])
            p.op("act", lambda e: e.activation(out=s_[:, 0:N], in_=g_[:, 0:N], func=AF.Silu), reads=[gk], writes=[sk])
            p.op("dve", lambda e: e.tensor_tensor(out=hT[:, c, 0:N], in0=u_[:, 0:N], in1=s_[:, 0:N], op=ALU.mult),
                 reads=[uk, sk], writes=[K("hT", c)])
        i = 0
        for b in range(nb):
            for hf in range(2):
                d_ = pd[i % 2]; dk = K("pd", i % 2); i += 1
                for c in range(FC):
                    p.op("pe", lambda e, c=c: e.matmul(d_[0:P, :], lhsT=hT[:, c, b * P:(b + 1) * P],
                                                       rhs=wd[:, c, hf * 512:(hf + 1) * 512],
                                                       start=(c == 0), stop=(c == FC - 1)),
                         reads=[K("hT", c), K("wd", c)], writes=[dk])
                p.op("dve", lambda e: e.scalar_tensor_tensor(
                    out=x[0:P, b, hf * 512:(hf + 1) * 512], in0=d_[0:P, :], scalar=0.5,
                    in1=x[0:P, b, hf * 512:(hf + 1) * 512], op0=ALU.mult, op1=ALU.add),
                    reads=[dk, xk], writes=[xk])
            if fin_gain is not None:
                xnb = xn[b % 2]
                xnk = K("xn", 0)
                p.op("act", lambda e: e.activation(out=xnb[0:P, :], in_=x[0:P, b, :], func=AF.Square,
                                                   accum_out=ss[0:P, 4 + b:5 + b]),
                     reads=[xk], writes=[xnk, K("ss", 4 + b)])
                p.op("act", lambda e: e.activation(out=rstd[0:P, 4 + b:5 + b], in_=ss[0:P, 4 + b:5 + b], func=AF.Sqrt,
                                                   scale=1.0 / D, bias=epsb[0:P, :]),
                     reads=[K("ss", 4 + b), K("epsb")], writes=[K("rstd", 4 + b)])
                p.op("dve", lambda e: e.reciprocal(out=rstd[0:P, 4 + b:5 + b], in_=rstd[0:P, 4 + b:5 + b]),
                     reads=[K("rstd", 4 + b)], writes=[K("rstd", 4 + b)])
                p.op("dve", lambda e: e.scalar_tensor_tensor(out=x[0:P, b, :], in0=x[0:P, b, :], scalar=rstd[0:P, 4 + b:5 + b],
                                                             in1=fgain[0:P, :], op0=ALU.mult, op1=ALU.mult),
                     reads=[xk, K("rstd", 4 + b), K("fgain")], writes=[xk])
        out_ap = ydst if fin_gain is not None else dst
        p.op("sp", lambda e: e.dma_start(out=out_ap.rearrange("(b p) d -> p b d", p=P), in_=x[0:P, 0:nb, :]),
             reads=[xk], writes=[K("Xout", ti)], dma=True)
        if on_tile_out is not None and P == 128:
            on_tile_out(ti, K("Xout", ti))

    for t0 in range(0, ntok, T):
        n = min(T, ntok - t0)
        do_tile(X_in[t0:t0 + n, :], X_out[t0:t0 + n, :] if X_out is not None else None,
                Y_out[t0:t0 + n, :] if Y_out is not None else None, 128, n // 128)
    do_tile(Xs_in, Xs_out, Ys_out, NS, 1)


def sample_mixer_phase(nc, p, st, tag, Xs_in, Xs_out, W, C, caches, outs, scr, shared=None):
    sb = lambda name, shape, dt: st.enter_context(nc.sbuf_tensor(tag + name, shape, dt))
    ps = lambda name, shape, dt: st.enter_context(nc.psum_tensor(tag + name, shape, dt))
    K = lambda *a: (tag,) + a
    P = NS
    if shared is not None:
        win, wout, poolw, identf, ident = (shared[k_] for k_ in ("win", "wout", "poolw", "identf", "ident"))
    else:
        win = sb("win", [128, KC, DIN], BF16)
        wout = sb("wout", [128, KC, D], BF16)
        poolw = sb("poolw", [128, 2, 128], BF16)
        identf = sb("identf", [128, 128], F32)
        ident = sb("ident", [128, 128], BF16)
    epsb = sb("epsb", [128, 1], F32)
    x4 = sb("x4", [P, D], F32)
    xn4 = sb("xn4", [P, D], BF16)
    xnT4 = sb("xnT4", [128, KC, P], BF16)
    gain4 = sb("gain4", [P, D], F32)
    ss = sb("ss", [P, 4], F32)
    rstd = sb("rstd", [P, 4], F32)
    pr = sb("pr", [P, DIN], F32)
    xac = sb("xac", [P, 16, 256], F32)
    sums = sb("sums", [P, 256], F32)
    d4 = sb("d4", [P, 256], BF16)
    dT = sb("dT", [128, 2, P], BF16)
    psc4 = sb("psc4", [P, 256], F32)
    cat4 = sb("cat4", [P, D], BF16)
    catT4 = sb("catT4", [128, KC, P], BF16)
    xbcc = sb("xbcc", [P, 4, 1024], F32)
    cw = sb("cw", [P, 4, 1024], F32)
    cb4 = sb("cb4", [P, 1024], F32)
    u4 = sb("u4", [P, 1032], F32)
    dtb4 = sb("dtb4", [P, 8], F32)
    dtr = sb("dtr", [P, 8], F32)
    dtt = sb("dtt", [P, 8], F32)
    x32 = sb("x32", [32, 64], F32)
    dt32 = sb("dt32", [32, 1], F32)
    a32 = sb("a32", [32, 1], F32)
    dA32 = sb("dA32", [32, 1], F32)
    dtx32 = sb("dtx32", [32, 64], F32)
    BC32 = sb("BC32", [32, 2, 128], F32)
    hc = [sb("hc%d" % i, [32, 16, 128], F32) for i in range(2)]
    tmp = sb("tmp", [32, 16, 128], F32)
    y32 = sb("y32", [32, 64], F32)
    y4 = sb("y4", [P, 512], F32)
    t4 = sb("t4", [P, 512], F32)
    zs4 = sb("zs4", [P, 512], F32)
    dsk4 = sb("dsk4", [P, 8], F32)
    ssmn4 = sb("ssmn4", [P, 512], F32)
    ssq = sb("ssq", [P, 2], F32)
    grs = sb("grs", [P, 2], F32)
    cos4 = sb("cos4", [P, 32], F32)
    sin4 = sb("sin4", [P, 32], F32)
    qkr4 = sb("qkr4", [P, 512], F32)
    rt = [sb("rt%d" % i, [P, 256], F32) for i in range(4)]
    qb = sb("qb", [128, P, 256], F32)
    Kg = [sb("Kg%d" % i, [128, 256], F32) for i in range(2)]
    Va = [sb("Va%d" % i, [128, 4, 65], F32) for i in range(2)]
    prod = sb("prod", [128, 256], F32)
    sc = sb("sc", [128, 4], F32)
    Pz = sb("Pz", [128, P, 4, P], F32)
    O4 = sb("O4", [P, 4, 65], F32)
    sn = sb("sn", [P, 4], F32)
    p3 = sb("p3", [P, 4], F32)
    den = sb("den", [P, 4], F32)
    num = sb("num", [P, 4, 64], F32)
    bank = [ps("b%d" % i, [128, 512], F32) for i in range(8)]
    bk = [K("bank", i) for i in range(8)]
    p.excl = set(p.excl) | set(bk)
    bfv = lambda i: bank[i][:].bitcast(BF16)

    def ld(dst, src, key, eng="sp", reads=()):
        p.op(eng, lambda e: e.dma_start(out=dst, in_=src), reads=list(reads), writes=[key], dma=True)

    if shared is None:
        for k in range(KC):
            ld(win[:, k, :], W["w_in"][k * 128:(k + 1) * 128, :], K("win", k), "pool")
        for k in range(KC):
            ld(wout[:, k, :], W["w_out"][k * 128:(k + 1) * 128, :], K("wout", k), "pool")
        p.op("pool", lambda e: e.memset(poolw[:], 0.0), writes=[K("poolw")])
        for g in range(4):
            p.op("pool", lambda e, g=g: e.dma_start(
                out=poolw[(g % 2) * 64:(g % 2) * 64 + 64, g // 2, (g % 2) * 64:(g % 2) * 64 + 64], in_=W["pool_w"][g]),
                reads=[K("poolw")], writes=[K("poolw")], dma=True)
    p.op("pool", lambda e: e.memset(epsb[:], EPS), writes=[K("epsb")])
    p.op("pool", lambda e: e.memset(Pz[:], 0.0), writes=[K("Pz", n) for n in range(P)])
    for i in range(2):
        p.op("pool", lambda e, i=i: e.memset(Va[i][:, :, 64:65], 1.0), writes=[K("Va1", i)])
    if shared is None:
        ld(identf[:], C["identf"], K("identf"))
        p.op("dve", lambda e: e.tensor_copy(out=ident[:], in_=identf[:]), reads=[K("identf")], writes=[K("ident")])
    ld(gain4[:], W["mix_norm"].partition_broadcast(P), K("gain4"))
    ld(psc4[:], W["pool_scale"].partition_broadcast(P), K("psc4"))
    ld(cw[:].rearrange("p k c -> p (k c)"), W["conv_w"].rearrange("k c -> (k c)").partition_broadcast(P), K("cw"))
    ld(cb4[:], W["conv_b"].partition_broadcast(P), K("cb4"))
    ld(dtb4[:], W["dt_bias"].partition_broadcast(P), K("dtb4"))
    ld(dsk4[:], W["d_skip"].partition_broadcast(P), K("dsk4"))
    ld(ssmn4[:], W["ssm_norm"].partition_broadcast(P), K("ssmn4"))
    ld(cos4[:], C["cos_s"].partition_broadcast(P), K("cos4"))
    ld(sin4[:], C["sin_s"].partition_broadcast(P), K("sin4"))
    for n in range(P):
        ld(a32[n * 8:(n + 1) * 8, :], W["a_log"].rearrange("(h o) -> h o", o=1), K("a32"), reads=[K("a32")] if n else [])
    p.op("act", lambda e: e.activation(out=a32[:], in_=a32[:], func=AF.Exp), reads=[K("a32")], writes=[K("a32")])
    p.op("dve", lambda e: e.tensor_scalar_mul(out=a32[:], in0=a32[:], scalar1=-1.0), reads=[K("a32")], writes=[K("a32")])

    ld(x4[:], Xs_in, K("x4"))
    p.op("act", lambda e: e.activation(out=xn4[:], in_=x4[:], func=AF.Square, accum_out=ss[:, 0:1]),
         reads=[K("x4")], writes=[K("xn4"), K("ss")])
    p.op("act", lambda e: e.activation(out=rstd[:, 0:1], in_=ss[:, 0:1], func=AF.Sqrt, scale=1.0 / D, bias=epsb[0:P, :]),
         reads=[K("ss"), K("epsb")], writes=[K("rstd")])
    p.op("dve", lambda e: e.reciprocal(out=rstd[:, 0:1], in_=rstd[:, 0:1]), reads=[K("rstd")], writes=[K("rstd")])
    p.op("dve", lambda e: e.scalar_tensor_tensor(out=xn4[:], in0=x4[:], scalar=rstd[:, 0:1], in1=gain4[:],
                                                 op0=ALU.mult, op1=ALU.mult),
         reads=[K("x4"), K("rstd"), K("gain4")], writes=[K("xn4")])

    def transpose4(src, nchunk, dstT, skey, dkey):
        for k in range(nchunk):
            p.op("pe", lambda e, k=k: e.transpose(out=bfv(7)[:, k * P:(k + 1) * P], in_=src[0:P, k * 128:(k + 1) * 128],
                                                  identity=ident[0:P, 0:P]),
                 reads=[skey, K("ident")], writes=[bk[7]])
        p.op("act", lambda e: e.copy(out=dstT[:, 0:nchunk, :], in_=bfv(7)[:, 0:nchunk * P].rearrange("p (k t) -> p k t", k=nchunk)),
             reads=[bk[7]], writes=[dkey])

    transpose4(xn4, KC, xnT4, K("xn4"), K("xnT4"))
    for cc in range(6):
        c0 = cc * 512
        n = min(512, DIN - c0)
        bi = cc % 2
        for k in range(KC):
            p.op("pe", lambda e, k=k: e.matmul(bank[bi][0:P, 0:n], lhsT=xnT4[:, k, :], rhs=win[:, k, c0:c0 + n],
                                               start=(k == 0), stop=(k == KC - 1)),
                 reads=[K("xnT4"), K("win", k)], writes=[bk[bi]])
        p.op("act", lambda e: e.copy(out=pr[:, c0:c0 + n], in_=bank[bi][0:P, 0:n]), reads=[bk[bi]], writes=[K("pr", cc)])
    ld(xac[:, 0:15, :], caches["pool"], K("xac"))
    p.op("dve", lambda e: e.tensor_copy(out=xac[:, 15, :], in_=pr[:, 0:256]), reads=[K("pr", 0), K("xac")], writes=[K("xac")])
    ld(outs["pool"], xac[:, 1:16, :], K("o_pool"), reads=[K("xac")])
    for g, ws in enumerate((2, 4, 8, 16)):
        p.op("dve", lambda e, g=g, ws=ws: e.tensor_reduce(
            out=sums[:, g * 64:(g + 1) * 64], in_=xac[:, 16 - ws:16, g * 64:(g + 1) * 64].rearrange("p r c -> p c r"),
            axis=AX.X, op=ALU.add),
            reads=[K("xac")], writes=[K("sums")])
        p.op("dve", lambda e, g=g, ws=ws: e.scalar_tensor_tensor(
            out=d4[:, g * 64:(g + 1) * 64], in0=sums[:, g * 64:(g + 1) * 64], scalar=1.0 / ws,
            in1=xac[:, 15, g * 64:(g + 1) * 64], op0=ALU.mult, op1=ALU.subtract),
            reads=[K("sums"), K("xac")], writes=[K("d4")])
    transpose4(d4, 2, dT, K("d4"), K("dT"))
    for c in range(2):
        p.op("pe", lambda e, c=c: e.matmul(bank[2][0:P, c * 128:(c + 1) * 128], lhsT=dT[:, c, :], rhs=poolw[:, c, :],
                                           start=True, stop=True),
             reads=[K("dT"), K("poolw")], writes=[bk[2]])
    p.op("dve", lambda e: e.tensor_tensor(out=cat4[:, 0:256], in0=bank[2][0:P, 0:256], in1=psc4[:], op=ALU.mult),
         reads=[bk[2], K("psc4")], writes=[K("cat4", 0)])
    ld(xbcc[:, 0:3, :], caches["conv"], K("xbcc"))
    p.op("dve", lambda e: e.tensor_copy(out=xbcc[:, 3, :], in_=pr[:, 1536:2560]),
         reads=[K("pr", 3), K("pr", 4), K("xbcc")], writes=[K("xbcc")])
    ld(outs["conv"], xbcc[:, 1:4, :], K("o_conv"), reads=[K("xbcc")])
    p.op("dve", lambda e: e.tensor_tensor(out=cw[:], in0=cw[:], in1=xbcc[:], op=ALU.mult),
         reads=[K("cw"), K("xbcc")], writes=[K("cw")])
    p.op("dve", lambda e: e.tensor_reduce(out=u4[:, 0:1024], in_=cw[:].rearrange("p k c -> p c k"), axis=AX.X, op=ALU.add),
         reads=[K("cw")], writes=[K("u4")])
    p.op("dve", lambda e: e.tensor_tensor(out=u4[:, 0:1024], in0=u4[:, 0:1024], in1=cb4[:], op=ALU.add),
         reads=[K("u4"), K("cb4")], writes=[K("u4")])
    p.op("act", lambda e: e.activation(out=u4[:, 0:1024], in_=u4[:, 0:1024], func=AF.Silu), reads=[K("u4")], writes=[K("u4")])
    p.op("dve", lambda e: e.tensor_tensor(out=dtr[:], in0=pr[:, 2560:2568], in1=dtb4[:], op=ALU.add),
         reads=[K("pr", 5), K("dtb4")], writes=[K("dtr")])
    p.op("dve", lambda e: e.tensor_scalar_mul(out=dtt[:], in0=dtr[:], scalar1=-1.0), reads=[K("dtr")], writes=[K("dtt")])
    p.op("dve", lambda e: e.tensor_tensor(out=dtt[:], in0=dtt[:], in1=dtr[:], op=ALU.max), reads=[K("dtr"), K("dtt")], writes=[K("dtt")])
    p.op("act", lambda e: e.activation(out=dtt[:], in_=dtt[:], func=AF.Exp, scale=-1.0), reads=[K("dtt")], writes=[K("dtt")])
    p.op("act", lambda e: e.activation(out=dtt[:], in_=dtt[:], func=AF.Ln, bias=1.0), reads=[K("dtt")], writes=[K("dtt")])
    p.op("dve", lambda e: e.scalar_tensor_tensor(out=u4[:, 1024:1032], in0=dtr[:], scalar=0.0, in1=dtt[:], op0=ALU.max, op1=ALU.add),
         reads=[K("dtr"), K("dtt"), K("u4")], writes=[K("u4")])
    SSx, SSbc, SSdt, SS2, SQ = scr["SSx"], scr["SSbc"], scr["SSdt"], scr["SS2"], scr["SQ"]
    ld(SSx, u4[:, 0:512], K("SSx"), reads=[K("u4")])
    ld(SSbc, u4[:, 512:1024], K("SSbc"), reads=[K("u4")])
    ld(SSdt, u4[:, 1024:1032], K("SSdt"), reads=[K("u4")])
    ld(x32[:], SSx.rearrange("n (h q) -> (n h) q", h=8), K("x32"), reads=[K("SSx")])
    ld(dt32[:], SSdt.rearrange("n (h o) -> (n h) o", o=1), K("dt32"), reads=[K("SSdt")])
    for n in range(P):
        for g in range(2):
            src = SSbc[n].rearrange("(t g k) -> g t k", t=2, g=2)[g]
            ld(BC32[n * 8 + g * 4:n * 8 + g * 4 + 4, :, :], src.partition_broadcast(4), K("BC32", n, g), reads=[K("SSbc")])
    bckeys = [K("BC32", n, g) for n in range(P) for g in range(2)]
    p.op("act", lambda e: e.activation(out=dA32[:], in_=dt32[:], func=AF.Exp, scale=a32[:, 0:1]),
         reads=[K("dt32"), K("a32")], writes=[K("dA32")])
    p.op("dve", lambda e: e.tensor_scalar_mul(out=dtx32[:], in0=x32[:], scalar1=dt32[:, 0:1]),
         reads=[K("x32"), K("dt32")], writes=[K("dtx32")])
    hview = caches["ssm"].rearrange("n h (q r) k -> q (n h) r k", q=4)
    oview = outs["ssm"].rearrange("n h (q r) k -> q (n h) r k", q=4)
    for q in range(4):
        h_ = hc[q % 2]
        hk = K("hc", q % 2)
        ld(h_[:], hview[q], hk)
        p.op("dve", lambda e, q=q: e.tensor_tensor(out=tmp[:], in0=dtx32[:, q * 16:(q + 1) * 16].unsqueeze(2).to_broadcast([32, 16, 128]),
                                                   in1=BC32[:, 0, :].unsqueeze(1).to_broadcast([32, 16, 128]), op=ALU.mult),
             reads=[K("dtx32")] + bckeys, writes=[K("tmp")])
        p.op("dve", lambda e, h_=h_: e.scalar_tensor_tensor(out=h_[:], in0=h_[:], scalar=dA32[:, 0:1], in1=tmp[:],
                                                            op0=ALU.mult, op1=ALU.add),
             reads=[hk, K("dA32"), K("tmp")], writes=[hk])
        ld(oview[q], h_[:], K("o_ssm", q), reads=[hk])
        p.op("dve", lambda e, h_=h_: e.tensor_tensor(out=tmp[:], in0=h_[:], in1=BC32[:, 1, :].unsqueeze(1).to_broadcast([32, 16, 128]),
                                                     op=ALU.mult),
             reads=[hk] + bckeys, writes=[K("tmp")])
        p.op("dve", lambda e, q=q: e.tensor_reduce(out=y32[:, q * 16:(q + 1) * 16], in_=tmp[:], axis=AX.X, op=ALU.add),
             reads=[K("tmp")], writes=[K("y32")])
    ld(SS2.rearrange("n (h q) -> (n h) q", h=8), y32[:], K("SS2"), reads=[K("y32")])
    ld(y4[:], SS2, K("y4"), reads=[K("SS2")])
    v8 = lambda t: t.rearrange("p (h j) -> p h j", h=8)
    p.op("dve", lambda e: e.tensor_tensor(out=v8(t4[:]), in0=v8(u4[:, 0:512]), in1=dsk4[:].unsqueeze(2).to_broadcast([P, 8, 64]),
                                          op=ALU.mult),
         reads=[K("u4"), K("dsk4")], writes=[K("t4")])
    p.op("dve", lambda e: e.tensor_tensor(out=y4[:], in0=y4[:], in1=t4[:], op=ALU.add), reads=[K("y4"), K("t4")], writes=[K("y4")])
    p.op("act", lambda e: e.activation(out=zs4[:], in_=pr[:, 1024:1536], func=AF.Silu), reads=[K("pr", 2)], writes=[K("zs4")])
    p.op("dve", lambda e: e.tensor_tensor(out=y4[:], in0=y4[:], in1=zs4[:], op=ALU.mult), reads=[K("y4"), K("zs4")], writes=[K("y4")])
    for g in range(2):
        p.op("act", lambda e, g=g: e.activation(out=t4[:, g * 256:(g + 1) * 256], in_=y4[:, g * 256:(g + 1) * 256],
                                               func=AF.Square, accum_out=ssq[:, g:g + 1]),
             reads=[K("y4")], writes=[K("t4"), K("ssq", g)])
        p.op("act", lambda e, g=g: e.activation(out=grs[:, g:g + 1], in_=ssq[:, g:g + 1], func=AF.Sqrt, scale=1.0 / 256,
                                               bias=epsb[0:P, :]),
             reads=[K("ssq", g), K("epsb")], writes=[K("grs", g)])
        p.op("dve", lambda e, g=g: e.reciprocal(out=grs[:, g:g + 1], in_=grs[:, g:g + 1]), reads=[K("grs", g)], writes=[K("grs", g)])
        p.op("dve", lambda e, g=g: e.scalar_tensor_tensor(
            out=cat4[:, 512 + g * 256:768 + g * 256], in0=y4[:, g * 256:(g + 1) * 256], scalar=grs[:, g:g + 1],
            in1=ssmn4[:, g * 256:(g + 1) * 256], op0=ALU.mult, op1=ALU.mult),
            reads=[K("y4"), K("grs", g), K("ssmn4")], writes=[K("cat4", 2 + g)])
    v4 = lambda t: t.rearrange("p (h s j) -> p h s j", h=8, s=2)
    r3 = lambda t: t[:].rearrange("p (h j) -> p h j", h=8)
    cb_ = cos4[:].unsqueeze(1).to_broadcast([P, 8, 32])
    sb_ = sin4[:].unsqueeze(1).to_broadcast([P, 8, 32])
    x1 = v4(pr[:, 256:768])[:, :, 0, :]
    x2 = v4(pr[:, 256:768])[:, :, 1, :]
    prk = [K("pr", 0), K("pr", 1)]
    p.op("dve", lambda e: e.tensor_tensor(out=r3(rt[0]), in0=x1, in1=cb_, op=ALU.mult), reads=prk + [K("cos4")], writes=[K("rt", 0)])
    p.op("dve", lambda e: e.tensor_tensor(out=r3(rt[1]), in0=x2, in1=sb_, op=ALU.mult), reads=prk + [K("sin4")], writes=[K("rt", 1)])
    p.op("dve", lambda e: e.tensor_tensor(out=v4(qkr4[:])[:, :, 0, :], in0=r3(rt[0]), in1=r3(rt[1]), op=ALU.subtract),
         reads=[K("rt", 0), K("rt", 1)], writes=[K("qkr4")])
    p.op("dve", lambda e: e.tensor_tensor(out=r3(rt[2]), in0=x2, in1=cb_, op=ALU.mult), reads=prk + [K("cos4")], writes=[K("rt", 2)])
    p.op("dve", lambda e: e.tensor_tensor(out=r3(rt[3]), in0=x1, in1=sb_, op=ALU.mult), reads=prk + [K("sin4")], writes=[K("rt", 3)])
    p.op("dve", lambda e: e.tensor_tensor(out=v4(qkr4[:])[:, :, 1, :], in0=r3(rt[2]), in1=r3(rt[3]), op=ALU.add),
         reads=[K("rt", 2), K("rt", 3), K("qkr4")], writes=[K("qkr4")])
    ld(outs["k"][:, WBK - 1, :], qkr4[:, 256:512], K("o_k1"), reads=[K("qkr4")])
    ld(outs["v"][:, WBK - 1, :], pr[:, 768:1024], K("o_v1"), reads=[K("pr", 1)])
    ld(SQ, qkr4[:, 0:256], K("SQ"), reads=[K("qkr4")])
    ld(qb[:].rearrange("p n c -> p (n c)"), SQ.rearrange("n c -> (n c)").partition_broadcast(128), K("qb"), reads=[K("SQ")])
    it = 0
    nmm = P * 3 * 4
    imm = 0
    for n in range(P):
        for dil in (1, 4, 16):
            kg = Kg[it % 2]; va = Va[it % 2]
            kk = K("Kg", it % 2); vk = K("Va", it % 2)
            s0 = WBK - 128 * dil
            ld(kg[:], caches["k"][n, s0:WBK:dil, :], kk)
            ld(va[:, :, 0:64], caches["v"][n, s0:WBK:dil, :].rearrange("r (h j) -> r h j", h=4), vk)
            p.op("dve", lambda e, n=n, kg=kg: e.tensor_tensor(out=prod[:], in0=kg[:], in1=qb[:, n, :], op=ALU.mult),
                 reads=[kk, K("qb")], writes=[K("prod")])
            p.op("dve", lambda e: e.tensor_reduce(out=sc[:], in_=prod[:].rearrange("p (h j) -> p h j", h=4), axis=AX.X, op=ALU.add),
                 reads=[K("prod")], writes=[K("sc")])
            p.op("act", lambda e, n=n: e.activation(out=Pz[:, n, :, n], in_=sc[:], func=AF.Exp, scale=0.125),
                 reads=[K("sc")], writes=[K("Pz", n)])
            for h in range(4):
                p.op("pe", lambda e, n=n, h=h, va=va, st_=(imm == 0), sp_=(imm == nmm - 1): e.matmul(
                    bank[3][0:P, h * 65:(h + 1) * 65], lhsT=Pz[:, n, h, :], rhs=va[:, h, :], start=st_, stop=sp_,
                    skip_group_check=True),
                    reads=[K("Pz", n), vk, K("Va1", it % 2)], writes=[bk[3]])
                imm += 1
            it += 1
    p.op("act", lambda e: e.copy(out=O4[:].rearrange("p h j -> p (h j)"), in_=bank[3][0:P, 0:260]), reads=[bk[3]], writes=[K("O4")])
    q4v = qkr4[:, 0:256].rearrange("p (h j) -> p h j", h=4)
    k4v = qkr4[:, 256:512].rearrange("p (h j) -> p h j", h=4)
    v4n = pr[:, 768:1024].rearrange("p (h j) -> p h j", h=4)
    p.op("dve", lambda e: e.tensor_tensor(out=num[:], in0=q4v, in1=k4v, op=ALU.mult), reads=[K("qkr4")], writes=[K("num")])
    p.op("dve", lambda e: e.tensor_reduce(out=sn[:], in_=num[:], axis=AX.X, op=ALU.add), reads=[K("num")], writes=[K("sn")])
    p.op("act", lambda e: e.activation(out=p3[:], in_=sn[:], func=AF.Exp, scale=0.125), reads=[K("sn")], writes=[K("p3")])
    p.op("dve", lambda e: e.tensor_scalar_mul(out=p3[:], in0=p3[:], scalar1=3.0), reads=[K("p3")], writes=[K("p3")])
    p.op("dve", lambda e: e.tensor_tensor(out=num[:], in0=v4n, in1=p3[:].unsqueeze(2).to_broadcast([P, 4, 64]), op=ALU.mult),
         reads=[K("pr", 1), K("p3"), K("num")], writes=[K("num")])
    p.op("dve", lambda e: e.tensor_tensor(out=num[:], in0=num[:], in1=O4[:, :, 0:64], op=ALU.add),
         reads=[K("num"), K("O4")], writes=[K("num")])
    p.op("dve", lambda e: e.tensor_tensor(out=den[:].unsqueeze(2), in0=O4[:, :, 64:65], in1=p3[:].unsqueeze(2), op=ALU.add),
         reads=[K("O4"), K("p3")], writes=[K("den")])
    p.op("dve", lambda e: e.reciprocal(out=den[:], in_=den[:]), reads=[K("den")], writes=[K("den")])
    p.op("dve", lambda e: e.tensor_tensor(out=cat4[:, 256:512].rearrange("p (h j) -> p h j", h=4), in0=num[:],
                                          in1=den[:].unsqueeze(2).to_broadcast([P, 4, 64]), op=ALU.mult),
         reads=[K("num"), K("den")], writes=[K("cat4", 1)])
    catkeys = [K("cat4", i) for i in range(4)]
    for k in range(KC):
        p.op("pe", lambda e, k=k: e.transpose(out=bfv(7)[:, k * P:(k + 1) * P], in_=cat4[0:P, k * 128:(k + 1) * 128],
                                              identity=ident[0:P, 0:P]),
             reads=catkeys + [K("ident")], writes=[bk[7]])
    p.op("act", lambda e: e.copy(out=catT4[:], in_=bfv(7)[:, 0:KC * P].rearrange("p (k t) -> p k t", k=KC)),
         reads=[bk[7]], writes=[K("catT4")])
    for hf in range(2):
        for k in range(KC):
            p.op("pe", lambda e, k=k, hf=hf: e.matmul(bank[4 + hf][0:P, :], lhsT=catT4[:, k, :], rhs=wout[:, k, hf * 512:(hf + 1) * 512],
                                                      start=(k == 0), stop=(k == KC - 1)),
                 reads=[K("catT4"), K("wout", k)], writes=[bk[4 + hf]])
        p.op("dve", lambda e, hf=hf: e.tensor_tensor(out=x4[:, hf * 512:(hf + 1) * 512], in0=bank[4 + hf][0:P, :],
                                                     in1=x4[:, hf * 512:(hf + 1) * 512], op=ALU.add),
             reads=[bk[4 + hf], K("x4")], writes=[K("x4")])
    ld(Xs_out, x4[:], K("Xs_out"), reads=[K("x4")])


WNAMES = ["ffn1_norm", "ffn1_w_gate", "ffn1_w_up", "ffn1_w_down", "mix_norm", "w_in", "pool_w", "pool_scale", "conv_w", "conv_b",
          "dt_bias", "a_log", "d_skip", "ssm_norm", "w_out", "ffn2_norm", "ffn2_w_gate", "ffn2_w_up", "ffn2_w_down"]
WSHAPES = {"ffn1_norm": [2, D], "ffn1_w_gate": [2, D, DFF], "ffn1_w_up": [2, D, DFF], "ffn1_w_down": [2, DFF, D],
           "mix_norm": [2, D], "w_in": [2, D, DIN], "pool_w": [2, 4, 64, 64], "pool_scale": [2, 256], "conv_w": [2, 4, 1024],
           "conv_b": [2, 1024], "dt_bias": [2, 8], "a_log": [2, 8], "d_skip": [2, 8], "ssm_norm": [2, 512], "w_out": [2, D, D],
           "ffn2_norm": [2, D], "ffn2_w_gate": [2, D, DFF], "ffn2_w_up": [2, D, DFF], "ffn2_w_down": [2, DFF, D],
           "pool_scale_l": [2, 128, 2], "conv_w_l": [2, 128, 8, 4], "conv_b_l": [2, 128, 8], "final_norm": [D]}
OUTSHAPES = {"y": [NT, D], "ys": [NS, D], "pool_p": [2, 15, 256], "pool_s": [2, NS, 15, 256], "k_p": [2, WBK, 256],
             "k_s": [2, NS, WBK, 256], "v_p": [2, WBK, 256], "v_s": [2, NS, WBK, 256], "conv_p": [2, 3, 1024],
             "conv_s": [2, NS, 3, 1024], "ssm_p": [2, 512, 128], "ssm_s": [2, NS, 8, 64, 128]}
CACHESHAPES = {"cache_pool": [2, NS, 15, 256], "cache_k": [2, NS, WBK, 256], "cache_v": [2, NS, WBK, 256],
               "state_conv": [2, NS, 3, 1024], "state_ssm": [2, NS, 8, 64, 128]}


def build_program(consts, ntc=NT // 2, depth=2, split=True):
    nc = bass.Bass("TRN2", target_bir_lowering=False)
    di = lambda n, s, dt=F32: nc.dram_tensor(n, list(s), dt, kind="ExternalInput").ap()
    do = lambda n, s: nc.dram_tensor(n, list(s), F32, kind="ExternalOutput").ap()
    X = di("x", [ntc, D])
    XS = di("xs", [NS, D])
    Wd = {k: di(k, v) for k, v in WSHAPES.items()}
    Cd = {k: di("c_" + k, v.shape, BF16 if k == "masks" else F32) for k, v in consts.items()}
    CA = {k: di(k, v) for k, v in CACHESHAPES.items()}
    osh = dict(OUTSHAPES)
    osh["y"] = [ntc, D]
    wbp = min(WBK, ntc)
    osh["k_p"] = [2, wbp, 256]
    osh["v_p"] = [2, wbp, 256]
    O = {k: do(k, v) for k, v in osh.items()}
    Xa = nc.dram_tensor("scr_xa", [ntc, D], F32, kind="Internal").ap()
    Xb = nc.dram_tensor("scr_xb", [ntc, D], F32, kind="Internal").ap()
    Sa = nc.dram_tensor("scr_sa", [NS, D], F32, kind="Internal").ap()
    Sb = nc.dram_tensor("scr_sb", [NS, D], F32, kind="Internal").ap()
    nch = ntc // 512
    G = [[nc.dram_tensor("scr_g%d_%d" % (L, j), [1024, D], F32, kind="Internal").ap() for j in range(nch)]
         for L in range(depth)] if split else None
    scr = {"SSx": nc.dram_tensor("scr_ssx", [NS, 512], F32, kind="Internal").ap(),
           "SSbc": nc.dram_tensor("scr_ssbc", [NS, 512], F32, kind="Internal").ap(),
           "SSdt": nc.dram_tensor("scr_ssdt", [NS, 8], F32, kind="Internal").ap(),
           "SS2": nc.dram_tensor("scr_ss2", [NS, 512], F32, kind="Internal").ap(),
           "SQ": nc.dram_tensor("scr_sq", [NS, 256], F32, kind="Internal").ap()}
    RG = [[0, 1], [2, 3], [4, 5], [6, 7]]
    with ExitStack() as top:
        p = Prog(nc)
        cur, cur_s = X, XS
        for L in range(depth):
            last = (L == depth - 1)
            W = {k: v[L] for k, v in Wd.items() if k != "final_norm"}

            def hook(ti, key, L=L):
                if ti == min(1, ntc // 512 - 1):
                    shift_copies(key)
                if not split:
                    return
                p.op("pool", lambda e: e.collective_compute("AllGather", ALU.bypass, replica_groups=RG,
                                                            ins=[Xa[ti * 512:(ti + 1) * 512, :]], outs=[G[L][ti]]),
                     reads=[key], writes=[("G", L, ti)], cc=True)

            def shift_copies(after_key, L=L):
                for nm_, src_, dst_ in (("k", CA["cache_k"][L], O["k_s"][L]), ("v", CA["cache_v"][L], O["v_s"][L])):
                    for n_ in range(NS):
                        p.op("act", lambda e: e.dma_start(out=dst_[n_, 0:WBK - 1, :], in_=src_[n_, 1:WBK, :]),
                             reads=[after_key], writes=[("cshift", L, nm_, n_)], dma=True)

            with ExitStack() as st:
                ffn_phase(nc, p, st, "f1_%d" % L, cur, Xa, cur_s, Sa, W["ffn1_norm"], W["ffn1_w_gate"], W["ffn1_w_up"],
                          W["ffn1_w_down"], ntc, on_tile_out=hook)
                p.barrier()
            with ExitStack() as wst:
                with ExitStack() as st:
                    outs = dict(pool=O["pool_p"][L], k=O["k_p"][L], v=O["v_p"][L], conv=O["conv_p"][L], ssm=O["ssm_p"][L])
                    xprev = (lambda pi, L=L: G[L][pi // 2][(pi % 2) * 256:(pi % 2) * 256 + 256, :]) if split else None
                    shared = mixer_phase(nc, p, st, "m%d" % L, ntc, Xa, Xb, W, Cd, outs, ntc - wbp, X_prev=xprev, npre=ntc // TM,
                                         wst=wst)
                    p.barrier()
                with ExitStack() as st:
                    caches = dict(pool=CA["cache_pool"][L], k=CA["cache_k"][L], v=CA["cache_v"][L], conv=CA["state_conv"][L],
                                  ssm=CA["state_ssm"][L])
                    outs = dict(pool=O["pool_s"][L], k=O["k_s"][L], v=O["v_s"][L], conv=O["conv_s"][L], ssm=O["ssm_s"][L])
                    sample_mixer_phase(nc, p, st, "s%d" % L, Sa, Sb, W, Cd, caches, outs, scr, shared=shared)
                    p.barrier()
            with ExitStack() as st:
                ffn_phase(nc, p, st, "f2_%d" % L, Xb, Xa, Sb, Sa, W["ffn2_norm"], W["ffn2_w_gate"], W["ffn2_w_up"],
                          W["ffn2_w_down"], ntc, fin_gain=(Wd["final_norm"] if last else None),
                          Y_out=(O["y"] if last else None), Ys_out=(O["ys"] if last else None))
                p.barrier()
            cur, cur_s = Xa, Sa
        p.finish()
        p.emit(top)
    return nc


def make_consts(ntc=NT // 2, half=0, split=True):
    c = host_consts(ntc, pos0=half * ntc)
    if split:
        prev = host_consts(ntc, pos0=(half - 1) * ntc)
        c["cos_prev"], c["sin_prev"] = prev["cos"], prev["sin"]
        c["flag"] = np.full((128, 1), float(half), np.float32)
        if half > 0:
            c["invcnt"][0] = c["invcnt"][1]
    h = 32
    inv = (10000.0 ** (-np.arange(h, dtype=np.float32) / h)).astype(np.float32)
    ang = np.float32(PAST) * inv
    c["cos_s"] = np.cos(ang).astype(np.float32)
    c["sin_s"] = np.sin(ang).astype(np.float32)
    return c


def make_in_maps(inputs, ntc=NT // 2, split=True):
    f = lambda a: np.ascontiguousarray(np.asarray(a, dtype=np.float32))
    shared = {k: f(inputs[k]) for k in WNAMES}
    shared["final_norm"] = f(inputs["final_norm"])
    shared["pool_scale_l"] = f(np.asarray(inputs["pool_scale"]).reshape(2, 2, 128).transpose(0, 2, 1))
    shared["conv_w_l"] = f(np.asarray(inputs["conv_w"]).reshape(2, 4, 8, 128).transpose(0, 3, 2, 1))
    shared["conv_b_l"] = f(np.asarray(inputs["conv_b"]).reshape(2, 8, 128).transpose(0, 2, 1))
    cons = [make_consts(ntc, hf, split) for hf in range(2 if split else 1)]
    maps = []
    xp = np.asarray(inputs["x_prompt"])
    xs = np.asarray(inputs["x_sample"])
    for c in range(8):
        seq, hf = (c // 2, c % 2) if split else (c % 4, 0)
        m = dict(shared)
        for k, v in cons[hf].items():
            m["c_" + k] = v
        m["x"] = f(xp[seq, hf * ntc:(hf + 1) * ntc])
        sl = slice(c * NS, (c + 1) * NS)
        m["xs"] = f(xs[sl, 0])
        m["cache_pool"] = f(np.asarray(inputs["cache_pool"])[:, sl])
        m["cache_k"] = f(np.asarray(inputs["cache_k"])[:, sl].reshape(2, NS, WBK, 256))
        m["cache_v"] = f(np.asarray(inputs["cache_v"])[:, sl].reshape(2, NS, WBK, 256))
        m["state_conv"] = f(np.asarray(inputs["state_conv"])[:, sl])
        m["state_ssm"] = f(np.asarray(inputs["state_ssm"])[:, sl])
        maps.append(m)
    return maps, cons[0]


def gather_outputs(r, wbp=WBK):
    cat = lambda name, rng, ax: np.concatenate([r[c][name] for c in rng], axis=ax)
    y_prompt = np.stack([np.concatenate([r[2 * s_]["y"], r[2 * s_ + 1]["y"]], 0) for s_ in range(4)], 0)
    y_sample = cat("ys", range(8), 0).reshape(32, 1, D)
    hi = [2 * s_ + 1 for s_ in range(4)]
    pool_p = np.stack([r[c]["pool_p"] for c in hi], 1)
    pool_s = cat("pool_s", range(8), 1)
    k_p = np.stack([r[c]["k_p"] for c in hi], 1).reshape(2, 4, wbp, 4, 64)
    k_s = cat("k_s", range(8), 1).reshape(2, 32, WBK, 4, 64)
    v_p = np.stack([r[c]["v_p"] for c in hi], 1).reshape(2, 4, wbp, 4, 64)
    v_s = cat("v_s", range(8), 1).reshape(2, 32, WBK, 4, 64)
    conv_p = np.stack([r[c]["conv_p"] for c in hi], 1)
    conv_s = cat("conv_s", range(8), 1)
    ssm_p = np.stack([r[c]["ssm_p"] for c in hi], 1).reshape(2, 4, 8, 64, 128)
    ssm_s = cat("ssm_s", range(8), 1)
    outs = (y_prompt, y_sample, pool_p, pool_s, k_p, k_s, v_p, v_s, conv_p, conv_s, ssm_p, ssm_s)
    return tuple(np.ascontiguousarray(o, dtype=np.float32) for o in outs)


def kernel(**inputs):
    ntc = NT // 2
    maps, c0 = make_in_maps(inputs, ntc, True)
    nc = build_program(c0, ntc, 2, True)
    res = run_bass_kernel_spmd(nc, maps, core_ids=list(range(8)))
    return gather_outputs(res.results)
```

```python
import numpy as np
from contextlib import ExitStack
import concourse.bass as bass
import concourse.mybir as mybir
from concourse.bass_utils import run_bass_kernel_spmd

F32 = mybir.dt.float32
BF16 = mybir.dt.bfloat16
AF = mybir.ActivationFunctionType
ALU = mybir.AluOpType
AX = mybir.AxisListType

ENGS = ("pe", "act", "dve", "pool", "sp")
NDMASEM = 24


class _Rec:
    def __getattr__(self, name):
        def f(*a, **k):
            self.call = (name, a, k)
            return self
        return f


def _free_elems(ap):
    n = 1
    for s in ap.shape[1:]:
        n *= int(s)
    return n


def _est_cost(eng, call, dma):
    name, a, k = call
    out = k.get("out", a[0] if a else None)
    try:
        n = _free_elems(out)
        parts = int(out.shape[0])
    except Exception:
        n, parts = 256, 128
    if dma:
        byt = n * parts * 4
        return (700.0 if eng == "pool" else 120.0), 2200.0 + byt / 80.0
    if eng == "pe":
        c = 70.0 + 0.45 * max(n, 64)
        try:
            lhs = k.get("lhsT", None)
            if lhs is not None and lhs.dtype == F32:
                c = 70.0 + 1.8 * max(n, 64)
        except Exception:
            pass
        return c, c
    if eng == "act":
        c = 230.0 + 0.62 * n
    elif eng == "dve":
        c = 120.0 + 1.0 * n
    elif eng == "pool":
        c = 300.0 + 3.0 * n
        if name in ("tensor_scalar_mul", "tensor_scalar"):
            c = 300.0 + 15.0 * n
    else:
        c = 100.0
    return c, c


class Prog:
    def __init__(self, nc, schedule=True):
        self.nc = nc
        self.nodes = []
        self.last_w = {}
        self.readers = {}
        self.excl = set()
        self.seg = 0
        self.schedule = schedule

    def op(self, eng, fn, reads=(), writes=(), dma=False, cc=False):
        rec = _Rec()
        fn(rec)
        xr = [b for b in reads if b in self.excl]
        if xr:
            writes = list(writes) + [b for b in xr if b not in writes]
        deps = set()
        for b in reads:
            if b in self.last_w:
                deps.add(self.last_w[b])
        for b in writes:
            if b in self.last_w:
                deps.add(self.last_w[b])
            deps.update(self.readers.get(b, ()))
        nid = len(self.nodes)
        busy, lat = _est_cost(eng, rec.call, dma)
        if cc:
            busy, lat = 1500.0, 60000.0
        tbl = 0
        if eng == "act" and rec.call[0] == "activation":
            f_ = rec.call[2].get("func", None)
            tbl = 1 if f_ in (AF.Exp, AF.Ln) else 2 if f_ == AF.Silu else 3 if f_ == AF.Sqrt else 0
        self.nodes.append(dict(eng=eng, call=rec.call, dma=dma, cc=cc, deps=deps, seg=self.seg, busy=busy, lat=lat, tbl=tbl))
        for b in writes:
            self.last_w[b] = nid
            self.readers[b] = []
        for b in reads:
            if b not in writes:
                self.readers.setdefault(b, []).append(nid)
        return nid

    def barrier(self):
        self.seg += 1
        self.last_w.clear()
        self.readers.clear()

    def finish(self, eng="sp"):
        self.final_eng = eng

    def _sched(self, ids):
        nodes = self.nodes
        if not self.schedule:
            return list(ids)
        idset = set(ids)
        succ = {i: [] for i in ids}
        ndep = {}
        for i in ids:
            ds = [d for d in nodes[i]["deps"] if d in idset]
            ndep[i] = len(ds)
            for d in ds:
                succ[d].append(i)
        prio = {}
        for i in reversed(ids):
            m = 0.0
            for s in succ[i]:
                if prio[s] > m:
                    m = prio[s]
            prio[i] = nodes[i]["lat"] + m
        ready = {e: [] for e in ENGS}
        for i in ids:
            if ndep[i] == 0:
                ready[nodes[i]["eng"]].append(i)
        efree = {e: 0.0 for e in ENGS}
        fin = {}
        rdy_t = {i: 0.0 for i in ids}
        order = []
        SYNC = 750.0
        WINDOW = 6000
        remaining = len(ids)
        cur_tbl = 0
        lo = 0
        done = set()
        while remaining:
            while lo < len(ids) and ids[lo] in done:
                lo += 1
            limit = ids[min(lo + WINDOW, len(ids) - 1)]
            best = None
            for e in ENGS:
                r = ready[e]
                if not r:
                    continue
                cb = None
                for i in r:
                    if i > limit:
                        continue
                    st = max(efree[e], rdy_t[i])
                    tb = nodes[i]["tbl"]
                    if tb and tb != cur_tbl:
                        st += 1300.0
                    key = (st, -prio[i], i)
                    if cb is None or key < cb[0]:
                        cb = (key, i)
                if cb is not None and (best is None or cb[0] < best[0]):
                    best = cb
            key, i = best
            e = nodes[i]["eng"]
            st = key[0]
            ready[e].remove(i)
            if nodes[i]["tbl"]:
                cur_tbl = nodes[i]["tbl"]
            efree[e] = st + nodes[i]["busy"]
            fin[i] = st + nodes[i]["lat"]
            order.append(i)
            done.add(i)
            remaining -= 1
            for s in succ[i]:
                ndep[s] -= 1
                t = fin[i] + (SYNC if nodes[s]["eng"] != e else 40.0)
                if t > rdy_t[s]:
                    rdy_t[s] = t
                if ndep[s] == 0:
                    ready[nodes[s]["eng"]].append(s)
        self.sim_time = getattr(self, "sim_time", 0.0) + max(fin.values())
        return order

    def emit(self, stack):
        nc = self.nc
        nodes = self.nodes
        nseg = self.seg + 1
        segs = [[] for _ in range(nseg)]
        for i, n in enumerate(nodes):
            segs[n["seg"]].append(i)
        streams = {e: [] for e in ENGS}
        tok = {}
        dma_cnt = [0] * NDMASEM
        dma_rr = 0
        seen = {e: {} for e in ENGS}
        ckeys = set()
        cckeys = []
        pending = {}
        for si, ids in enumerate(segs):
            if not ids:
                continue
            order = self._sched(ids)
            count = {e: 0 for e in ENGS}
            first = {e: True for e in ENGS}
            for i in order:
                n = nodes[i]
                e = n["eng"]
                deps = set(tok[d] for d in n["deps"])
                if first[e] and pending:
                    deps.update(pending.items())
                first[e] = False
                if n["cc"]:
                    tok[i] = (("cc", len(cckeys)), 1)
                    cckeys.append(tok[i][0])
                    inc = 1
                elif n["dma"]:
                    k = dma_rr
                    dma_rr = (dma_rr + 1) % NDMASEM
                    if dma_cnt[k] > 0:
                        deps.add((("dma", k), dma_cnt[k]))
                    dma_cnt[k] += 16
                    tok[i] = (("dma", k), dma_cnt[k])
                    inc = 16
                else:
                    count[e] += 1
                    tok[i] = ((e, si), count[e])
                    ckeys.add((e, si))
                    inc = 1
                waits = []
                sn = seen[e]
                for key, val in sorted(deps, key=lambda kv: (str(kv[0]), kv[1])):
                    if e == "pe" and key[0] == "pe":
                        continue
                    if sn.get(key, 0) >= val:
                        continue
                    sn[key] = val
                    waits.append((key, val))
                streams[e].append((n["call"], waits, tok[i][0], inc))
            pending = {(e, si): count[e] for e in ENGS if count[e] > 0}
            for k in range(NDMASEM):
                if dma_cnt[k] > 0:
                    pending[("dma", k)] = dma_cnt[k]
            for ck in cckeys:
                pending[ck] = 1
        final = sorted(pending.items(), key=lambda kv: str(kv[0]))
        sems = {}
        for ck in sorted(ckeys):
            sems[ck] = stack.enter_context(nc.semaphore("c_%s_%d" % ck))
        for k in range(NDMASEM):
            sems[("dma", k)] = stack.enter_context(nc.semaphore("d_%d" % k))
        for ck in cckeys:
            sems[ck] = stack.enter_context(nc.semaphore("cc_%d" % ck[1]))
        block = stack.enter_context(nc.Block())
        reg = {"pe": block.tensor, "act": block.scalar, "dve": block.vector,
               "pool": block.gpsimd, "sp": block.sync}
        for e in ENGS:
            stream = streams[e]
            fin_ = final if e == self.final_eng else []

            def body(engine, stream=stream, fin_=fin_):
                for fn, waits, skey, inc in stream:
                    for key, val in waits:
                        engine.wait_ge(sems[key], val)
                    ins = getattr(engine, fn[0])(*fn[1], **fn[2])
                    ins.then_inc(sems[skey], inc)
                for key, val in fin_:
                    engine.wait_ge(sems[key], val)

            reg[e](body)


D = 1024
KC = 8
EPS = 1e-6
DIN = 2568
TM = 256
NRING = 18
NPRE_TILES = 16


def host_consts(ntok, pos0=0):
    import ml_dtypes
    c = {}
    i = np.arange(128)
    c["tri"] = (i[:, None] <= i[None, :]).astype(np.float32)
    c["ustr"] = (i[None, :] < i[:, None]).astype(np.float32)
    c["ones"] = np.ones((128, 128), np.float32)
    c["identf"] = np.eye(128, dtype=np.float32)
    m = np.zeros((128, NRING, TM), np.float32)
    for r in range(NRING):
        for jb in range(2):
            b = 16 + jb - r
            if b < 0 or b > 16:
                continue
            d = 128 * b + i[None, :] - i[:, None]
            w = ((d >= 0) & (d <= 128)).astype(np.float32)
            w += ((d >= 0) & (d <= 512) & (d % 4 == 0))
            w += ((d >= 0) & (d <= 2048) & (d % 16 == 0))
            m[:, r, jb * 128:(jb + 1) * 128] = w
    c["masks"] = m.astype(ml_dtypes.bfloat16)
    half = 32
    inv = (10000.0 ** (-np.arange(half, dtype=np.float32) / half)).astype(np.float32)
    pos = (pos0 + np.arange(ntok)).astype(np.float32)
    ang = pos[:, None] * inv[None]
    c["cos"] = np.cos(ang).astype(np.float32)
    c["sin"] = np.sin(ang).astype(np.float32)
    ic = np.zeros((2, 128, 2, TM), np.float32)
    for which in range(2):
        for ch in range(2):
            for hfp in range(2):
                w = (2, 4, 8, 16)[ch * 2 + hfp]
                p = np.arange(TM) + (0 if which == 0 else 100000)
                ic[which, hfp * 64:(hfp + 1) * 64, ch, :] = 1.0 / np.minimum(w, p + 1)
    c["invcnt"] = ic
    return c


def mixer_phase(nc, p, st, tag, ntok, X_in, X_out, W, C, outs, last_tok_base, X_prev=None, npre=0, wst=None):
    sb = lambda name, shape, dt: st.enter_context(nc.sbuf_tensor(tag + name, shape, dt))
    ps = lambda name, shape, dt: st.enter_context(nc.psum_tensor(tag + name, shape, dt))
    K = lambda *a: (tag,) + a

    sbw = (lambda name, shape, dt: wst.enter_context(nc.sbuf_tensor(tag + name, shape, dt))) if wst is not None else sb
    win = sbw("win", [128, KC, DIN], BF16)
    wout = sbw("wout", [128, KC, D], BF16)
    poolw = sbw("poolw", [128, 2, 128], BF16)
    identf = sbw("identf", [128, 128], F32)
    ident = sbw("ident", [128, 128], BF16)
    gain = sb("gain", [128, D], F32)
    pscale = sb("pscale", [128, 2], F32)
    convw = sb("convw", [128, 8, 4], F32)
    convb = sb("convb", [128, 8], F32)
    dtb = sb("dtb", [128, 8], F32)
    aneg = sb("aneg", [128, 8], F32)
    dskip = sb("dskip", [128, 8], F32)
    ssmn = sb("ssmn", [128, 512], F32)
    tri = sb("tri", [128, 128], F32)
    ustr = sb("ustr", [128, 128], F32)
    ones = sb("ones", [128, 128], F32)
    masks = sb("masks", [128, NRING, TM], BF16)
    invcnt = sb("invcnt", [128, 2, 2, TM], F32)
    epsb = sb("epsb", [128, 1], F32)
    kring = sb("kring", [128, 2, NRING * 128], BF16)
    vring = sb("vring", [128, NRING, 4, 65], BF16)
    hT = sb("hT", [128, 8, 64], F32)
    hTb = sb("hTb", [128, 8, 64], BF16)

    xt = [sb("x%d" % i, [128, 2, D], F32) for i in range(2)]
    xn = [sb("xn%d" % i, [128, D], BF16) for i in range(2)]
    ss = sb("ss", [128, 4], F32)
    rstd = sb("rstd", [128, 4], F32)
    xnT = sb("xnT", [128, KC, TM], BF16)
    xa = sb("xa", [128, 2, 15 + TM], F32)
    s2 = sb("s2", [128, 2, 15 + TM], F32)
    s4 = sb("s4", [128, 2, 15 + TM], F32)
    pd_ = sb("pd", [128, 2, TM], BF16)
    xbc = sb("xbc", [128, 8, 3 + TM], BF16)
    dconv = sb("dconv", [128, 8, 4, 128], BF16)
    uT = sb("uT", [128, 8, TM], BF16)
    cosb = sb("cosb", [128, 2, 32], F32)
    sinb = sb("sinb", [128, 2, 32], F32)
    qk = [sb("qk%d" % i, [128, 512], F32) for i in range(2)]
    qkr = [sb("qkr%d" % i, [128, 512], F32) for i in range(2)]
    rt = [sb("rt%d" % i, [128, 256], F32) for i in range(4)]
    qkb = sb("qkb", [128, 512], BF16)
    qTz = sb("qTz", [128, 2, 2, TM], BF16)
    vf = [sb("vf%d" % i, [128, 256], F32) for i in range(2)]
    zs = sb("zs", [128, 2, 512], F32)
    Pt = [sb("P%d" % i, [128, 2, TM], BF16) for i in range(3)]
    rl = sb("rl", [128, 8], F32)
    ybt = [sb("ybt%d" % i, [128, 256], BF16) for i in range(2)]
    catT = sb("catT", [128, KC, TM], BF16)
    xs_tm = sb("xs_tm", [128, 512], BF16)
    B_tm = sb("B_tm", [128, 256], BF16)
    dtr = sb("dtr", [128, 8], F32)
    dtt = sb("dtt", [128, 8], F32)
    dt1 = sb("dt1", [128, 8], F32)
    dta = sb("dta", [128, 8], F32)
    R = sb("R", [128, 8, 128], F32)
    E = sb("E", [128, 8, 128], BF16)
    CBm = sb("CBm", [128, 2, 128], BF16)
    MT = sb("MT", [128, 8, 128], BF16)
    xdt = sb("xdt", [128, 8, 64], BF16)
    xde = sb("xde", [128, 8, 64], BF16)
    eacs = sb("eacs", [128, 8], F32)
    cdb = sb("cdb", [128, 8], F32)
    yt = sb("yt", [128, 512], F32)
    y2 = sb("y2", [128, 512], F32)
    ssq = sb("ssq", [128, 2], F32)
    grs = sb("grs", [128, 2], F32)
    yc = sb("yc", [128, 512], BF16)
    stg = sb("stg", [128, 1280], F32)
    hfin = sb("hfin", [128, 4, 128], F32)

    bank = [ps("b%d" % i, [128, 512], F32) for i in range(8)]
    bk = [K("bank", i) for i in range(8)]
    p.excl = set(getattr(p, "excl", ())) | set(bk)

    def bfv(i):
        return bank[i][:].bitcast(BF16)

    for k in range(KC):
        p.op("pool", lambda e, k=k: e.dma_start(out=win[:, k, :], in_=W["w_in"][k * 128:(k + 1) * 128, :]),
             writes=[K("win", k)], dma=True)
    for k in range(KC):
        p.op("pool", lambda e, k=k: e.dma_start(out=wout[:, k, :], in_=W["w_out"][k * 128:(k + 1) * 128, :]),
             writes=[K("wout", k)], dma=True)
    p.op("pool", lambda e: e.memset(poolw[:], 0.0), writes=[K("poolw")])
    for g in range(4):
        p.op("pool", lambda e, g=g: e.dma_start(
            out=poolw[(g % 2) * 64:(g % 2) * 64 + 64, g // 2, (g % 2) * 64:(g % 2) * 64 + 64], in_=W["pool_w"][g]),
            reads=[K("poolw")], writes=[K("poolw")], dma=True)
    p.op("pool", lambda e: e.memset(epsb[:], EPS), writes=[K("epsb")])
    p.op("pool", lambda e: e.memset(qTz[:], 0.0), writes=[K("qTz")])
    p.op("pool", lambda e: e.memset(hT[:], 0.0), writes=[K("hT")])
    p.op("pool", lambda e: e.memset(hTb[:], 0.0), writes=[K("hTb")])
    p.op("pool", lambda e: e.memset(xa[:, :, 0:15], 0.0), writes=[K("xa")])
    p.op("pool", lambda e: e.memset(xbc[:, :, 0:3], 0.0), writes=[K("xbc", c_) for c_ in range(8)])
    if X_prev is None:
        p.op("pool", lambda e: e.memset(vring[:, :, :, 64:65], 1.0), writes=[K("vring", s_) for s_ in range(NRING)])
    else:
        p.op("pool", lambda e: e.memset(vring[:], 0.0), writes=[K("vring", s_) for s_ in range(NRING)])
        p.op("pool", lambda e: e.memset(kring[:], 0.0), writes=[K("kring", s_) for s_ in range(NRING)])

    def ld(dst, src, key, eng="sp"):
        p.op(eng, lambda e: e.dma_start(out=dst, in_=src), writes=[key], dma=True)

    ld(gain[:], W["mix_norm"].partition_broadcast(128), K("gain"))
    ld(pscale[:], W["pool_scale_l"], K("pscale"))
    ld(convw[:], W["conv_w_l"], K("convw"))
    ld(convb[:], W["conv_b_l"], K("convb"))
    ld(dtb[:], W["dt_bias"].partition_broadcast(128), K("dtb"))
    ld(aneg[:], W["a_log"].partition_broadcast(128), K("aneg"))
    ld(dskip[:], W["d_skip"].partition_broadcast(128), K("dskip"))
    ld(ssmn[:], W["ssm_norm"].partition_broadcast(128), K("ssmn"))
    ld(tri[:], C["tri"], K("tri"))
    ld(ustr[:], C["ustr"], K("ustr"))
    ld(ones[:], C["ones"], K("ones"))
    ld(identf[:], C["identf"], K("identf"))
    ld(masks[:], C["masks"], K("masks"))
    ld(invcnt[:], C["invcnt"].rearrange("w p c t -> p w c t"), K("invcnt"))
    p.op("dve", lambda e: e.tensor_copy(out=ident[:], in_=identf[:]), reads=[K("identf")], writes=[K("ident")])
    for c_ in range(8):
        for k_ in range(4):
            p.op("dve" if (c_ + k_) % 2 else "pool", lambda e, c_=c_, k_=k_: e.tensor_scalar_mul(
                out=dconv[:, c_, k_, :], in0=identf[:], scalar1=convw[:, c_, k_:k_ + 1]) if (c_ + k_) % 2 else
                e.tensor_tensor(out=dconv[:, c_, k_, :], in0=identf[:], in1=convw[:, c_, k_:k_ + 1].to_broadcast([128, 128]),
                                op=ALU.mult),
                reads=[K("identf"), K("convw")], writes=[K("dconv", c_, k_)])
    p.op("act", lambda e: e.activation(out=aneg[:], in_=aneg[:], func=AF.Exp), reads=[K("aneg")], writes=[K("aneg")])
    p.op("dve", lambda e: e.tensor_scalar_mul(out=aneg[:], in0=aneg[:], scalar1=-1.0), reads=[K("aneg")], writes=[K("aneg")])

    npre = 0 if X_prev is None else npre
    if X_prev is not None:
        flag = sb("flag", [128, 1], F32)
        wl = sb("wl", [128, 8], F32)
        ld(flag[:], C["flag"], K("flag"))

    def pre_proj_fm(bi, col0):
        for k in range(KC):
            p.op("pe", lambda e, k=k: e.matmul(bank[bi][:, 0:TM], lhsT=win[:, k, col0:col0 + 128],
                                               rhs=xnT[:, k, :], start=(k == 0), stop=(k == KC - 1)),
                 reads=[K("xnT", 0), K("xnT", 1), K("win", k)], writes=[bk[bi]])

    def pre_proj_tm(bi, b, col0, ncols):
        for k in range(KC):
            p.op("pe", lambda e, k=k: e.matmul(bank[bi][:, 0:ncols], lhsT=xnT[:, k, b * 128:(b + 1) * 128],
                                               rhs=win[:, k, col0:col0 + ncols], start=(k == 0), stop=(k == KC - 1)),
                 reads=[K("xnT", b), K("win", k)], writes=[bk[bi]])

    for pi in range(npre):
        x = xt[pi % 2]
        xk = K("x", pi % 2)
        src = X_prev(pi)
        last_pre = (pi == npre - 1)
        kv_pre = (pi >= npre - 8)
        p.op("sp", lambda e: e.dma_start(out=x[:], in_=src.rearrange("(b p) d -> p b d", p=128)), writes=[xk], dma=True)
        if kv_pre:
            t0p = pi * TM
            p.op("sp", lambda e: e.dma_start(out=cosb[:], in_=C["cos_prev"][t0p:t0p + TM, :].rearrange("(b p) d -> p b d", p=128)),
                 writes=[K("cosb")], dma=True)
            p.op("sp", lambda e: e.dma_start(out=sinb[:], in_=C["sin_prev"][t0p:t0p + TM, :].rearrange("(b p) d -> p b d", p=128)),
                 writes=[K("sinb")], dma=True)
        for b in range(2):
            xnb = xn[b]
            xnk = K("xn", b)
            p.op("act", lambda e: e.activation(out=xnb[:], in_=x[:, b, :], func=AF.Square, accum_out=ss[:, b:b + 1]),
                 reads=[xk], writes=[xnk, K("ss", b)])
            p.op("act", lambda e: e.activation(out=rstd[:, b:b + 1], in_=ss[:, b:b + 1], func=AF.Ln, scale=1.0 / D, bias=epsb[:]),
                 reads=[K("ss", b), K("epsb")], writes=[K("rstd", b)])
            p.op("act", lambda e: e.activation(out=rstd[:, b:b + 1], in_=rstd[:, b:b + 1], func=AF.Exp, scale=-0.5),
                 reads=[K("rstd", b)], writes=[K("rstd", b)])
            p.op("dve", lambda e: e.scalar_tensor_tensor(out=xnb[:], in0=x[:, b, :], scalar=rstd[:, b:b + 1], in1=gain[:],
                                                         op0=ALU.mult, op1=ALU.mult),
                 reads=[xk, K("rstd", b), K("gain")], writes=[xnk])
            for k in range(KC):
                p.op("pe", lambda e, k=k: e.transpose(out=bfv(7)[:, k * 128:(k + 1) * 128], in_=xnb[:, k * 128:(k + 1) * 128],
                                                      identity=ident[:]),
                     reads=[xnk, K("ident")], writes=[bk[7]])
            p.op("act", lambda e: e.copy(out=xnT[:, :, b * 128:(b + 1) * 128], in_=bfv(7).rearrange("p (k t) -> p k t", k=KC)),
                 reads=[bk[7]], writes=[K("xnT", b)])
        if kv_pre:
            for b in range(2):
                ab = 2 * (pi - npre) + b
                slot = ab % NRING
                pre_proj_tm(2, b, 512, 512)
                p.op("act", lambda e: e.copy(out=qk[b][:, 0:256], in_=bank[2][:, 0:256]), reads=[bk[2]], writes=[K("qk", b)])
                p.op("act", lambda e: e.activation(out=vring[:, slot, :, 0:64],
                                                   in_=bank[2][:, 256:512].rearrange("p (h j) -> p h j", h=4),
                                                   func=AF.Copy, scale=flag[:, 0:1]),
                     reads=[bk[2], K("flag")], writes=[K("vring", slot)])
                p.op("dve", lambda e: e.tensor_copy(out=vring[:, slot, :, 64:65],
                                                    in_=flag[:, 0:1].unsqueeze(1).to_broadcast([128, 4, 1])),
                     reads=[K("flag")], writes=[K("vring", slot)])
                v4k = lambda t: t.rearrange("p (h s j) -> p h s j", h=4, s=2)
                r3k = lambda t: t[:, 0:128].rearrange("p (h j) -> p h j", h=4)
                cb_ = cosb[:, b, :].unsqueeze(1).to_broadcast([128, 4, 32])
                sb_ = sinb[:, b, :].unsqueeze(1).to_broadcast([128, 4, 32])
                x1 = v4k(qk[b][:, 0:256])[:, :, 0, :]
                x2 = v4k(qk[b][:, 0:256])[:, :, 1, :]
                o1 = v4k(qkr[b][:, 0:256])[:, :, 0, :]
                o2 = v4k(qkr[b][:, 0:256])[:, :, 1, :]
                p.op("dve", lambda e: e.tensor_tensor(out=r3k(rt[0]), in0=x1, in1=cb_, op=ALU.mult),
                     reads=[K("qk", b), K("cosb")], writes=[K("rt", 0)])
                p.op("dve", lambda e: e.tensor_tensor(out=r3k(rt[1]), in0=x2, in1=sb_, op=ALU.mult),
                     reads=[K("qk", b), K("sinb")], writes=[K("rt", 1)])
                p.op("dve", lambda e: e.tensor_tensor(out=o1, in0=r3k(rt[0]), in1=r3k(rt[1]), op=ALU.subtract),
                     reads=[K("rt", 0), K("rt", 1)], writes=[K("qkr", b)])
                p.op("pool", lambda e: e.tensor_tensor(out=r3k(rt[2]), in0=x2, in1=cb_, op=ALU.mult),
                     reads=[K("qk", b), K("cosb")], writes=[K("rt", 2)])
                p.op("pool", lambda e: e.tensor_tensor(out=r3k(rt[3]), in0=x1, in1=sb_, op=ALU.mult),
                     reads=[K("qk", b), K("sinb")], writes=[K("rt", 3)])
                p.op("pool", lambda e: e.tensor_tensor(out=o2, in0=r3k(rt[2]), in1=r3k(rt[3]), op=ALU.add),
                     reads=[K("rt", 2), K("rt", 3)], writes=[K("qkr", b)])
                p.op("act", lambda e: e.copy(out=qkb[:, 0:256], in_=qkr[b][:, 0:256]), reads=[K("qkr", b)], writes=[K("qkb")])
                for c in range(2):
                    p.op("pe", lambda e, c=c: e.transpose(out=bfv(7)[:, c * 128:(c + 1) * 128], in_=qkb[:, c * 128:(c + 1) * 128],
                                                          identity=ident[:]),
                         reads=[K("qkb"), K("ident")], writes=[bk[7]])
                p.op("act", lambda e: e.copy(out=kring[:, :, slot * 128:(slot + 1) * 128],
                                             in_=bfv(7)[:, 0:256].rearrange("p (c t) -> p c t", c=2)),
                     reads=[bk[7]], writes=[K("kring", slot)])
        if last_pre:
            for c in range(2):
                pre_proj_fm(c, c * 128)
                p.op("act", lambda e, c=c: e.copy(out=xa[:, c, 15:15 + TM], in_=bank[c][:, 0:TM]), reads=[bk[c]], writes=[K("xa")])
            p.op("pool", lambda e: e.tensor_copy(out=xa[:, :, 0:15], in_=xa[:, :, TM:TM + 15]), reads=[K("xa")], writes=[K("xa")])
        nconv = 8 if last_pre else 6
        for c in range(nconv):
            bi = c % 2
            cb_ = 2 + (c % 2)
            pre_proj_fm(bi, 1536 + c * 128)
            p.op("act", lambda e, c=c, bi=bi: e.copy(out=xbc[:, c, 3:3 + TM], in_=bank[bi][:, 0:TM]), reads=[bk[bi]], writes=[K("xbc", c)])
            if c >= 6:
                continue
            for kk in range(4):
                p.op("pe", lambda e, c=c, kk=kk: e.matmul(bank[cb_][:, 0:TM], lhsT=dconv[:, c, kk, :], rhs=xbc[:, c, kk:kk + TM],
                                                          start=(kk == 0), stop=(kk == 3)),
                     reads=[K("xbc", c), K("dconv", c, kk)], writes=[bk[cb_]])
            p.op("act", lambda e, c=c: e.activation(out=uT[:, c, :], in_=bank[cb_][:, 0:TM], func=AF.Silu, bias=convb[:, c:c + 1]),
                 reads=[bk[cb_], K("convb")], writes=[K("uT", c)])
        p.op("pool", lambda e: e.tensor_copy(out=xbc[:, :, 0:3], in_=xbc[:, :, TM:TM + 3]),
             reads=[K("xbc", c_) for c_ in range(8)], writes=[K("xbc", c_) for c_ in range(8)])
        if last_pre:
            p.op("dve", lambda e: e.tensor_scalar_mul(out=xa[:, :, 0:15], in0=xa[:, :, 0:15], scalar1=flag[:, 0:1]),
                 reads=[K("xa"), K("flag")], writes=[K("xa")])
            p.op("dve", lambda e: e.tensor_scalar_mul(out=xbc[:, :, 0:3], in0=xbc[:, :, 0:3], scalar1=flag[:, 0:1]),
                 reads=[K("xbc", c_) for c_ in range(8)] + [K("flag")], writes=[K("xbc", c_) for c_ in range(8)])
        for b in range(2):
            cs = slice(b * 128, (b + 1) * 128)
            pre_proj_tm(0, b, 2560, 8)
            p.op("dve", lambda e: e.tensor_tensor(out=dtr[:], in0=bank[0][:, 0:8], in1=dtb[:], op=ALU.add),
                 reads=[bk[0], K("dtb")], writes=[K("dtr")])
            p.op("dve", lambda e: e.tensor_scalar_mul(out=dtt[:], in0=dtr[:], scalar1=-1.0), reads=[K("dtr")], writes=[K("dtt")])
            p.op("dve", lambda e: e.tensor_tensor(out=dtt[:], in0=dtt[:], in1=dtr[:], op=ALU.max),
                 reads=[K("dtr"), K("dtt")], writes=[K("dtt")])
            p.op("act", lambda e: e.activation(out=dtt[:], in_=dtt[:], func=AF.Exp, scale=-1.0), reads=[K("dtt")], writes=[K("dtt")])
            p.op("act", lambda e: e.activation(out=dtt[:], in_=dtt[:], func=AF.Ln, bias=1.0), reads=[K("dtt")], writes=[K("dtt")])
            p.op("dve", lambda e: e.scalar_tensor_tensor(out=dt1[:], in0=dtr[:], scalar=0.0, in1=dtt[:], op0=ALU.max, op1=ALU.add),
                 reads=[K("dtr"), K("dtt")], writes=[K("dt1")])
            p.op("dve", lambda e: e.tensor_scalar_mul(out=dt1[:], in0=dt1[:], scalar1=flag[:, 0:1]),
                 reads=[K("dt1"), K("flag")], writes=[K("dt1")])
            p.op("dve", lambda e: e.tensor_tensor(out=dta[:], in0=dt1[:], in1=aneg[:], op=ALU.mult),
                 reads=[K("dt1"), K("aneg")], writes=[K("dta")])
            for c in range(6):
                p.op("pe", lambda e, c=c: e.transpose(out=bfv(7)[:, c * 128:(c + 1) * 128], in_=uT[:, c, cs], identity=ident[:]),
                     reads=[K("uT", c), K("ident")], writes=[bk[7]])
            p.op("act", lambda e: e.copy(out=xs_tm[:], in_=bfv(7)[:, 0:512]), reads=[bk[7]], writes=[K("xs_tm")])
            p.op("act", lambda e: e.copy(out=B_tm[:], in_=bfv(7)[:, 512:768]), reads=[bk[7]], writes=[K("B_tm")])
            p.op("pe", lambda e: e.matmul(bank[0][:, 16:24], lhsT=tri[:], rhs=dta[:], start=True, stop=True),
                 reads=[K("tri"), K("dta")], writes=[bk[0]])
            p.op("pe", lambda e: e.matmul(bank[0][:, 32:40], lhsT=ones[:], rhs=dta[:], start=True, stop=True),
                 reads=[K("ones"), K("dta")], writes=[bk[0]])
            p.op("act", lambda e: e.activation(out=cdb[:], in_=bank[0][:, 32:40], func=AF.Exp), reads=[bk[0]], writes=[K("cdb")])
            p.op("act", lambda e: e.copy(out=eacs[:], in_=bank[0][:, 16:24]), reads=[bk[0]], writes=[K("eacs")])
            p.op("dve", lambda e: e.tensor_tensor(out=wl[:], in0=bank[0][:, 32:40], in1=eacs[:], op=ALU.subtract),
                 reads=[bk[0], K("eacs")], writes=[K("wl")])
            p.op("act", lambda e: e.activation(out=wl[:], in_=wl[:], func=AF.Exp), reads=[K("wl")], writes=[K("wl")])
            p.op("dve", lambda e: e.tensor_tensor(out=wl[:], in0=wl[:], in1=dt1[:], op=ALU.mult),
                 reads=[K("wl"), K("dt1")], writes=[K("wl")])
            p.op("dve", lambda e: e.tensor_tensor(out=xde[:], in0=xs_tm[:].rearrange("p (h j) -> p h j", h=8),
                                                  in1=wl[:].unsqueeze(2).to_broadcast([128, 8, 64]), op=ALU.mult),
                 reads=[K("xs_tm"), K("wl")], writes=[K("xde")])
            for g in range(2):
                p.op("pe", lambda e, g=g: e.matmul(bank[2][:, g * 256:(g + 1) * 256], lhsT=B_tm[:, g * 128:(g + 1) * 128],
                                                   rhs=xde[:, g * 4:(g + 1) * 4, :].rearrange("p h j -> p (h j)"),
                                                   start=True, stop=True),
                     reads=[K("B_tm"), K("xde")], writes=[bk[2]])
            p.op("dve", lambda e: e.tensor_tensor(out=hT[:], in0=hT[:], in1=cdb[:].unsqueeze(2).to_broadcast([128, 8, 64]),
                                                  op=ALU.mult),
                 reads=[K("hT"), K("cdb")], writes=[K("hT")])
            p.op("dve", lambda e: e.tensor_tensor(out=hT[:].rearrange("p h j -> p (h j)"), in0=hT[:].rearrange("p h j -> p (h j)"),
                                                  in1=bank[2][:], op=ALU.add),
                 reads=[K("hT"), bk[2]], writes=[K("hT")])
        if last_pre:
            p.op("act", lambda e: e.copy(out=hTb[:], in_=hT[:]), reads=[K("hT")], writes=[K("hTb")])

    winkeys = [K("win", k) for k in range(KC)]
    ntiles = ntok // TM
    for ti in range(ntiles):
        t0 = ti * TM
        x = xt[ti % 2]
        xk = K("x", ti % 2)
        p.op("sp", lambda e, x=x, t0=t0: e.dma_start(
            out=x[:], in_=X_in[t0:t0 + TM, :].rearrange("(b p) d -> p b d", p=128)), writes=[xk], dma=True)
        p.op("sp", lambda e, t0=t0: e.dma_start(
            out=cosb[:], in_=C["cos"][t0:t0 + TM, :].rearrange("(b p) d -> p b d", p=128)), writes=[K("cosb")], dma=True)
        p.op("sp", lambda e, t0=t0: e.dma_start(
            out=sinb[:], in_=C["sin"][t0:t0 + TM, :].rearrange("(b p) d -> p b d", p=128)), writes=[K("sinb")], dma=True)
        for b in range(2):
            xnb = xn[b]
            xnk = K("xn", b)
            p.op("act", lambda e, b=b, xnb=xnb, x=x: e.activation(out=xnb[:], in_=x[:, b, :], func=AF.Square,
                                                              accum_out=ss[:, b:b + 1]),
                 reads=[xk], writes=[xnk, K("ss", b)])
            p.op("act", lambda e, b=b: e.activation(out=rstd[:, b:b + 1], in_=ss[:, b:b + 1], func=AF.Ln,
                                                   scale=1.0 / D, bias=epsb[:]),
                 reads=[K("ss", b), K("epsb")], writes=[K("rstd", b)])
            p.op("act", lambda e, b=b: e.activation(out=rstd[:, b:b + 1], in_=rstd[:, b:b + 1], func=AF.Exp, scale=-0.5),
                 reads=[K("rstd", b)], writes=[K("rstd", b)])
            p.op("dve", lambda e, b=b, xnb=xnb, x=x: e.scalar_tensor_tensor(
                out=xnb[:], in0=x[:, b, :], scalar=rstd[:, b:b + 1], in1=gain[:], op0=ALU.mult, op1=ALU.mult),
                reads=[xk, K("rstd", b), K("gain")], writes=[xnk])
            for k in range(KC):
                p.op("pe", lambda e, xnb=xnb, k=k: e.transpose(
                    out=bfv(7)[:, k * 128:(k + 1) * 128], in_=xnb[:, k * 128:(k + 1) * 128], identity=ident[:]),
                    reads=[xnk, K("ident")], writes=[bk[7]])
            p.op("act", lambda e, b=b: e.copy(
                out=xnT[:, :, b * 128:(b + 1) * 128], in_=bfv(7).rearrange("p (k t) -> p k t", k=KC)),
                reads=[bk[7]], writes=[K("xnT", b)])
        xk2 = [K("xnT", 0), K("xnT", 1)]

        def proj_fm(bi, col0, ncols=128):
            for k in range(KC):
                p.op("pe", lambda e, k=k: e.matmul(bank[bi][0:ncols, 0:TM], lhsT=win[:, k, col0:col0 + ncols],
                                                   rhs=xnT[:, k, :], start=(k == 0), stop=(k == KC - 1)),
                     reads=xk2 + [K("win", k)], writes=[bk[bi]])

        def proj_tm(bi, b, col0, ncols, ocol=0):
            for k in range(KC):
                p.op("pe", lambda e, k=k: e.matmul(bank[bi][:, ocol:ocol + ncols], lhsT=xnT[:, k, b * 128:(b + 1) * 128],
                                                   rhs=win[:, k, col0:col0 + ncols], start=(k == 0), stop=(k == KC - 1)),
                     reads=[K("xnT", b), K("win", k)], writes=[bk[bi]])

        for c in range(2):
            proj_fm(c, c * 128)
            p.op("act", lambda e, c=c: e.copy(out=xa[:, c, 15:15 + TM], in_=bank[c][:, 0:TM]),
                 reads=[bk[c]], writes=[K("xa")])
        L = 15 + TM
        p.op("dve", lambda e: e.tensor_tensor(out=s2[:, :, 1:L], in0=xa[:, :, 1:L], in1=xa[:, :, 0:L - 1], op=ALU.add),
             reads=[K("xa")], writes=[K("s2")])
        p.op("dve", lambda e: e.tensor_tensor(out=s4[:, :, 3:L], in0=s2[:, :, 3:L], in1=s2[:, :, 1:L - 2], op=ALU.add),
             reads=[K("s2")], writes=[K("s4")])
        p.op("dve", lambda e: e.tensor_tensor(out=s2[:, 1, 7:L], in0=s4[:, 1, 7:L], in1=s4[:, 1, 3:L - 4], op=ALU.add),
             reads=[K("s4"), K("s2")], writes=[K("s2")])
        p.op("dve", lambda e: e.tensor_tensor(out=s4[64:128, 1, 15:L], in0=s2[64:128, 1, 15:L], in1=s2[64:128, 1, 7:L - 8],
                                              op=ALU.add),
             reads=[K("s2"), K("s4")], writes=[K("s4")])
        wh = 0 if ti == 0 else 1
        srcs = [(s2, 0, 0), (s4, 64, 0), (s2, 0, 1), (s4, 64, 1)]
        for g, (sbuf_, r0, c) in enumerate(srcs):
            p.op("dve", lambda e, sbuf_=sbuf_, r0=r0, c=c: e.tensor_tensor(
                out=sbuf_[r0:r0 + 64, c, 15:L], in0=sbuf_[r0:r0 + 64, c, 15:L], in1=invcnt[r0:r0 + 64, wh, c, :], op=ALU.mult),
                reads=[K("s2"), K("s4"), K("invcnt")], writes=[K("s2"), K("s4")])
            p.op("dve", lambda e, sbuf_=sbuf_, r0=r0, c=c: e.tensor_tensor(
                out=pd_[r0:r0 + 64, c, :], in0=sbuf_[r0:r0 + 64, c, 15:L], in1=xa[r0:r0 + 64, c, 15:L], op=ALU.subtract),
                reads=[K("s2"), K("s4"), K("xa")], writes=[K("pd")])
        for c in range(2):
            p.op("pe", lambda e, c=c: e.matmul(bank[c][:, 0:TM], lhsT=poolw[:, c, :], rhs=pd_[:, c, :], start=True, stop=True),
                 reads=[K("pd"), K("poolw")], writes=[bk[c]])
            p.op("act", lambda e, c=c: e.activation(out=catT[:, c, :], in_=bank[c][:, 0:TM], func=AF.Copy,
                                                   scale=pscale[:, c:c + 1]),
                 reads=[bk[c], K("pscale")], writes=[K("catT", c)])
        p.op("pool", lambda e: e.tensor_copy(out=xa[:, :, 0:15], in_=xa[:, :, TM:TM + 15]), reads=[K("xa")], writes=[K("xa")])

        for b in range(2):
            ab = 2 * ti + b
            slot = ab % NRING
            proj_tm(2, b, 256, 512)
            p.op("act", lambda e, b=b: e.copy(out=qk[b][:], in_=bank[2][:]), reads=[bk[2]], writes=[K("qk", b)])
            v4 = lambda t: t[:].rearrange("p (h s j) -> p h s j", h=8, s=2)
            cb_ = cosb[:, b, :].unsqueeze(1).to_broadcast([128, 8, 32])
            sb_ = sinb[:, b, :].unsqueeze(1).to_broadcast([128, 8, 32])
            r3 = lambda t: t[:].rearrange("p (h j) -> p h j", h=8)
            x1 = v4(qk[b])[:, :, 0, :]
            x2 = v4(qk[b])[:, :, 1, :]
            o1 = v4(qkr[b])[:, :, 0, :]
            o2 = v4(qkr[b])[:, :, 1, :]
            p.op("dve", lambda e, x1=x1, cb_=cb_: e.tensor_tensor(out=r3(rt[0]), in0=x1, in1=cb_, op=ALU.mult),
                 reads=[K("qk", b), K("cosb")], writes=[K("rt", 0)])
            p.op("dve", lambda e, x2=x2, sb_=sb_: e.tensor_tensor(out=r3(rt[1]), in0=x2, in1=sb_, op=ALU.mult),
                 reads=[K("qk", b), K("sinb")], writes=[K("rt", 1)])
            p.op("dve", lambda e, o1=o1: e.tensor_tensor(out=o1, in0=r3(rt[0]), in1=r3(rt[1]), op=ALU.subtract),
                 reads=[K("rt", 0), K("rt", 1)], writes=[K("qkr", b)])
            p.op("pool", lambda e, x2=x2, cb_=cb_: e.tensor_tensor(out=r3(rt[2]), in0=x2, in1=cb_, op=ALU.mult),
                 reads=[K("qk", b), K("cosb")], writes=[K("rt", 2)])
            p.op("pool", lambda e, x1=x1, sb_=sb_: e.tensor_tensor(out=r3(rt[3]), in0=x1, in1=sb_, op=ALU.mult),
                 reads=[K("qk", b), K("sinb")], writes=[K("rt", 3)])
            p.op("pool", lambda e, o2=o2: e.tensor_tensor(out=o2, in0=r3(rt[2]), in1=r3(rt[3]), op=ALU.add),
                 reads=[K("rt", 2), K("rt", 3)], writes=[K("qkr", b)])
            tok0 = t0 + b * 128
            if tok0 >= last_tok_base:
                o0 = tok0 - last_tok_base
                p.op("sp", lambda e, b=b, o0=o0: e.dma_start(out=outs["k"][o0:o0 + 128, :], in_=qkr[b][:, 256:512]),
                     reads=[K("qkr", b)], writes=[K("kout", tok0)], dma=True)
            p.op("act", lambda e, b=b: e.copy(out=qkb[:], in_=qkr[b][:]), reads=[K("qkr", b)], writes=[K("qkb")])
            for c in range(4):
                p.op("pe", lambda e, c=c: e.transpose(out=bfv(7)[:, c * 128:(c + 1) * 128], in_=qkb[:, c * 128:(c + 1) * 128],
                                                      identity=ident[:]),
                     reads=[K("qkb"), K("ident")], writes=[bk[7]])
            p.op("dve", lambda e, b=b: e.tensor_copy(out=qTz[0:64, :, 0, b * 128:(b + 1) * 128],
                                                     in_=bfv(7)[0:64, 0:256].rearrange("p (c t) -> p c t", c=2)),
                 reads=[bk[7], K("qTz")], writes=[K("qTz")])
            p.op("act", lambda e, b=b: e.copy(out=qTz[64:128, :, 1, b * 128:(b + 1) * 128],
                                             in_=bfv(7)[64:128, 0:256].rearrange("p (c t) -> p c t", c=2)),
                 reads=[bk[7], K("qTz")], writes=[K("qTz")])
            p.op("act", lambda e, slot=slot: e.copy(out=kring[:, :, slot * 128:(slot + 1) * 128],
                                                   in_=bfv(7)[:, 256:512].rearrange("p (c t) -> p c t", c=2)),
                 reads=[bk[7]], writes=[K("kring", slot)])
            proj_tm(3, b, 768, 256)
            p.op("act", lambda e, slot=slot: e.copy(out=vring[:, slot, :, 0:64],
                                                   in_=bank[3][:, 0:256].rearrange("p (h j) -> p h j", h=4)),
                 reads=[bk[3]], writes=[K("vring", slot)])
            if X_prev is not None:
                p.op("pool", lambda e, slot=slot: e.memset(vring[:, slot, :, 64:65], 1.0),
                     writes=[K("vring", slot)])
            if tok0 >= last_tok_base:
                o0 = tok0 - last_tok_base
                p.op("dve", lambda e, b=b: e.tensor_copy(out=vf[b][:], in_=bank[3][:, 0:256]), reads=[bk[3]], writes=[K("vf", b)])
                p.op("sp", lambda e, b=b, o0=o0: e.dma_start(out=outs["v"][o0:o0 + 128, :], in_=vf[b][:]),
                     reads=[K("vf", b)], writes=[K("vout", tok0)], dma=True)
            proj_tm(4, b, 1024, 512)
            p.op("act", lambda e, b=b: e.activation(out=zs[:, b, :], in_=bank[4][:], func=AF.Silu),
                 reads=[bk[4]], writes=[K("zs", b)])

        rs = [r for r in range(NRING) if (X_prev is not None) or 2 * ti - 16 + r >= 0]
        lastr = {0: max(r for r in rs if 0 <= 16 - r <= 16), 1: max(r for r in rs if 0 <= 17 - r <= 16)}
        for hp in range(2):
            first = {0: True, 1: True}
            for n_, r in enumerate(rs):
                ab = 2 * ti - 16 + r
                slot = ab % NRING
                js = [j for j in (0, 1) if 0 <= 16 + j - r <= 16]
                si = 5 + (n_ % 2)
                Pb = Pt[n_ % 3]
                pk = K("P", n_ % 3)
                for hh in range(2):
                    p.op("pe", lambda e: e.matmul(
                        bank[si][:, hh * TM:(hh + 1) * TM], lhsT=kring[:, hp, slot * 128:(slot + 1) * 128],
                        rhs=qTz[:, hp, hh, :], start=True, stop=True, skip_group_check=True),
                        reads=[K("kring", slot), K("qTz")], writes=[bk[si]])
                p.op("act", lambda e: e.activation(out=Pb[:].rearrange("p h t -> p (h t)"), in_=bank[si][:],
                                                   func=AF.Exp, scale=0.125),
                     reads=[bk[si]], writes=[pk])
                p.op("dve", lambda e: e.tensor_tensor(out=Pb[:], in0=Pb[:],
                                                      in1=masks[:, r, :].unsqueeze(1).to_broadcast([128, 2, TM]), op=ALU.mult),
                     reads=[pk, K("masks")], writes=[pk])
                for j in js:
                    for hh in range(2):
                        h = 2 * hp + hh
                        p.op("pe", lambda e, st_=(first[j] and hh == 0), sp_=(r == lastr[j]): e.matmul(
                            bank[3 + j][:, h * 65:(h + 1) * 65], lhsT=Pb[:, hh, j * 128:(j + 1) * 128], rhs=vring[:, slot, h, :],
                            start=st_, stop=sp_, skip_group_check=True),
                            reads=[pk, K("vring", slot)], writes=[bk[3 + j]])
                    first[j] = False
            for hh in range(2):
                h = 2 * hp + hh
                for j in range(2):
                    p.op("dve", lambda e: e.reciprocal(out=rl[:, 2 * h + j:2 * h + j + 1],
                                                       in_=bank[3 + j][:, h * 65 + 64:h * 65 + 65]),
                         reads=[bk[3 + j]], writes=[K("rl", h, j)])
                    p.op("dve", lambda e: e.tensor_scalar_mul(out=ybt[j][:, h * 64:(h + 1) * 64],
                                                              in0=bank[3 + j][:, h * 65:h * 65 + 64],
                                                              scalar1=rl[:, 2 * h + j:2 * h + j + 1]),
                         reads=[bk[3 + j], K("rl", h, j)], writes=[K("ybt", j)])
        for j in range(2):
            for c in range(2):
                p.op("pe", lambda e, j=j, c=c: e.transpose(out=bfv(7)[:, c * 128:(c + 1) * 128],
                                                          in_=ybt[j][:, c * 128:(c + 1) * 128], identity=ident[:]),
                     reads=[K("ybt", j), K("ident")], writes=[bk[7]])
            p.op("act", lambda e, j=j: e.copy(out=catT[:, 2:4, j * 128:(j + 1) * 128],
                                             in_=bfv(7)[:, 0:256].rearrange("p (c t) -> p c t", c=2)),
                 reads=[bk[7]], writes=[K("catT", 2, j)])

        for c in range(8):
            bi = c % 2
            cb_ = 2 + (c % 2)
            proj_fm(bi, 1536 + c * 128)
            p.op("act", lambda e, c=c, bi=bi: e.copy(out=xbc[:, c, 3:3 + TM], in_=bank[bi][:, 0:TM]),
                 reads=[bk[bi]], writes=[K("xbc", c)])
            for kk in range(4):
                p.op("pe", lambda e, c=c, kk=kk: e.matmul(bank[cb_][:, 0:TM], lhsT=dconv[:, c, kk, :], rhs=xbc[:, c, kk:kk + TM],
                                                          start=(kk == 0), stop=(kk == 3)),
                     reads=[K("xbc", c), K("dconv", c, kk)], writes=[bk[cb_]])
            p.op("act", lambda e, c=c: e.activation(out=uT[:, c, :], in_=bank[cb_][:, 0:TM], func=AF.Silu, bias=convb[:, c:c + 1]),
                 reads=[bk[cb_], K("convb")], writes=[K("uT", c)])
        p.op("pool", lambda e: e.tensor_copy(out=xbc[:, :, 0:3], in_=xbc[:, :, TM:TM + 3]),
             reads=[K("xbc", c_) for c_ in range(8)], writes=[K("xbc", c_) for c_ in range(8)])

        for b in range(2):
            cs = slice(b * 128, (b + 1) * 128)
            proj_tm(0, b, 2560, 8)
            p.op("dve", lambda e: e.tensor_tensor(out=dtr[:], in0=bank[0][:, 0:8], in1=dtb[:], op=ALU.add),
                 reads=[bk[0], K("dtb")], writes=[K("dtr")])
            p.op("dve", lambda e: e.tensor_scalar_mul(out=dtt[:], in0=dtr[:], scalar1=-1.0),
                 reads=[K("dtr")], writes=[K("dtt")])
            p.op("dve", lambda e: e.tensor_tensor(out=dtt[:], in0=dtt[:], in1=dtr[:], op=ALU.max),
                 reads=[K("dtr"), K("dtt")], writes=[K("dtt")])
            p.op("act", lambda e: e.activation(out=dtt[:], in_=dtt[:], func=AF.Exp, scale=-1.0), reads=[K("dtt")], writes=[K("dtt")])
            p.op("act", lambda e: e.activation(out=dtt[:], in_=dtt[:], func=AF.Ln, bias=1.0), reads=[K("dtt")], writes=[K("dtt")])
            p.op("dve", lambda e: e.scalar_tensor_tensor(out=dt1[:], in0=dtr[:], scalar=0.0, in1=dtt[:], op0=ALU.max, op1=ALU.add),
                 reads=[K("dtr"), K("dtt")], writes=[K("dt1")])
            p.op("dve", lambda e: e.tensor_tensor(out=dta[:], in0=dt1[:], in1=aneg[:], op=ALU.mult),
                 reads=[K("dt1"), K("aneg")], writes=[K("dta")])
            for c in range(6):
                p.op("pe", lambda e, c=c: e.transpose(out=bfv(7)[:, c * 128:(c + 1) * 128], in_=uT[:, c, cs], identity=ident[:]),
                     reads=[K("uT", c), K("ident")], writes=[bk[7]])
            p.op("act", lambda e: e.copy(out=xs_tm[:], in_=bfv(7)[:, 0:512]), reads=[bk[7]], writes=[K("xs_tm")])
            p.op("act", lambda e: e.copy(out=B_tm[:], in_=bfv(7)[:, 512:768]), reads=[bk[7]], writes=[K("B_tm")])
            p.op("pool", lambda e: e.tensor_tensor(out=R[:], in0=tri[:].unsqueeze(1).to_broadcast([128, 8, 128]),
                                                   in1=dta[:].unsqueeze(2).to_broadcast([128, 8, 128]), op=ALU.mult),
                 reads=[K("tri"), K("dta")], writes=[K("R")])
            for hf in range(2):
                p.op("pe", lambda e, hf=hf: e.matmul(bank[1 + hf][:], lhsT=ustr[:],
                                                     rhs=R[:, hf * 4:(hf + 1) * 4, :].rearrange("p h l -> p (h l)"),
                                                     start=True, stop=True),
                     reads=[K("ustr"), K("R")], writes=[bk[1 + hf]])
                p.op("act", lambda e, hf=hf: e.activation(out=E[:, hf * 4:(hf + 1) * 4, :].rearrange("p h l -> p (h l)"),
                                                         in_=bank[1 + hf][:], func=AF.Exp),
                     reads=[bk[1 + hf]], writes=[K("E", hf)])
            p.op("pe", lambda e: e.matmul(bank[0][:, 16:24], lhsT=tri[:], rhs=dta[:], start=True, stop=True),
                 reads=[K("tri"), K("dta")], writes=[bk[0]])
            p.op("pe", lambda e: e.matmul(bank[0][:, 32:40], lhsT=ones[:], rhs=dta[:], start=True, stop=True),
                 reads=[K("ones"), K("dta")], writes=[bk[0]])
            p.op("act", lambda e: e.activation(out=eacs[:], in_=bank[0][:, 16:24], func=AF.Exp), reads=[bk[0]], writes=[K("eacs")])
            p.op("act", lambda e: e.activation(out=cdb[:], in_=bank[0][:, 32:40], func=AF.Exp), reads=[bk[0]], writes=[K("cdb")])
            for g in range(2):
                p.op("pe", lambda e, g=g: e.matmul(bank[0][:, 128 + g * 128:256 + g * 128], lhsT=uT[:, 4 + g, cs], rhs=uT[:, 6 + g, cs],
                                                   start=True, stop=True),
                     reads=[K("uT", 4 + g), K("uT", 6 + g)], writes=[bk[0]])
            p.op("dve", lambda e: e.tensor_tensor(out=CBm[:], in0=bank[0][:, 128:384].rearrange("p (g l) -> p g l", g=2),
                                                  in1=tri[:].unsqueeze(1).to_broadcast([128, 2, 128]), op=ALU.mult),
                 reads=[bk[0], K("tri")], writes=[K("CBm")])
            p.op("dve", lambda e: e.tensor_tensor(out=MT[:].rearrange("p (g j) l -> p g j l", g=2),
                                                  in0=E[:].rearrange("p (g j) l -> p g j l", g=2),
                                                  in1=CBm[:].unsqueeze(2).to_broadcast([128, 2, 4, 128]), op=ALU.mult),
                 reads=[K("E", 0), K("E", 1), K("CBm")], writes=[K("MT")])
            p.op("pool", lambda e: e.tensor_tensor(out=xdt[:], in0=xs_tm[:].rearrange("p (h j) -> p h j", h=8),
                                                   in1=dt1[:].unsqueeze(2).to_broadcast([128, 8, 64]), op=ALU.mult),
                 reads=[K("xs_tm"), K("dt1")], writes=[K("xdt")])
            for h in range(8):
                p.op("pe", lambda e, h=h: e.matmul(bank[3][:, h * 64:(h + 1) * 64], lhsT=MT[:, h, :], rhs=xdt[:, h, :],
                                                   start=True, stop=True),
                     reads=[K("MT"), K("xdt")], writes=[bk[3]])
            for h in range(8):
                p.op("pe", lambda e, h=h: e.matmul(bank[4][:, h * 64:(h + 1) * 64], lhsT=uT[:, 6 + h // 4, cs], rhs=hTb[:, h, :],
                                                   start=True, stop=True),
                     reads=[K("uT", 6 + h // 4), K("hTb")], writes=[bk[4]])
            v8 = lambda t: t.rearrange("p (h j) -> p h j", h=8)
            p.op("dve", lambda e: e.tensor_tensor(out=v8(yt[:]), in0=v8(bank[4][:]),
                                                  in1=eacs[:].unsqueeze(2).to_broadcast([128, 8, 64]), op=ALU.mult),
                 reads=[bk[4], K("eacs")], writes=[K("yt")])
            p.op("dve", lambda e: e.tensor_tensor(out=yt[:], in0=yt[:], in1=bank[3][:], op=ALU.add),
                 reads=[bk[3], K("yt")], writes=[K("yt")])
            p.op("pool", lambda e: e.tensor_tensor(out=v8(y2[:]), in0=v8(xs_tm[:]),
                                                   in1=dskip[:].unsqueeze(2).to_broadcast([128, 8, 64]), op=ALU.mult),
                 reads=[K("xs_tm"), K("dskip")], writes=[K("y2")])
            p.op("pool", lambda e: e.tensor_tensor(out=yt[:], in0=yt[:], in1=y2[:], op=ALU.add),
                 reads=[K("yt"), K("y2")], writes=[K("yt")])
            p.op("pool", lambda e, b=b: e.tensor_tensor(out=yt[:], in0=yt[:], in1=zs[:, b, :], op=ALU.mult),
                 reads=[K("yt"), K("zs", b)], writes=[K("yt")])
            for g in range(2):
                p.op("act", lambda e, g=g: e.activation(out=y2[:, g * 256:(g + 1) * 256], in_=yt[:, g * 256:(g + 1) * 256],
                                                       func=AF.Square, accum_out=ssq[:, g:g + 1]),
                     reads=[K("yt")], writes=[K("y2"), K("ssq", g)])
                p.op("act", lambda e, g=g: e.activation(out=grs[:, g:g + 1], in_=ssq[:, g:g + 1], func=AF.Ln,
                                                       scale=1.0 / 256, bias=epsb[:]),
                     reads=[K("ssq", g), K("epsb")], writes=[K("grs", g)])
                p.op("act", lambda e, g=g: e.activation(out=grs[:, g:g + 1], in_=grs[:, g:g + 1], func=AF.Exp, scale=-0.5),
                     reads=[K("grs", g)], writes=[K("grs", g)])
                p.op("dve", lambda e, g=g: e.scalar_tensor_tensor(
                    out=yc[:, g * 256:(g + 1) * 256], in0=yt[:, g * 256:(g + 1) * 256], scalar=grs[:, g:g + 1],
                    in1=ssmn[:, g * 256:(g + 1) * 256], op0=ALU.mult, op1=ALU.mult),
                    reads=[K("yt"), K("grs", g), K("ssmn")], writes=[K("yc")])
            for c in range(4):
                p.op("pe", lambda e, c=c: e.transpose(out=bfv(7)[:, c * 128:(c + 1) * 128], in_=yc[:, c * 128:(c + 1) * 128],
                                                      identity=ident[:]),
                     reads=[K("yc"), K("ident")], writes=[bk[7]])
            p.op("act", lambda e, b=b: e.copy(out=catT[:, 4:8, b * 128:(b + 1) * 128],
                                             in_=bfv(7)[:, 0:512].rearrange("p (c t) -> p c t", c=4)),
                 reads=[bk[7]], writes=[K("catT", 4, b)])
            p.op("pool", lambda e: e.tensor_tensor(out=xde[:], in0=xdt[:], in1=E[:, :, 127:128].to_broadcast([128, 8, 64]),
                                                   op=ALU.mult),
                 reads=[K("xdt"), K("E", 0), K("E", 1)], writes=[K("xde")])
            for g in range(2):
                p.op("pe", lambda e, g=g: e.matmul(bank[2][:, g * 256:(g + 1) * 256], lhsT=B_tm[:, g * 128:(g + 1) * 128],
                                                   rhs=xde[:, g * 4:(g + 1) * 4, :].rearrange("p h j -> p (h j)"),
                                                   start=True, stop=True),
                     reads=[K("B_tm"), K("xde")], writes=[bk[2]])
            p.op("dve", lambda e: e.tensor_tensor(out=hT[:], in0=hT[:], in1=cdb[:].unsqueeze(2).to_broadcast([128, 8, 64]),
                                                  op=ALU.mult),
                 reads=[K("hT"), K("cdb")], writes=[K("hT")])
            p.op("dve", lambda e: e.tensor_tensor(out=hT[:].rearrange("p h j -> p (h j)"), in0=hT[:].rearrange("p h j -> p (h j)"),
                                                  in1=bank[2][:], op=ALU.add),
                 reads=[K("hT"), bk[2]], writes=[K("hT")])
            p.op("act", lambda e: e.copy(out=hTb[:], in_=hT[:]), reads=[K("hT")], writes=[K("hTb")])

        ckeys = [K("catT", 0), K("catT", 1), K("catT", 2, 0), K("catT", 2, 1), K("catT", 4, 0), K("catT", 4, 1)]
        i = 0
        for b in range(2):
            for hf in range(2):
                bi = 5 + (i % 2); i += 1
                for k in range(KC):
                    p.op("pe", lambda e, k=k, b=b, hf=hf, bi=bi: e.matmul(
                        bank[bi][:], lhsT=catT[:, k, b * 128:(b + 1) * 128], rhs=wout[:, k, hf * 512:(hf + 1) * 512],
                        start=(k == 0), stop=(k == KC - 1)),
                        reads=ckeys + [K("wout", k)], writes=[bk[bi]])
                p.op("dve", lambda e, b=b, hf=hf, bi=bi, x=x: e.tensor_tensor(
                    out=x[:, b, hf * 512:(hf + 1) * 512], in0=bank[bi][:], in1=x[:, b, hf * 512:(hf + 1) * 512], op=ALU.add),
                    reads=[bk[bi], xk], writes=[xk])
        p.op("sp", lambda e, x=x, t0=t0: e.dma_start(
            out=X_out[t0:t0 + TM, :].rearrange("(b p) d -> p b d", p=128), in_=x[:]),
            reads=[xk], writes=[K("Xout", ti)], dma=True)

        if ti == ntiles - 1:
            proj_tm(0, 1, 0, 256)
            proj_tm(1, 1, 1536, 512)
            proj_tm(2, 1, 2048, 512)
            p.op("act", lambda e: e.copy(out=stg[:, 0:256], in_=bank[0][:, 0:256]), reads=[bk[0]], writes=[K("stg", 0)])
            p.op("act", lambda e: e.copy(out=stg[:, 256:768], in_=bank[1][:]), reads=[bk[1]], writes=[K("stg", 1)])
            p.op("act", lambda e: e.copy(out=stg[:, 768:1280], in_=bank[2][:]), reads=[bk[2]], writes=[K("stg", 2)])
            p.op("sp", lambda e: e.dma_start(out=outs["pool"], in_=stg[113:128, 0:256]),
                 reads=[K("stg", 0)], writes=[K("poolout")], dma=True)
            p.op("sp", lambda e: e.dma_start(out=outs["conv"], in_=stg[125:128, 256:1280]),
                 reads=[K("stg", 1), K("stg", 2)], writes=[K("convout")], dma=True)
            for c in range(4):
                p.op("pe", lambda e, c=c: e.transpose(out=bank[3][:, c * 128:(c + 1) * 128],
                                                      in_=hT[:, 2 * c:2 * c + 2, :].rearrange("p h j -> p (h j)"),
                                                      identity=identf[:]),
                     reads=[K("hT"), K("identf")], writes=[bk[3]])
            p.op("act", lambda e: e.copy(out=hfin[:].rearrange("p c n -> p (c n)"), in_=bank[3][:]), reads=[bk[3]], writes=[K("hfin")])
            p.op("sp", lambda e: e.dma_start(out=outs["ssm"].rearrange("(c hp) n -> hp c n", c=4), in_=hfin[:]),
                 reads=[K("hfin")], writes=[K("ssmout")], dma=True)
    return dict(win=win, wout=wout, poolw=poolw, identf=identf, ident=ident)


DFF = 2816
FC = DFF // 128
NT = 8192
NS = 4
PAST = 16384
WBK = 2048


def ffn_phase(nc, p, st, tag, X_in, X_out, Xs_in, Xs_out, gain_d, wg_d, wu_d, wd_d, ntok,
              fin_gain=None, Y_out=None, Ys_out=None, on_tile_out=None):
    T = 512
    sb = lambda name, shape, dt: st.enter_context(nc.sbuf_tensor(tag + name, shape, dt))
    ps = lambda name, shape, dt: st.enter_context(nc.psum_tensor(tag + name, shape, dt))
    K = lambda *a: (tag,) + a
    wg = sb("wg", [128, KC, DFF], BF16)
    wu = sb("wu", [128, KC, DFF], BF16)
    wd = sb("wd", [128, FC, D], BF16)
    gain = sb("gain", [128, D], F32)
    ident = sb("ident", [128, 128], BF16)
    identf = sb("identf", [128, 128], F32)
    xt = [sb("x%d" % i, [128, 4, D], F32) for i in range(2)]
    xn_ = sb("xn0", [128, D], BF16)
    xn = [xn_, xn_]
    ss = sb("ss", [128, 8], F32)
    epsb = sb("epsb", [128, 1], F32)
    rstd = sb("rstd", [128, 8], F32)
    xnT = sb("xnT", [128, KC, T], BF16)
    hT = sb("hT", [128, FC, T], BF16)
    sg = [sb("sg%d" % i, [128, T], BF16) for i in range(2)]
    pT = [ps("pT%d" % i, [128, 1024], BF16) for i in range(2)]
    pg = [ps("pg%d" % i, [128, T], F32) for i in range(2)]
    pu = [ps("pu%d" % i, [128, T], F32) for i in range(2)]
    pd = [ps("pd%d" % i, [128, 512], F32) for i in range(2)]
    p.excl = set(p.excl) | {K("pT", i) for i in range(2)} | {K("pg", i) for i in range(2)} \
        | {K("pu", i) for i in range(2)} | {K("pd", i) for i in range(2)}
    if fin_gain is not None:
        fgain = xnT
        fgain = sb("fgain", [128, D], F32)

    p.op("pool", lambda e: e.memset(epsb[:], EPS), writes=[K("epsb")])
    p.op("pool", lambda e: e.memset(identf[:], 0.0), writes=[K("identf")])
    p.op("pool", lambda e: e.affine_select(out=identf[:], in_=identf[:], pattern=[[-1, 128]],
                                          compare_op=ALU.not_equal, fill=1.0, base=0, channel_multiplier=1),
         reads=[K("identf")], writes=[K("identf")])
    p.op("dve", lambda e: e.tensor_copy(out=ident[:], in_=identf[:]), reads=[K("identf")], writes=[K("ident")])
    p.op("sp", lambda e: e.dma_start(out=gain[:], in_=gain_d.partition_broadcast(128)), writes=[K("gain")], dma=True)
    CB = 1408
    for cb in range(0, DFF, CB):
        w_ = min(CB, DFF - cb)
        for k in range(KC):
            p.op("pool", lambda e, k=k: e.dma_start(out=wg[:, k, cb:cb + w_], in_=wg_d[k * 128:(k + 1) * 128, cb:cb + w_]),
                 writes=[K("wg", k, cb // CB)], dma=True)
            p.op("pool", lambda e, k=k: e.dma_start(out=wu[:, k, cb:cb + w_], in_=wu_d[k * 128:(k + 1) * 128, cb:cb + w_]),
                 writes=[K("wu", k, cb // CB)], dma=True)
    for c in range(FC):
        p.op("pool", lambda e, c=c: e.dma_start(out=wd[:, c, :], in_=wd_d[c * 128:(c + 1) * 128, :]),
             writes=[K("wd", c)], dma=True)
    if fin_gain is not None:
        p.op("sp", lambda e: e.dma_start(out=fgain[:], in_=fin_gain.partition_broadcast(128)), writes=[K("fgain")], dma=True)

    state = {"ti": 0}

    def do_tile(src, dst, ydst, P, nb):
        ti = state["ti"]
        state["ti"] += 1
        N = nb * P
        x = xt[ti % 2]
        xk = K("x", ti % 2)
        p.op("sp", lambda e: e.dma_start(out=x[0:P, 0:nb, :], in_=src.rearrange("(b p) d -> p b d", p=P)),
             writes=[xk], dma=True)
        for b in range(nb):
            xnb = xn[b % 2]
            xnk = K("xn", 0)
            p.op("act", lambda e: e.activation(out=xnb[0:P, :], in_=x[0:P, b, :], func=AF.Square, accum_out=ss[0:P, b:b + 1]),
                 reads=[xk], writes=[xnk, K("ss", b)])
            p.op("act", lambda e: e.activation(out=rstd[0:P, b:b + 1], in_=ss[0:P, b:b + 1], func=AF.Sqrt,
                                               scale=1.0 / D, bias=epsb[0:P, :]),
                 reads=[K("ss", b), K("epsb")], writes=[K("rstd", b)])
            p.op("dve", lambda e: e.reciprocal(out=rstd[0:P, b:b + 1], in_=rstd[0:P, b:b + 1]),
                 reads=[K("rstd", b)], writes=[K("rstd", b)])
            p.op("dve", lambda e: e.scalar_tensor_tensor(out=xnb[0:P, :], in0=x[0:P, b, :], scalar=rstd[0:P, b:b + 1],
                                                         in1=gain[0:P, :], op0=ALU.mult, op1=ALU.mult),
                 reads=[xk, K("rstd", b), K("gain")], writes=[xnk])
            pt = pT[b % 2]
            ptk = K("pT", b % 2)
            for k in range(KC):
                p.op("pe", lambda e, k=k: e.transpose(out=pt[:, k * P:(k + 1) * P], in_=xnb[0:P, k * 128:(k + 1) * 128],
                                                      identity=ident[0:P, 0:P]),
                     reads=[xnk, K("ident")], writes=[ptk])
            p.op("act", lambda e: e.copy(out=xnT[:, :, b * P:(b + 1) * P],
                                         in_=pt[:, 0:KC * P].rearrange("p (k t) -> p k t", k=KC)),
                 reads=[ptk], writes=[K("xnT", b)])
        xnT_keys = [K("xnT", b) for b in range(nb)]
        for c in range(FC):
            g_ = pg[c % 2]; u_ = pu[c % 2]; s_ = sg[c % 2]
            gk = K("pg", c % 2); uk = K("pu", c % 2); sk = K("sg", c % 2)
            for k in range(KC):
                p.op("pe", lambda e, k=k: e.matmul(g_[:, 0:N], lhsT=wg[:, k, c * 128:(c + 1) * 128], rhs=xnT[:, k, 0:N],
                                                   start=(k == 0), stop=(k == KC - 1)),
                     reads=xnT_keys + [K("wg", k, (c * 128) // 1408)], writes=[gk])
            for k in range(KC):
                p.op("pe", lambda e, k=k: e.matmul(u_[:, 0:N], lhsT=wu[:, k, c * 128:(c + 1) * 128], rhs=xnT[:, k, 0:N],
                                                   start=(k == 0), stop=(k == KC - 1)),
                     reads=xnT_keys + [K("wu", k, (c * 128) // 1408)], writes=[uk])
            p.op("act", lambda e: e.activation(out=s_[:, 0:N], in_=g_[:, 0:N], func=AF.Silu), reads=[gk], writes=[sk])
            p.op("dve", lambda e: e.tensor_tensor(out=hT[:, c, 0:N], in0=u_[:, 0:N], in1=s_[:, 0:N], op=ALU.mult),
                 reads=[uk, sk], writes=[K("hT", c)])
        i = 0
        for b in range(nb):
            for hf in range(2):
                d_ = pd[i % 2]; dk = K("pd", i % 2); i += 1
                for c in range(FC):
                    p.op("pe", lambda e, c=c: e.matmul(d_[0:P, :], lhsT=hT[:, c, b * P:(b + 1) * P],
                                                       rhs=wd[:, c, hf * 512:(hf + 1) * 512],
                                                       start=(c == 0), stop=(c == FC - 1)),
                         reads=[K("hT", c), K("wd", c)], writes=[dk])
                p.op("dve", lambda e: e.scalar_tensor_tensor(
                    out=x[0:P, b, hf * 512:(hf + 1) * 512], in0=d_[0:P, :], scalar=0.5,
                    in1=x[0:P, b, hf * 512:(hf + 1) * 512], op0=ALU.mult, op1=ALU.add),
                    reads=[dk, xk], writes=[xk])
            if fin_gain is not None:
                xnb = xn[b % 2]
                xnk = K("xn", 0)
                p.op("act", lambda e: e.activation(out=xnb[0:P, :], in_=x[0:P, b, :], func=AF.Square,
                                                   accum_out=ss[0:P, 4 + b:5 + b]),
                     reads=[xk], writes=[xnk, K("ss", 4 + b)])
                p.op("act", lambda e: e.activation(out=rstd[0:P, 4 + b:5 + b], in_=ss[0:P, 4 + b:5 + b], func=AF.Sqrt,
                                                   scale=1.0 / D, bias=epsb[0:P, :]),
                     reads=[K("ss", 4 + b), K("epsb")], writes=[K("rstd", 4 + b)])
                p.op("dve", lambda e: e.reciprocal(out=rstd[0:P, 4 + b:5 + b], in_=rstd[0:P, 4 + b:5 + b]),
                     reads=[K("rstd", 4 + b)], writes=[K("rstd", 4 + b)])
                p.op("dve", lambda e: e.scalar_tensor_tensor(out=x[0:P, b, :], in0=x[0:P, b, :], scalar=rstd[0:P, 4 + b:5 + b],
                                                             in1=fgain[0:P, :], op0=ALU.mult, op1=ALU.mult),
                     reads=[xk, K("rstd", 4 + b), K("fgain")], writes=[xk])
        out_ap = ydst if fin_gain is not None else dst
        p.op("sp", lambda e: e.dma_start(out=out_ap.rearrange("(b p) d -> p b d", p=P), in_=x[0:P, 0:nb, :]),
             reads=[xk], writes=[K("Xout", ti)], dma=True)
        if on_tile_out is not None and P == 128:
            on_tile_out(ti, K("Xout", ti))

    for t0 in range(0, ntok, T):
        n = min(T, ntok - t0)
        do_tile(X_in[t0:t0 + n, :], X_out[t0:t0 + n, :] if X_out is not None else None,
                Y_out[t0:t0 + n, :] if Y_out is not None else None, 128, n // 128)
    do_tile(Xs_in, Xs_out, Ys_out, NS, 1)


def sample_mixer_phase(nc, p, st, tag, Xs_in, Xs_out, W, C, caches, outs, scr, shared=None):
    sb = lambda name, shape, dt: st.enter_context(nc.sbuf_tensor(tag + name, shape, dt))
    ps = lambda name, shape, dt: st.enter_context(nc.psum_tensor(tag + name, shape, dt))
    K = lambda *a: (tag,) + a
    P = NS
    if shared is not None:
        win, wout, poolw, identf, ident = (shared[k_] for k_ in ("win", "wout", "poolw", "identf", "ident"))
    else:
        win = sb("win", [128, KC, DIN], BF16)
        wout = sb("wout", [128, KC, D], BF16)
        poolw = sb("poolw", [128, 2, 128], BF16)
        identf = sb("identf", [128, 128], F32)
        ident = sb("ident", [128, 128], BF16)
    epsb = sb("epsb", [128, 1], F32)
    x4 = sb("x4", [P, D], F32)
    xn4 = sb("xn4", [P, D], BF16)
    xnT4 = sb("xnT4", [128, KC, P], BF16)
    gain4 = sb("gain4", [P, D], F32)
    ss = sb("ss", [P, 4], F32)
    rstd = sb("rstd", [P, 4], F32)
    pr = sb("pr", [P, DIN], F32)
    xac = sb("xac", [P, 16, 256], F32)
    sums = sb("sums", [P, 256], F32)
    d4 = sb("d4", [P, 256], BF16)
    dT = sb("dT", [128, 2, P], BF16)
    psc4 = sb("psc4", [P, 256], F32)
    cat4 = sb("cat4", [P, D], BF16)
    catT4 = sb("catT4", [128, KC, P], BF16)
    xbcc = sb("xbcc", [P, 4, 1024], F32)
    cw = sb("cw", [P, 4, 1024], F32)
    cb4 = sb("cb4", [P, 1024], F32)
    u4 = sb("u4", [P, 1032], F32)
    dtb4 = sb("dtb4", [P, 8], F32)
    dtr = sb("dtr", [P, 8], F32)
    dtt = sb("dtt", [P, 8], F32)
    x32 = sb("x32", [32, 64], F32)
    dt32 = sb("dt32", [32, 1], F32)
    a32 = sb("a32", [32, 1], F32)
    dA32 = sb("dA32", [32, 1], F32)
    dtx32 = sb("dtx32", [32, 64], F32)
    BC32 = sb("BC32", [32, 2, 128], F32)
    hc = [sb("hc%d" % i, [32, 16, 128], F32) for i in range(2)]
    tmp = sb("tmp", [32, 16, 128], F32)
    y32 = sb("y32", [32, 64], F32)
    y4 = sb("y4", [P, 512], F32)
    t4 = sb("t4", [P, 512], F32)
    zs4 = sb("zs4", [P, 512], F32)
    dsk4 = sb("dsk4", [P, 8], F32)
    ssmn4 = sb("ssmn4", [P, 512], F32)
    ssq = sb("ssq", [P, 2], F32)
    grs = sb("grs", [P, 2], F32)
    cos4 = sb("cos4", [P, 32], F32)
    sin4 = sb("sin4", [P, 32], F32)
    qkr4 = sb("qkr4", [P, 512], F32)
    rt = [sb("rt%d" % i, [P, 256], F32) for i in range(4)]
    qb = sb("qb", [128, P, 256], F32)
    Kg = [sb("Kg%d" % i, [128, 256], F32) for i in range(2)]
    Va = [sb("Va%d" % i, [128, 4, 65], F32) for i in range(2)]
    prod = sb("prod", [128, 256], F32)
    sc = sb("sc", [128, 4], F32)
    Pz = sb("Pz", [128, P, 4, P], F32)
    O4 = sb("O4", [P, 4, 65], F32)
    sn = sb("sn", [P, 4], F32)
    p3 = sb("p3", [P, 4], F32)
    den = sb("den", [P, 4], F32)
    num = sb("num", [P, 4, 64], F32)
    bank = [ps("b%d" % i, [128, 512], F32) for i in range(8)]
    bk = [K("bank", i) for i in range(8)]
    p.excl = set(p.excl) | set(bk)
    bfv = lambda i: bank[i][:].bitcast(BF16)

    def ld(dst, src, key, eng="sp", reads=()):
        p.op(eng, lambda e: e.dma_start(out=dst, in_=src), reads=list(reads), writes=[key], dma=True)

    if shared is None:
        for k in range(KC):
            ld(win[:, k, :], W["w_in"][k * 128:(k + 1) * 128, :], K("win", k), "pool")
        for k in range(KC):
            ld(wout[:, k, :], W["w_out"][k * 128:(k + 1) * 128, :], K("wout", k), "pool")
        p.op("pool", lambda e: e.memset(poolw[:], 0.0), writes=[K("poolw")])
        for g in range(4):
            p.op("pool", lambda e, g=g: e.dma_start(
                out=poolw[(g % 2) * 64:(g % 2) * 64 + 64, g // 2, (g % 2) * 64:(g % 2) * 64 + 64], in_=W["pool_w"][g]),
                reads=[K("poolw")], writes=[K("poolw")], dma=True)
    p.op("pool", lambda e: e.memset(epsb[:], EPS), writes=[K("epsb")])
    p.op("pool", lambda e: e.memset(Pz[:], 0.0), writes=[K("Pz", n) for n in range(P)])
    for i in range(2):
        p.op("pool", lambda e, i=i: e.memset(Va[i][:, :, 64:65], 1.0), writes=[K("Va1", i)])
    if shared is None:
        ld(identf[:], C["identf"], K("identf"))
        p.op("dve", lambda e: e.tensor_copy(out=ident[:], in_=identf[:]), reads=[K("identf")], writes=[K("ident")])
    ld(gain4[:], W["mix_norm"].partition_broadcast(P), K("gain4"))
    ld(psc4[:], W["pool_scale"].partition_broadcast(P), K("psc4"))
    ld(cw[:].rearrange("p k c -> p (k c)"), W["conv_w"].rearrange("k c -> (k c)").partition_broadcast(P), K("cw"))
    ld(cb4[:], W["conv_b"].partition_broadcast(P), K("cb4"))
    ld(dtb4[:], W["dt_bias"].partition_broadcast(P), K("dtb4"))
    ld(dsk4[:], W["d_skip"].partition_broadcast(P), K("dsk4"))
    ld(ssmn4[:], W["ssm_norm"].partition_broadcast(P), K("ssmn4"))
    ld(cos4[:], C["cos_s"].partition_broadcast(P), K("cos4"))
    ld(sin4[:], C["sin_s"].partition_broadcast(P), K("sin4"))
    for n in range(P):
        ld(a32[n * 8:(n + 1) * 8, :], W["a_log"].rearrange("(h o) -> h o", o=1), K("a32"), reads=[K("a32")] if n else [])
    p.op("act", lambda e: e.activation(out=a32[:], in_=a32[:], func=AF.Exp), reads=[K("a32")], writes=[K("a32")])
    p.op("dve", lambda e: e.tensor_scalar_mul(out=a32[:], in0=a32[:], scalar1=-1.0), reads=[K("a32")], writes=[K("a32")])

    ld(x4[:], Xs_in, K("x4"))
    p.op("act", lambda e: e.activation(out=xn4[:], in_=x4[:], func=AF.Square, accum_out=ss[:, 0:1]),
         reads=[K("x4")], writes=[K("xn4"), K("ss")])
    p.op("act", lambda e: e.activation(out=rstd[:, 0:1], in_=ss[:, 0:1], func=AF.Sqrt, scale=1.0 / D, bias=epsb[0:P, :]),
         reads=[K("ss"), K("epsb")], writes=[K("rstd")])
    p.op("dve", lambda e: e.reciprocal(out=rstd[:, 0:1], in_=rstd[:, 0:1]), reads=[K("rstd")], writes=[K("rstd")])
    p.op("dve", lambda e: e.scalar_tensor_tensor(out=xn4[:], in0=x4[:], scalar=rstd[:, 0:1], in1=gain4[:],
                                                 op0=ALU.mult, op1=ALU.mult),
         reads=[K("x4"), K("rstd"), K("gain4")], writes=[K("xn4")])

    def transpose4(src, nchunk, dstT, skey, dkey):
        for k in range(nchunk):
            p.op("pe", lambda e, k=k: e.transpose(out=bfv(7)[:, k * P:(k + 1) * P], in_=src[0:P, k * 128:(k + 1) * 128],
                                                  identity=ident[0:P, 0:P]),
                 reads=[skey, K("ident")], writes=[bk[7]])
        p.op("act", lambda e: e.copy(out=dstT[:, 0:nchunk, :], in_=bfv(7)[:, 0:nchunk * P].rearrange("p (k t) -> p k t", k=nchunk)),
             reads=[bk[7]], writes=[dkey])

    transpose4(xn4, KC, xnT4, K("xn4"), K("xnT4"))
    for cc in range(6):
        c0 = cc * 512
        n = min(512, DIN - c0)
        bi = cc % 2
        for k in range(KC):
            p.op("pe", lambda e, k=k: e.matmul(bank[bi][0:P, 0:n], lhsT=xnT4[:, k, :], rhs=win[:, k, c0:c0 + n],
                                               start=(k == 0), stop=(k == KC - 1)),
                 reads=[K("xnT4"), K("win", k)], writes=[bk[bi]])
        p.op("act", lambda e: e.copy(out=pr[:, c0:c0 + n], in_=bank[bi][0:P, 0:n]), reads=[bk[bi]], writes=[K("pr", cc)])
    ld(xac[:, 0:15, :], caches["pool"], K("xac"))
    p.op("dve", lambda e: e.tensor_copy(out=xac[:, 15, :], in_=pr[:, 0:256]), reads=[K("pr", 0), K("xac")], writes=[K("xac")])
    ld(outs["pool"], xac[:, 1:16, :], K("o_pool"), reads=[K("xac")])
    for g, ws in enumerate((2, 4, 8, 16)):
        p.op("dve", lambda e, g=g, ws=ws: e.tensor_reduce(
            out=sums[:, g * 64:(g + 1) * 64], in_=xac[:, 16 - ws:16, g * 64:(g + 1) * 64].rearrange("p r c -> p c r"),
            axis=AX.X, op=ALU.add),
            reads=[K("xac")], writes=[K("sums")])
        p.op("dve", lambda e, g=g, ws=ws: e.scalar_tensor_tensor(
            out=d4[:, g * 64:(g + 1) * 64], in0=sums[:, g * 64:(g + 1) * 64], scalar=1.0 / ws,
            in1=xac[:, 15, g * 64:(g + 1) * 64], op0=ALU.mult, op1=ALU.subtract),
            reads=[K("sums"), K("xac")], writes=[K("d4")])
    transpose4(d4, 2, dT, K("d4"), K("dT"))
    for c in range(2):
        p.op("pe", lambda e, c=c: e.matmul(bank[2][0:P, c * 128:(c + 1) * 128], lhsT=dT[:, c, :], rhs=poolw[:, c, :],
                                           start=True, stop=True),
             reads=[K("dT"), K("poolw")], writes=[bk[2]])
    p.op("dve", lambda e: e.tensor_tensor(out=cat4[:, 0:256], in0=bank[2][0:P, 0:256], in1=psc4[:], op=ALU.mult),
         reads=[bk[2], K("psc4")], writes=[K("cat4", 0)])
    ld(xbcc[:, 0:3, :], caches["conv"], K("xbcc"))
    p.op("dve", lambda e: e.tensor_copy(out=xbcc[:, 3, :], in_=pr[:, 1536:2560]),
         reads=[K("pr", 3), K("pr", 4), K("xbcc")], writes=[K("xbcc")])
    ld(outs["conv"], xbcc[:, 1:4, :], K("o_conv"), reads=[K("xbcc")])
    p.op("dve", lambda e: e.tensor_tensor(out=cw[:], in0=cw[:], in1=xbcc[:], op=ALU.mult),
         reads=[K("cw"), K("xbcc")], writes=[K("cw")])
    p.op("dve", lambda e: e.tensor_reduce(out=u4[:, 0:1024], in_=cw[:].rearrange("p k c -> p c k"), axis=AX.X, op=ALU.add),
         reads=[K("cw")], writes=[K("u4")])
    p.op("dve", lambda e: e.tensor_tensor(out=u4[:, 0:1024], in0=u4[:, 0:1024], in1=cb4[:], op=ALU.add),
         reads=[K("u4"), K("cb4")], writes=[K("u4")])
    p.op("act", lambda e: e.activation(out=u4[:, 0:1024], in_=u4[:, 0:1024], func=AF.Silu), reads=[K("u4")], writes=[K("u4")])
    p.op("dve", lambda e: e.tensor_tensor(out=dtr[:], in0=pr[:, 2560:2568], in1=dtb4[:], op=ALU.add),
         reads=[K("pr", 5), K("dtb4")], writes=[K("dtr")])
    p.op("dve", lambda e: e.tensor_scalar_mul(out=dtt[:], in0=dtr[:], scalar1=-1.0), reads=[K("dtr")], writes=[K("dtt")])
    p.op("dve", lambda e: e.tensor_tensor(out=dtt[:], in0=dtt[:], in1=dtr[:], op=ALU.max), reads=[K("dtr"), K("dtt")], writes=[K("dtt")])
    p.op("act", lambda e: e.activation(out=dtt[:], in_=dtt[:], func=AF.Exp, scale=-1.0), reads=[K("dtt")], writes=[K("dtt")])
    p.op("act", lambda e: e.activation(out=dtt[:], in_=dtt[:], func=AF.Ln, bias=1.0), reads=[K("dtt")], writes=[K("dtt")])
    p.op("dve", lambda e: e.scalar_tensor_tensor(out=u4[:, 1024:1032], in0=dtr[:], scalar=0.0, in1=dtt[:], op0=ALU.max, op1=ALU.add),
         reads=[K("dtr"), K("dtt"), K("u4")], writes=[K("u4")])
    SSx, SSbc, SSdt, SS2, SQ = scr["SSx"], scr["SSbc"], scr["SSdt"], scr["SS2"], scr["SQ"]
    ld(SSx, u4[:, 0:512], K("SSx"), reads=[K("u4")])
    ld(SSbc, u4[:, 512:1024], K("SSbc"), reads=[K("u4")])
    ld(SSdt, u4[:, 1024:1032], K("SSdt"), reads=[K("u4")])
    ld(x32[:], SSx.rearrange("n (h q) -> (n h) q", h=8), K("x32"), reads=[K("SSx")])
    ld(dt32[:], SSdt.rearrange("n (h o) -> (n h) o", o=1), K("dt32"), reads=[K("SSdt")])
    for n in range(P):
        for g in range(2):
            src = SSbc[n].rearrange("(t g k) -> g t k", t=2, g=2)[g]
            ld(BC32[n * 8 + g * 4:n * 8 + g * 4 + 4, :, :], src.partition_broadcast(4), K("BC32", n, g), reads=[K("SSbc")])
    bckeys = [K("BC32", n, g) for n in range(P) for g in range(2)]
    p.op("act", lambda e: e.activation(out=dA32[:], in_=dt32[:], func=AF.Exp, scale=a32[:, 0:1]),
         reads=[K("dt32"), K("a32")], writes=[K("dA32")])
    p.op("dve", lambda e: e.tensor_scalar_mul(out=dtx32[:], in0=x32[:], scalar1=dt32[:, 0:1]),
         reads=[K("x32"), K("dt32")], writes=[K("dtx32")])
    hview = caches["ssm"].rearrange("n h (q r) k -> q (n h) r k", q=4)
    oview = outs["ssm"].rearrange("n h (q r) k -> q (n h) r k", q=4)
    for q in range(4):
        h_ = hc[q % 2]
        hk = K("hc", q % 2)
        ld(h_[:], hview[q], hk)
        p.op("dve", lambda e, q=q: e.tensor_tensor(out=tmp[:], in0=dtx32[:, q * 16:(q + 1) * 16].unsqueeze(2).to_broadcast([32, 16, 128]),
                                                   in1=BC32[:, 0, :].unsqueeze(1).to_broadcast([32, 16, 128]), op=ALU.mult),
             reads=[K("dtx32")] + bckeys, writes=[K("tmp")])
        p.op("dve", lambda e, h_=h_: e.scalar_tensor_tensor(out=h_[:], in0=h_[:], scalar=dA32[:, 0:1], in1=tmp[:],
                                                            op0=ALU.mult, op1=ALU.add),
             reads=[hk, K("dA32"), K("tmp")], writes=[hk])
        ld(oview[q], h_[:], K("o_ssm", q), reads=[hk])
        p.op("dve", lambda e, h_=h_: e.tensor_tensor(out=tmp[:], in0=h_[:], in1=BC32[:, 1, :].unsqueeze(1).to_broadcast([32, 16, 128]),
                                                     op=ALU.mult),
             reads=[hk] + bckeys, writes=[K("tmp")])
        p.op("dve", lambda e, q=q: e.tensor_reduce(out=y32[:, q * 16:(q + 1) * 16], in_=tmp[:], axis=AX.X, op=ALU.add),
             reads=[K("tmp")], writes=[K("y32")])
    ld(SS2.rearrange("n (h q) -> (n h) q", h=8), y32[:], K("SS2"), reads=[K("y32")])
    ld(y4[:], SS2, K("y4"), reads=[K("SS2")])
    v8 = lambda t: t.rearrange("p (h j) -> p h j", h=8)
    p.op("dve", lambda e: e.tensor_tensor(out=v8(t4[:]), in0=v8(u4[:, 0:512]), in1=dsk4[:].unsqueeze(2).to_broadcast([P, 8, 64]),
                                          op=ALU.mult),
         reads=[K("u4"), K("dsk4")], writes=[K("t4")])
    p.op("dve", lambda e: e.tensor_tensor(out=y4[:], in0=y4[:], in1=t4[:], op=ALU.add), reads=[K("y4"), K("t4")], writes=[K("y4")])
    p.op("act", lambda e: e.activation(out=zs4[:], in_=pr[:, 1024:1536], func=AF.Silu), reads=[K("pr", 2)], writes=[K("zs4")])
    p.op("dve", lambda e: e.tensor_tensor(out=y4[:], in0=y4[:], in1=zs4[:], op=ALU.mult), reads=[K("y4"), K("zs4")], writes=[K("y4")])
    for g in range(2):
        p.op("act", lambda e, g=g: e.activation(out=t4[:, g * 256:(g + 1) * 256], in_=y4[:, g * 256:(g + 1) * 256],
                                               func=AF.Square, accum_out=ssq[:, g:g + 1]),
             reads=[K("y4")], writes=[K("t4"), K("ssq", g)])
        p.op("act", lambda e, g=g: e.activation(out=grs[:, g:g + 1], in_=ssq[:, g:g + 1], func=AF.Sqrt, scale=1.0 / 256,
                                               bias=epsb[0:P, :]),
             reads=[K("ssq", g), K("epsb")], writes=[K("grs", g)])
        p.op("dve", lambda e, g=g: e.reciprocal(out=grs[:, g:g + 1], in_=grs[:, g:g + 1]), reads=[K("grs", g)], writes=[K("grs", g)])
        p.op("dve", lambda e, g=g: e.scalar_tensor_tensor(
            out=cat4[:, 512 + g * 256:768 + g * 256], in0=y4[:, g * 256:(g + 1) * 256], scalar=grs[:, g:g + 1],
            in1=ssmn4[:, g * 256:(g + 1) * 256], op0=ALU.mult, op1=ALU.mult),
            reads=[K("y4"), K("grs", g), K("ssmn4")], writes=[K("cat4", 2 + g)])
    v4 = lambda t: t.rearrange("p (h s j) -> p h s j", h=8, s=2)
    r3 = lambda t: t[:].rearrange("p (h j) -> p h j", h=8)
    cb_ = cos4[:].unsqueeze(1).to_broadcast([P, 8, 32])
    sb_ = sin4[:].unsqueeze(1).to_broadcast([P, 8, 32])
    x1 = v4(pr[:, 256:768])[:, :, 0, :]
    x2 = v4(pr[:, 256:768])[:, :, 1, :]
    prk = [K("pr", 0), K("pr", 1)]
    p.op("dve", lambda e: e.tensor_tensor(out=r3(rt[0]), in0=x1, in1=cb_, op=ALU.mult), reads=prk + [K("cos4")], writes=[K("rt", 0)])
    p.op("dve", lambda e: e.tensor_tensor(out=r3(rt[1]), in0=x2, in1=sb_, op=ALU.mult), reads=prk + [K("sin4")], writes=[K("rt", 1)])
    p.op("dve", lambda e: e.tensor_tensor(out=v4(qkr4[:])[:, :, 0, :], in0=r3(rt[0]), in1=r3(rt[1]), op=ALU.subtract),
         reads=[K("rt", 0), K("rt", 1)], writes=[K("qkr4")])
    p.op("dve", lambda e: e.tensor_tensor(out=r3(rt[2]), in0=x2, in1=cb_, op=ALU.mult), reads=prk + [K("cos4")], writes=[K("rt", 2)])
    p.op("dve", lambda e: e.tensor_tensor(out=r3(rt[3]), in0=x1, in1=sb_, op=ALU.mult), reads=prk + [K("sin4")], writes=[K("rt", 3)])
    p.op("dve", lambda e: e.tensor_tensor(out=v4(qkr4[:])[:, :, 1, :], in0=r3(rt[2]), in1=r3(rt[3]), op=ALU.add),
         reads=[K("rt", 2), K("rt", 3), K("qkr4")], writes=[K("qkr4")])
    ld(outs["k"][:, WBK - 1, :], qkr4[:, 256:512], K("o_k1"), reads=[K("qkr4")])
    ld(outs["v"][:, WBK - 1, :], pr[:, 768:1024], K("o_v1"), reads=[K("pr", 1)])
    ld(SQ, qkr4[:, 0:256], K("SQ"), reads=[K("qkr4")])
    ld(qb[:].rearrange("p n c -> p (n c)"), SQ.rearrange("n c -> (n c)").partition_broadcast(128), K("qb"), reads=[K("SQ")])
    it = 0
    nmm = P * 3 * 4
    imm = 0
    for n in range(P):
        for dil in (1, 4, 16):
            kg = Kg[it % 2]; va = Va[it % 2]
            kk = K("Kg", it % 2); vk = K("Va", it % 2)
            s0 = WBK - 128 * dil
            ld(kg[:], caches["k"][n, s0:WBK:dil, :], kk)
            ld(va[:, :, 0:64], caches["v"][n, s0:WBK:dil, :].rearrange("r (h j) -> r h j", h=4), vk)
            p.op("dve", lambda e, n=n, kg=kg: e.tensor_tensor(out=prod[:], in0=kg[:], in1=qb[:, n, :], op=ALU.mult),
                 reads=[kk, K("qb")], writes=[K("prod")])
            p.op("dve", lambda e: e.tensor_reduce(out=sc[:], in_=prod[:].rearrange("p (h j) -> p h j", h=4), axis=AX.X, op=ALU.add),
                 reads=[K("prod")], writes=[K("sc")])
            p.op("act", lambda e, n=n: e.activation(out=Pz[:, n, :, n], in_=sc[:], func=AF.Exp, scale=0.125),
                 reads=[K("sc")], writes=[K("Pz", n)])
            for h in range(4):
                p.op("pe", lambda e, n=n, h=h, va=va, st_=(imm == 0), sp_=(imm == nmm - 1): e.matmul(
                    bank[3][0:P, h * 65:(h + 1) * 65], lhsT=Pz[:, n, h, :], rhs=va[:, h, :], start=st_, stop=sp_,
                    skip_group_check=True),
                    reads=[K("Pz", n), vk, K("Va1", it % 2)], writes=[bk[3]])
                imm += 1
            it += 1
    p.op("act", lambda e: e.copy(out=O4[:].rearrange("p h j -> p (h j)"), in_=bank[3][0:P, 0:260]), reads=[bk[3]], writes=[K("O4")])
    q4v = qkr4[:, 0:256].rearrange("p (h j) -> p h j", h=4)
    k4v = qkr4[:, 256:512].rearrange("p (h j) -> p h j", h=4)
    v4n = pr[:, 768:1024].rearrange("p (h j) -> p h j", h=4)
    p.op("dve", lambda e: e.tensor_tensor(out=num[:], in0=q4v, in1=k4v, op=ALU.mult), reads=[K("qkr4")], writes=[K("num")])
    p.op("dve", lambda e: e.tensor_reduce(out=sn[:], in_=num[:], axis=AX.X, op=ALU.add), reads=[K("num")], writes=[K("sn")])
    p.op("act", lambda e: e.activation(out=p3[:], in_=sn[:], func=AF.Exp, scale=0.125), reads=[K("sn")], writes=[K("p3")])
    p.op("dve", lambda e: e.tensor_scalar_mul(out=p3[:], in0=p3[:], scalar1=3.0), reads=[K("p3")], writes=[K("p3")])
    p.op("dve", lambda e: e.tensor_tensor(out=num[:], in0=v4n, in1=p3[:].unsqueeze(2).to_broadcast([P, 4, 64]), op=ALU.mult),
         reads=[K("pr", 1), K("p3"), K("num")], writes=[K("num")])
    p.op("dve", lambda e: e.tensor_tensor(out=num[:], in0=num[:], in1=O4[:, :, 0:64], op=ALU.add),
         reads=[K("num"), K("O4")], writes=[K("num")])
    p.op("dve", lambda e: e.tensor_tensor(out=den[:].unsqueeze(2), in0=O4[:, :, 64:65], in1=p3[:].unsqueeze(2), op=ALU.add),
         reads=[K("O4"), K("p3")], writes=[K("den")])
    p.op("dve", lambda e: e.reciprocal(out=den[:], in_=den[:]), reads=[K("den")], writes=[K("den")])
    p.op("dve", lambda e: e.tensor_tensor(out=cat4[:, 256:512].rearrange("p (h j) -> p h j", h=4), in0=num[:],
                                          in1=den[:].unsqueeze(2).to_broadcast([P, 4, 64]), op=ALU.mult),
         reads=[K("num"), K("den")], writes=[K("cat4", 1)])
    catkeys = [K("cat4", i) for i in range(4)]
    for k in range(KC):
        p.op("pe", lambda e, k=k: e.transpose(out=bfv(7)[:, k * P:(k + 1) * P], in_=cat4[0:P, k * 128:(k + 1) * 128],
                                              identity=ident[0:P, 0:P]),
             reads=catkeys + [K("ident")], writes=[bk[7]])
    p.op("act", lambda e: e.copy(out=catT4[:], in_=bfv(7)[:, 0:KC * P].rearrange("p (k t) -> p k t", k=KC)),
         reads=[bk[7]], writes=[K("catT4")])
    for hf in range(2):
        for k in range(KC):
            p.op("pe", lambda e, k=k, hf=hf: e.matmul(bank[4 + hf][0:P, :], lhsT=catT4[:, k, :], rhs=wout[:, k, hf * 512:(hf + 1) * 512],
                                                      start=(k == 0), stop=(k == KC - 1)),
                 reads=[K("catT4"), K("wout", k)], writes=[bk[4 + hf]])
        p.op("dve", lambda e, hf=hf: e.tensor_tensor(out=x4[:, hf * 512:(hf + 1) * 512], in0=bank[4 + hf][0:P, :],
                                                     in1=x4[:, hf * 512:(hf + 1) * 512], op=ALU.add),
             reads=[bk[4 + hf], K("x4")], writes=[K("x4")])
    ld(Xs_out, x4[:], K("Xs_out"), reads=[K("x4")])


WNAMES = ["ffn1_norm", "ffn1_w_gate", "ffn1_w_up", "ffn1_w_down", "mix_norm", "w_in", "pool_w", "pool_scale", "conv_w", "conv_b",
          "dt_bias", "a_log", "d_skip", "ssm_norm", "w_out", "ffn2_norm", "ffn2_w_gate", "ffn2_w_up", "ffn2_w_down"]
WSHAPES = {"ffn1_norm": [2, D], "ffn1_w_gate": [2, D, DFF], "ffn1_w_up": [2, D, DFF], "ffn1_w_down": [2, DFF, D],
           "mix_norm": [2, D], "w_in": [2, D, DIN], "pool_w": [2, 4, 64, 64], "pool_scale": [2, 256], "conv_w": [2, 4, 1024],
           "conv_b": [2, 1024], "dt_bias": [2, 8], "a_log": [2, 8], "d_skip": [2, 8], "ssm_norm": [2, 512], "w_out": [2, D, D],
           "ffn2_norm": [2, D], "ffn2_w_gate": [2, D, DFF], "ffn2_w_up": [2, D, DFF], "ffn2_w_down": [2, DFF, D],
           "pool_scale_l": [2, 128, 2], "conv_w_l": [2, 128, 8, 4], "conv_b_l": [2, 128, 8], "final_norm": [D]}
OUTSHAPES = {"y": [NT, D], "ys": [NS, D], "pool_p": [2, 15, 256], "pool_s": [2, NS, 15, 256], "k_p": [2, WBK, 256],
             "k_s": [2, NS, WBK, 256], "v_p": [2, WBK, 256], "v_s": [2, NS, WBK, 256], "conv_p": [2, 3, 1024],
             "conv_s": [2, NS, 3, 1024], "ssm_p": [2, 512, 128], "ssm_s": [2, NS, 8, 64, 128]}
CACHESHAPES = {"cache_pool": [2, NS, 15, 256], "cache_k": [2, NS, WBK, 256], "cache_v": [2, NS, WBK, 256],
               "state_conv": [2, NS, 3, 1024], "state_ssm": [2, NS, 8, 64, 128]}


def build_program(consts, ntc=NT // 2, depth=2, split=True):
    nc = bass.Bass("TRN2", target_bir_lowering=False)
    di = lambda n, s, dt=F32: nc.dram_tensor(n, list(s), dt, kind="ExternalInput").ap()
    do = lambda n, s: nc.dram_tensor(n, list(s), F32, kind="ExternalOutput").ap()
    X = di("x", [ntc, D])
    XS = di("xs", [NS, D])
    Wd = {k: di(k, v) for k, v in WSHAPES.items()}
    Cd = {k: di("c_" + k, v.shape, BF16 if k == "masks" else F32) for k, v in consts.items()}
    CA = {k: di(k, v) for k, v in CACHESHAPES.items()}
    osh = dict(OUTSHAPES)
    osh["y"] = [ntc, D]
    wbp = min(WBK, ntc)
    osh["k_p"] = [2, wbp, 256]
    osh["v_p"] = [2, wbp, 256]
    O = {k: do(k, v) for k, v in osh.items()}
    Xa = nc.dram_tensor("scr_xa", [ntc, D], F32, kind="Internal").ap()
    Xb = nc.dram_tensor("scr_xb", [ntc, D], F32, kind="Internal").ap()
    Sa = nc.dram_tensor("scr_sa", [NS, D], F32, kind="Internal").ap()
    Sb = nc.dram_tensor("scr_sb", [NS, D], F32, kind="Internal").ap()
    nch = ntc // 512
    G = [[nc.dram_tensor("scr_g%d_%d" % (L, j), [1024, D], F32, kind="Internal").ap() for j in range(nch)]
         for L in range(depth)] if split else None
    scr = {"SSx": nc.dram_tensor("scr_ssx", [NS, 512], F32, kind="Internal").ap(),
           "SSbc": nc.dram_tensor("scr_ssbc", [NS, 512], F32, kind="Internal").ap(),
           "SSdt": nc.dram_tensor("scr_ssdt", [NS, 8], F32, kind="Internal").ap(),
           "SS2": nc.dram_tensor("scr_ss2", [NS, 512], F32, kind="Internal").ap(),
           "SQ": nc.dram_tensor("scr_sq", [NS, 256], F32, kind="Internal").ap()}
    RG = [[0, 1], [2, 3], [4, 5], [6, 7]]
    with ExitStack() as top:
        p = Prog(nc)
        cur, cur_s = X, XS
        for L in range(depth):
            last = (L == depth - 1)
            W = {k: v[L] for k, v in Wd.items() if k != "final_norm"}

            def hook(ti, key, L=L):
                if ti == min(1, ntc // 512 - 1):
                    shift_copies(key)
                if not split:
                    return
                p.op("pool", lambda e: e.collective_compute("AllGather", ALU.bypass, replica_groups=RG,
                                                            ins=[Xa[ti * 512:(ti + 1) * 512, :]], outs=[G[L][ti]]),
                     reads=[key], writes=[("G", L, ti)], cc=True)

            def shift_copies(after_key, L=L):
                for nm_, src_, dst_ in (("k", CA["cache_k"][L], O["k_s"][L]), ("v", CA["cache_v"][L], O["v_s"][L])):
                    for n_ in range(NS):
                        p.op("act", lambda e: e.dma_start(out=dst_[n_, 0:WBK - 1, :], in_=src_[n_, 1:WBK, :]),
                             reads=[after_key], writes=[("cshift", L, nm_, n_)], dma=True)

            with ExitStack() as st:
                ffn_phase(nc, p, st, "f1_%d" % L, cur, Xa, cur_s, Sa, W["ffn1_norm"], W["ffn1_w_gate"], W["ffn1_w_up"],
                          W["ffn1_w_down"], ntc, on_tile_out=hook)
                p.barrier()
            with ExitStack() as wst:
                with ExitStack() as st:
                    outs = dict(pool=O["pool_p"][L], k=O["k_p"][L], v=O["v_p"][L], conv=O["conv_p"][L], ssm=O["ssm_p"][L])
                    xprev = (lambda pi, L=L: G[L][pi // 2][(pi % 2) * 256:(pi % 2) * 256 + 256, :]) if split else None
                    shared = mixer_phase(nc, p, st, "m%d" % L, ntc, Xa, Xb, W, Cd, outs, ntc - wbp, X_prev=xprev, npre=ntc // TM,
                                         wst=wst)
                    p.barrier()
                with ExitStack() as st:
                    caches = dict(pool=CA["cache_pool"][L], k=CA["cache_k"][L], v=CA["cache_v"][L], conv=CA["state_conv"][L],
                                  ssm=CA["state_ssm"][L])
                    outs = dict(pool=O["pool_s"][L], k=O["k_s"][L], v=O["v_s"][L], conv=O["conv_s"][L], ssm=O["ssm_s"][L])
                    sample_mixer_phase(nc, p, st, "s%d" % L, Sa, Sb, W, Cd, caches, outs, scr, shared=shared)
                    p.barrier()
            with ExitStack() as st:
                ffn_phase(nc, p, st, "f2_%d" % L, Xb, Xa, Sb, Sa, W["ffn2_norm"], W["ffn2_w_gate"], W["ffn2_w_up"],
                          W["ffn2_w_down"], ntc, fin_gain=(Wd["final_norm"] if last else None),
                          Y_out=(O["y"] if last else None), Ys_out=(O["ys"] if last else None))
                p.barrier()
            cur, cur_s = Xa, Sa
        p.finish()
        p.emit(top)
    return nc


def make_consts(ntc=NT // 2, half=0, split=True):
    c = host_consts(ntc, pos0=half * ntc)
    if split:
        prev = host_consts(ntc, pos0=(half - 1) * ntc)
        c["cos_prev"], c["sin_prev"] = prev["cos"], prev["sin"]
        c["flag"] = np.full((128, 1), float(half), np.float32)
        if half > 0:
            c["invcnt"][0] = c["invcnt"][1]
    h = 32
    inv = (10000.0 ** (-np.arange(h, dtype=np.float32) / h)).astype(np.float32)
    ang = np.float32(PAST) * inv
    c["cos_s"] = np.cos(ang).astype(np.float32)
    c["sin_s"] = np.sin(ang).astype(np.float32)
    return c


def make_in_maps(inputs, ntc=NT // 2, split=True):
    f = lambda a: np.ascontiguousarray(np.asarray(a, dtype=np.float32))
    shared = {k: f(inputs[k]) for k in WNAMES}
    shared["final_norm"] = f(inputs["final_norm"])
    shared["pool_scale_l"] = f(np.asarray(inputs["pool_scale"]).reshape(2, 2, 128).transpose(0, 2, 1))
    shared["conv_w_l"] = f(np.asarray(inputs["conv_w"]).reshape(2, 4, 8, 128).transpose(0, 3, 2, 1))
    shared["conv_b_l"] = f(np.asarray(inputs["conv_b"]).reshape(2, 8, 128).transpose(0, 2, 1))
    cons = [make_consts(ntc, hf, split) for hf in range(2 if split else 1)]
    maps = []
    xp = np.asarray(inputs["x_prompt"])
    xs = np.asarray(inputs["x_sample"])
    for c in range(8):
        seq, hf = (c // 2, c % 2) if split else (c % 4, 0)
        m = dict(shared)
        for k, v in cons[hf].items():
            m["c_" + k] = v
        m["x"] = f(xp[seq, hf * ntc:(hf + 1) * ntc])
        sl = slice(c * NS, (c + 1) * NS)
        m["xs"] = f(xs[sl, 0])
        m["cache_pool"] = f(np.asarray(inputs["cache_pool"])[:, sl])
        m["cache_k"] = f(np.asarray(inputs["cache_k"])[:, sl].reshape(2, NS, WBK, 256))
        m["cache_v"] = f(np.asarray(inputs["cache_v"])[:, sl].reshape(2, NS, WBK, 256))
        m["state_conv"] = f(np.asarray(inputs["state_conv"])[:, sl])
        m["state_ssm"] = f(np.asarray(inputs["state_ssm"])[:, sl])
        maps.append(m)
    return maps, cons[0]


def gather_outputs(r, wbp=WBK):
    cat = lambda name, rng, ax: np.concatenate([r[c][name] for c in rng], axis=ax)
    y_prompt = np.stack([np.concatenate([r[2 * s_]["y"], r[2 * s_ + 1]["y"]], 0) for s_ in range(4)], 0)
    y_sample = cat("ys", range(8), 0).reshape(32, 1, D)
    hi = [2 * s_ + 1 for s_ in range(4)]
    pool_p = np.stack([r[c]["pool_p"] for c in hi], 1)
    pool_s = cat("pool_s", range(8), 1)
    k_p = np.stack([r[c]["k_p"] for c in hi], 1).reshape(2, 4, wbp, 4, 64)
    k_s = cat("k_s", range(8), 1).reshape(2, 32, WBK, 4, 64)
    v_p = np.stack([r[c]["v_p"] for c in hi], 1).reshape(2, 4, wbp, 4, 64)
    v_s = cat("v_s", range(8), 1).reshape(2, 32, WBK, 4, 64)
    conv_p = np.stack([r[c]["conv_p"] for c in hi], 1)
    conv_s = cat("conv_s", range(8), 1)
    ssm_p = np.stack([r[c]["ssm_p"] for c in hi], 1).reshape(2, 4, 8, 64, 128)
    ssm_s = cat("ssm_s", range(8), 1)
    outs = (y_prompt, y_sample, pool_p, pool_s, k_p, k_s, v_p, v_s, conv_p, conv_s, ssm_p, ssm_s)
    return tuple(np.ascontiguousarray(o, dtype=np.float32) for o in outs)


def kernel(**inputs):
    ntc = NT // 2
    maps, c0 = make_in_maps(inputs, ntc, True)
    nc = build_program(c0, ntc, 2, True)
    res = run_bass_kernel_spmd(nc, maps, core_ids=list(range(8)))
    return gather_outputs(res.results)
```
